# Optimizing a Trainium2 kernel written in Bass

```python
import math
import jax, jax.numpy as jnp
from jax import lax
import numpy as np

D_MODEL = 1024
BATCH = 8
SEQ = 8192
DEPTH = 2

N_MIXERS = 4
GROUP_WIDTH = D_MODEL // N_MIXERS
MIX_WIDTH = N_MIXERS * GROUP_WIDTH
HEAD_DIM = 64
N_HEADS = GROUP_WIDTH // HEAD_DIM
IN_WIDTH = 11 * GROUP_WIDTH
SHORT_CONV_WIDTH = 3
CONFORMER_CONV_WIDTH = 31
SB_Q_BLOCK = 128
MOBA_BLOCK = 256
MOBA_TOPK = 3
MOBA_Q_CHUNK = 64
ROPE_THETA = 10000.0
D_FF = 4 * D_MODEL
EPS = 1e-6

kernel_name = 'hybrid_parallel_groups_sb_moba_conv'


def rms_norm(x, g):
    xf = x.astype(jnp.float32)
    y = xf * lax.rsqrt(jnp.mean(xf * xf, axis=-1, keepdims=True) + EPS)
    return (y * g.astype(jnp.float32)).astype(x.dtype)


def layer_norm(x, g, b):
    xf = x.astype(jnp.float32)
    mu = jnp.mean(xf, axis=-1, keepdims=True)
    var = jnp.mean(jnp.square(xf - mu), axis=-1, keepdims=True)
    y = (xf - mu) * lax.rsqrt(var + EPS)
    return (y * g.astype(jnp.float32) + b.astype(jnp.float32)).astype(x.dtype)


def causal_depthwise_conv(x, w):
    k = w.shape[0]
    ch = x.shape[-1]
    return lax.conv_general_dilated(
        x, w.astype(x.dtype)[:, None, :], window_strides=(1,), padding=[(k - 1, 0)],
        dimension_numbers=('NWC', 'WIO', 'NWC'), feature_group_count=ch)


def to_heads(t):
    b, s, _ = t.shape
    return t.reshape(b, s, N_HEADS, HEAD_DIM).transpose(0, 2, 1, 3)


def from_heads(t):
    b, h, s, d = t.shape
    return t.transpose(0, 2, 1, 3).reshape(b, s, h * d)


def apply_rope(t, positions):
    half = HEAD_DIM // 2
    inv_freq = jnp.exp(-math.log(ROPE_THETA) * jnp.arange(half, dtype=jnp.float32) / half)
    ang = positions.astype(jnp.float32)[:, None, :, None] * inv_freq
    cos, sin = jnp.cos(ang), jnp.sin(ang)
    tf = t.astype(jnp.float32)
    t1, t2 = tf[..., :half], tf[..., half:]
    return jnp.concatenate([t1 * cos - t2 * sin, t2 * cos + t1 * sin], axis=-1).astype(t.dtype)


def stick_breaking_attention(q, k, v):
    b, h, s, d = q.shape
    nq = s // SB_Q_BLOCK
    scale = 1.0 / math.sqrt(d)
    q_blocks = q.reshape(b, h, nq, SB_Q_BLOCK, d).transpose(2, 0, 1, 3, 4)
    starts = jnp.arange(nq, dtype=jnp.int32) * SB_Q_BLOCK
    kf = k.astype(jnp.float32)
    vf = v.astype(jnp.float32)
    kpos = jnp.arange(s, dtype=jnp.int32)

    def block(args):
        qi, start = args
        z = jnp.einsum('bhqd,bhkd->bhqk', qi.astype(jnp.float32), kf) * scale
        qpos = start + jnp.arange(SB_Q_BLOCK, dtype=jnp.int32)
        mask = kpos[None, :] < qpos[:, None]
        log_1mb = jnp.where(mask, jax.nn.log_sigmoid(-z), 0.0)
        suffix = lax.cumsum(log_1mb, axis=3, reverse=True) - log_1mb
        a = jnp.where(mask, jnp.exp(jax.nn.log_sigmoid(z) + suffix), 0.0)
        return jnp.einsum('bhqk,bhkd->bhqd', a, vf)

    out = lax.map(block, (q_blocks, starts))
    return out.transpose(1, 2, 0, 3, 4).reshape(b, h, s, d).astype(q.dtype)


def moba_attention(q, k, v):
    b, h, s, d = q.shape
    nb = -(-s // MOBA_BLOCK)
    pad = nb * MOBA_BLOCK - s
    kp = jnp.pad(k, ((0, 0), (0, 0), (0, pad), (0, 0)))
    vp = jnp.pad(v, ((0, 0), (0, 0), (0, pad), (0, 0)))
    k_blocks = kp.reshape(b, h, nb, MOBA_BLOCK, d)
    v_blocks = vp.reshape(b, h, nb, MOBA_BLOCK, d)
    k_mean = jnp.mean(k_blocks.astype(jnp.float32), axis=3)
    topk = min(MOBA_TOPK, nb)
    scale = 1.0 / math.sqrt(d)
    nq = s // MOBA_Q_CHUNK
    q_chunks = q.reshape(b, h, nq, MOBA_Q_CHUNK, d).transpose(2, 0, 1, 3, 4)
    starts = jnp.arange(nq, dtype=jnp.int32) * MOBA_Q_CHUNK
    bidx = jnp.arange(b)[:, None, None, None]
    hidx = jnp.arange(h)[None, :, None, None]
    block_ids = jnp.arange(nb, dtype=jnp.int32)

    def chunk(args):
        qi, start = args
        qf = qi.astype(jnp.float32)
        own = start // MOBA_BLOCK
        gate = jnp.einsum('bhqd,bhnd->bhqn', qf, k_mean)
        gate = jnp.where(block_ids < own, gate, -jnp.inf)
        _, idx = lax.top_k(gate, topk)
        sel_valid = jnp.arange(topk, dtype=jnp.int32) < own
        kg = k_blocks[bidx, hidx, idx].astype(jnp.float32)
        vg = v_blocks[bidx, hidx, idx].astype(jnp.float32)
        s_sel = jnp.einsum('bhqd,bhqnkd->bhqnk', qf, kg) * scale
        s_sel = jnp.where(sel_valid[:, None], s_sel, -jnp.inf)
        s_sel = s_sel.reshape(b, h, MOBA_Q_CHUNK, topk * MOBA_BLOCK)
        k_own = lax.dynamic_slice_in_dim(kp, own * MOBA_BLOCK, MOBA_BLOCK, axis=2).astype(jnp.float32)
        v_own = lax.dynamic_slice_in_dim(vp, own * MOBA_BLOCK, MOBA_BLOCK, axis=2).astype(jnp.float32)
        s_own = jnp.einsum('bhqd,bhkd->bhqk', qf, k_own) * scale
        qpos = start + jnp.arange(MOBA_Q_CHUNK, dtype=jnp.int32)
        kpos = own * MOBA_BLOCK + jnp.arange(MOBA_BLOCK, dtype=jnp.int32)
        s_own = jnp.where(kpos[None, :] <= qpos[:, None], s_own, -jnp.inf)
        p = jax.nn.softmax(jnp.concatenate([s_sel, s_own], axis=-1), axis=-1)
        p_sel, p_own = p[..., :topk * MOBA_BLOCK], p[..., topk * MOBA_BLOCK:]
        vg = vg.reshape(b, h, MOBA_Q_CHUNK, topk * MOBA_BLOCK, d)
        return (jnp.einsum('bhqm,bhqmd->bhqd', p_sel, vg)
                + jnp.einsum('bhqk,bhkd->bhqd', p_own, v_own))

    out = lax.map(chunk, (q_chunks, starts))
    return out.transpose(1, 2, 0, 3, 4).reshape(b, h, s, d).astype(q.dtype)


def qk_norm(t, g):
    return rms_norm(t, g)


def hybrid_layer(x, c_act, positions, w_ada, b_ada, g_norm1, w_in, w_sconv, w_cconv, b_cconv,
                 g_cln, b_cln, g_q, g_k, w_out, g_norm2, w_mlp1, w_mlp2):
    G = GROUP_WIDTH
    mod = (c_act @ w_ada + b_ada)[:, None, :]
    shift1, scale1, gate1, shift2, scale2, gate2 = jnp.split(mod, 6, axis=-1)

    hcur = rms_norm(x, g_norm1) * (1.0 + scale1) + shift1
    proj = hcur @ w_in
    sc_b, sc_c, sc_h = proj[..., 0:G], proj[..., G:2 * G], proj[..., 2 * G:3 * G]
    sb_q, sb_k, sb_v = proj[..., 3 * G:4 * G], proj[..., 4 * G:5 * G], proj[..., 5 * G:6 * G]
    mb_q, mb_k, mb_v = proj[..., 6 * G:7 * G], proj[..., 7 * G:8 * G], proj[..., 8 * G:9 * G]
    cf_a, cf_g = proj[..., 9 * G:10 * G], proj[..., 10 * G:11 * G]

    y_a = sc_b * causal_depthwise_conv(sc_c * sc_h, w_sconv)
    y_b = from_heads(stick_breaking_attention(to_heads(sb_q), to_heads(sb_k), to_heads(sb_v)))
    q = apply_rope(qk_norm(to_heads(mb_q), g_q), positions)
    k = apply_rope(qk_norm(to_heads(mb_k), g_k), positions)
    y_c = from_heads(moba_attention(q, k, to_heads(mb_v)))
    u = cf_a * jax.nn.sigmoid(cf_g)
    u = causal_depthwise_conv(u, w_cconv) + b_cconv
    y_d = jax.nn.silu(layer_norm(u, g_cln, b_cln))

    mix = jnp.concatenate([y_a, y_b, y_c, y_d], axis=-1) @ w_out
    x = x + gate1 * mix

    h2 = rms_norm(x, g_norm2) * (1.0 + scale2) + shift2
    f = jnp.square(jax.nn.relu(h2 @ w_mlp1)) @ w_mlp2
    return x + gate2 * f


def setup_inputs(seed: int = 0) -> dict:
    key = jax.random.key(seed)
    ks = jax.random.split(key, 18)
    nrm = jax.random.normal
    f32 = jnp.float32
    G = GROUP_WIDTH
    x = nrm(ks[0], (BATCH, SEQ, D_MODEL), f32)
    c = nrm(ks[1], (BATCH, D_MODEL), f32)
    offsets = jax.random.randint(ks[2], (BATCH, 1), 0, 4096, dtype=jnp.int32)
    positions = (offsets + jnp.arange(SEQ, dtype=jnp.int32)[None, :]).astype(jnp.int32)
    w_ada = nrm(ks[3], (DEPTH, D_MODEL, 6 * D_MODEL), f32) * (0.5 * D_MODEL ** -0.5)
    b_ada = 0.02 * nrm(ks[4], (DEPTH, 6 * D_MODEL), f32)
    g_norm1 = 1.0 + 0.05 * nrm(ks[5], (DEPTH, D_MODEL), f32)
    w_in = nrm(ks[6], (DEPTH, D_MODEL, IN_WIDTH), f32) * D_MODEL ** -0.5
    w_sconv = nrm(ks[7], (DEPTH, SHORT_CONV_WIDTH, G), f32) * SHORT_CONV_WIDTH ** -0.5
    w_cconv = nrm(ks[8], (DEPTH, CONFORMER_CONV_WIDTH, G), f32) * CONFORMER_CONV_WIDTH ** -0.5
    b_cconv = 0.02 * nrm(ks[9], (DEPTH, G), f32)
    g_cln = 1.0 + 0.05 * nrm(ks[10], (DEPTH, G), f32)
    b_cln = 0.02 * nrm(ks[11], (DEPTH, G), f32)
    g_q = 1.0 + 0.05 * nrm(ks[12], (DEPTH, HEAD_DIM), f32)
    g_k = 1.0 + 0.05 * nrm(ks[13], (DEPTH, HEAD_DIM), f32)
    w_out = nrm(ks[14], (DEPTH, MIX_WIDTH, D_MODEL), f32) * MIX_WIDTH ** -0.5
    g_norm2 = 1.0 + 0.05 * nrm(ks[15], (DEPTH, D_MODEL), f32)
    w_mlp1 = nrm(ks[16], (DEPTH, D_MODEL, D_FF), f32) * D_MODEL ** -0.5
    w_mlp2 = nrm(ks[17], (DEPTH, D_FF, D_MODEL), f32) * D_FF ** -0.5
    return {'x': x, 'c': c, 'positions': positions, 'w_ada': w_ada, 'b_ada': b_ada,
            'g_norm1': g_norm1, 'w_in': w_in, 'w_sconv': w_sconv, 'w_cconv': w_cconv,
            'b_cconv': b_cconv, 'g_cln': g_cln, 'b_cln': b_cln, 'g_q': g_q, 'g_k': g_k,
            'w_out': w_out, 'g_norm2': g_norm2, 'w_mlp1': w_mlp1, 'w_mlp2': w_mlp2}


def reference(x, c, positions, w_ada, b_ada, g_norm1, w_in, w_sconv, w_cconv, b_cconv,
              g_cln, b_cln, g_q, g_k, w_out, g_norm2, w_mlp1, w_mlp2):
    c_act = jax.nn.silu(c)
    for l in range(DEPTH):
        x = hybrid_layer(x, c_act, positions, w_ada[l], b_ada[l], g_norm1[l], w_in[l],
                         w_sconv[l], w_cconv[l], b_cconv[l], g_cln[l], b_cln[l],
                         g_q[l], g_k[l], w_out[l], g_norm2[l], w_mlp1[l], w_mlp2[l])
    return x
```

```python
import numpy as np
from contextlib import ExitStack
import concourse.bass as bass
import concourse.mybir as mybir
from concourse.bass_utils import run_bass_kernel_spmd

F32 = mybir.dt.float32
BF16 = mybir.dt.bfloat16
I32 = mybir.dt.int32
AF = mybir.ActivationFunctionType
ALU = mybir.AluOpType
AX = mybir.AxisListType

NDSEM = 12
D = 1024
G = 256
NH = 4
HD = 64
DFF = 4096
INW = 11 * G
EPS = 1e-6
NEG = -30000.0


class V:
    __slots__ = ("key", "ap")

    def __init__(self, key, ap):
        self.key = key
        self.ap = ap

    def __getitem__(self, idx):
        return V(self.key, self.ap[idx])

    def k(self, sub):
        return V((self.key, sub), self.ap)

    def re(self, pat, **kw):
        return V(self.key, self.ap.rearrange(pat, **kw))

    def bc(self, dt):
        return V(self.key, self.ap.bitcast(dt))

    def bcast(self, shape):
        return V(self.key, self.ap.broadcast_to(list(shape)))

    def unsq(self, ax):
        return V(self.key, self.ap.unsqueeze(ax))


class Op:
    __slots__ = ("eng", "emit", "deps", "signal", "is_dma", "ev_sem", "ev_val")

    def __init__(self, eng, emit, is_dma=False):
        self.eng = eng
        self.emit = emit
        self.deps = []
        self.signal = False
        self.is_dma = is_dma
        self.ev_sem = None
        self.ev_val = 0


class Prog:
    ENGS = ("sp", "act", "dve", "pool", "pe")

    def __init__(self, nc):
        self.nc = nc
        self.es = ExitStack()
        self.ops = {e: [] for e in self.ENGS}
        self.lastreal = {e: None for e in self.ENGS}
        self.lastw = {}
        self.readers = {}
        self.dma_count = {e: 0 for e in self.ENGS}
        self.dma_last = {}
        self.n = 0

    def sb(self, name, shape, dt):
        t = self.es.enter_context(self.nc.sbuf_tensor(name, list(shape), dt))
        return V(name, t[:])

    def ps(self, name, shape, dt):
        t = self.es.enter_context(self.nc.psum_tensor(name, list(shape), dt))
        return V(name, t[:])

    def dram(self, name, shape, dt, kind="Internal"):
        t = self.nc.dram_tensor(name, list(shape), dt, kind=kind)
        return V(name, t.ap())

    def _add(self, eng, emit, reads, writes, is_dma=False):
        op = Op(eng, emit, is_dma)
        rk = [v.key for v in reads if v is not None]
        wk = [v.key for v in writes if v is not None]
        deps = []
        for k in rk:
            p = self.lastw.get(k)
            if p is not None:
                deps.append((p, True))
        for k in wk:
            p = self.lastw.get(k)
            if p is not None:
                deps.append((p, False))
            for r in self.readers.get(k, ()):
                deps.append((r, False))
        for p, raw in deps:
            if p is op:
                continue
            if p.is_dma:
                need = True
            elif p.eng != eng:
                need = True
            elif eng == "pe":
                need = False
            else:
                need = True
            if need and p not in op.deps:
                p.signal = True
                op.deps.append(p)
        if is_dma:
            op.signal = True
            c = self.dma_count[eng]
            self.dma_count[eng] = c + 1
            slot = (eng, c % NDSEM)
            prev = self.dma_last.get(slot)
            if prev is not None:
                op.deps.append(prev)
            self.dma_last[slot] = op
            op.ev_sem = slot
            op.ev_val = 16 * (c // NDSEM + 1)
        for k in wk:
            self.lastw[k] = op
            self.readers[k] = []
        for k in rk:
            if k not in wk:
                self.readers.setdefault(k, []).append(op)
        self.ops[eng].append(op)
        self.lastreal[eng] = op
        self.n += 1
        return op

    def barrier(self):
        lasts = []
        for e in self.ENGS:
            p = self.lastreal[e]
            if p is not None and not p.is_dma:
                p.signal = True
                lasts.append(p)
        lasts.extend(self.dma_last.values())
        for e in self.ENGS:
            op = Op(e, None)
            op.deps = [p for p in lasts if (p.is_dma or p.eng != e)]
            self.ops[e].append(op)
        self.lastw.clear()
        self.readers.clear()

    def dma(self, out, in_, eng="sp", **kw):
        return self._add(eng, lambda e: e.dma_start(out=out.ap, in_=in_.ap, **kw), [in_], [out], is_dma=True)

    def mm(self, out, lhsT, rhs, start=True, stop=True, **kw):
        return self._add("pe", lambda e: e.matmul(out.ap, lhsT.ap, rhs.ap, start=start, stop=stop, **kw),
                         [lhsT, rhs], [out])

    def transpose(self, out, in_, ident):
        return self._add("pe", lambda e: e.transpose(out.ap, in_.ap, ident.ap), [in_, ident], [out])

    def act(self, out, in_, func, bias=None, scale=None, accum_out=None, eng="act"):
        def emit(e):
            kw = {}
            if bias is not None:
                kw["bias"] = bias.ap if isinstance(bias, V) else bias
            if scale is not None:
                kw["scale"] = scale.ap if isinstance(scale, V) else scale
            if accum_out is not None:
                kw["accum_out"] = accum_out.ap
            return e.activation(out=out.ap, in_=in_.ap, func=func, **kw)
        rd = [in_] + [b for b in (bias, scale) if isinstance(b, V)]
        wr = [out] + ([accum_out] if accum_out is not None else [])
        return self._add(eng, emit, rd, wr)

    def ts(self, out, in0, s1, s2, op0, op1=None, eng="dve"):
        def emit(e):
            a1 = s1.ap if isinstance(s1, V) else s1
            a2 = s2.ap if isinstance(s2, V) else s2
            kw = {}
            if op1 is not None:
                kw["op1"] = op1
            return e.tensor_scalar(out.ap, in0.ap, a1, a2, op0, **kw)
        rd = [in0] + [s for s in (s1, s2) if isinstance(s, V)]
        return self._add(eng, emit, rd, [out])

    def tt(self, out, in0, in1, op, eng="dve"):
        return self._add(eng, lambda e: e.tensor_tensor(out.ap, in0.ap, in1.ap, op), [in0, in1], [out])

    def stt(self, out, in0, scalar, in1, op0, op1, eng="dve"):
        def emit(e):
            s = scalar.ap if isinstance(scalar, V) else scalar
            return e.scalar_tensor_tensor(out.ap, in0.ap, s, in1.ap, op0, op1)
        rd = [in0, in1] + ([scalar] if isinstance(scalar, V) else [])
        return self._add(eng, emit, rd, [out])

    def copy(self, out, in_, eng="dve"):
        if eng == "act":
            return self.act(out, in_, AF.Copy)
        return self._add(eng, lambda e: e.tensor_copy(out.ap, in_.ap), [in_], [out])

    def memset(self, out, val, eng="dve"):
        return self._add(eng, lambda e: e.memset(out.ap, val), [], [out])

    def reduce(self, out, in_, op, eng="dve"):
        return self._add(eng, lambda e: e.tensor_reduce(out.ap, in_.ap, AX.X, op), [in_], [out])

    def recip(self, out, in_):
        return self._add("dve", lambda e: e.reciprocal(out.ap, in_.ap), [in_], [out])

    def generic(self, eng, fn, reads, writes):
        return self._add(eng, fn, reads, writes)

    def emit(self):
        nc = self.nc
        es = self.es
        esem = {e: es.enter_context(nc.semaphore("s_" + e)) for e in self.ENGS}
        dsem = {}
        for e in self.ENGS:
            for i in range(min(NDSEM, self.dma_count[e])):
                dsem[(e, i)] = es.enter_context(nc.semaphore("d_%s_%d" % (e, i)))
        for e in self.ENGS:
            c = 0
            for op in self.ops[e]:
                if op.is_dma or op.emit is None:
                    continue
                if op.signal:
                    c += 1
                    op.ev_sem = e
                    op.ev_val = c
        prog = self

        def run(ename, eng):
            waited = {}
            for op in prog.ops[ename]:
                for p in op.deps:
                    key = p.ev_sem
                    sem = dsem[key] if p.is_dma else esem[key]
                    if waited.get(key, 0) < p.ev_val:
                        eng.wait_ge(sem, p.ev_val)
                        waited[key] = p.ev_val
                if op.emit is None:
                    continue
                ins = op.emit(eng)
                if op.signal:
                    if op.is_dma:
                        ins.then_inc(dsem[op.ev_sem], 16)
                    else:
                        ins.then_inc(esem[ename], 1)
            for (e2, i), last in prog.dma_last.items():
                if e2 == ename and waited.get((e2, i), 0) < last.ev_val:
                    eng.wait_ge(dsem[(e2, i)], last.ev_val)

        with nc.Block() as block:
            @block.sync
            def _(e):
                run("sp", e)

            @block.scalar
            def _(e):
                run("act", e)

            @block.vector
            def _(e):
                run("dve", e)

            @block.gpsimd
            def _(e):
                run("pool", e)

            @block.tensor
            def _(e):
                run("pe", e)
        es.close()


class Arena:
    def __init__(self, P, name, nbytes):
        self.v = P.sb(name, [128, nbytes // 4], F32)
        self.cap = nbytes
        self.off = 0
        self.gen = 0

    def reset(self):
        self.off = 0
        self.gen += 1

    def alloc(self, name, shape, dt):
        esz = 4 if dt in (F32, I32) else 2
        nel = int(np.prod(shape[1:]))
        n4 = (nel * esz + 3) // 4
        o4 = self.off // 4
        assert self.off + n4 * 4 <= self.cap, ("arena overflow", name, self.off, n4 * 4, self.cap)
        ap = self.v.ap[0:shape[0], o4:o4 + n4]
        if dt != F32:
            ap = ap.bitcast(dt)
            if esz == 2 and nel != n4 * 2:
                ap = ap[:, 0:nel]
        if len(shape) == 3:
            ap = ap.rearrange("p (a b) -> p a b", a=shape[1])
        elif len(shape) == 4:
            ap = ap.rearrange("p (a b c) -> p a b c", a=shape[1], b=shape[2])
        self.off += ((n4 * 4 + 63) // 64) * 64
        return V((name, self.gen), ap)


def host_consts():
    c = {}
    c["ident"] = np.eye(128, dtype=np.float32)
    s1 = np.arange(128)[:, None]
    s0 = np.arange(128)[None, :]
    c["negT"] = np.where(s1 >= s0, -1.0, 0.0).astype(np.float32)
    t = np.arange(512)[None, :]
    sb = np.zeros((128, 4, 512), np.float32)
    cb = np.zeros((128, 4, 512), np.float32)
    for cc in range(4):
        sb[:, cc, :] = (t > 128 * cc + s1).astype(np.float32)
        cb[:, cc, :] = np.where(t >= 128 * cc + s1, 0.0, NEG)
    c["sbmask"] = sb
    c["mbcb"] = cb
    c["blk"] = np.tile(np.arange(32, dtype=np.float32)[None, :], (128, 1))
    half = HD // 2
    inv = np.exp(-np.log(10000.0) * np.arange(half, dtype=np.float32) / half).astype(np.float32)
    c["invf"] = np.tile(inv[None, :], (128, 1)).astype(np.float32)
    cs = np.zeros((33, 128), np.float32)
    cs[0, :] = 1.0
    cs[32, :] = 1.0
    c["csel"] = cs
    return c


def build(S, depth):
    assert S % 512 == 0
    nc = bass.Bass("TRN2", target_bir_lowering=False)
    P = Prog(nc)
    NT = S // 512
    NS = S // 128
    NB = S // 256
    ext = lambda n, s, d: P.dram(n, s, d, kind="ExternalInput")
    x_in = ext("x", [S, D], F32)
    cT_in = ext("cT", [128, 8], F32)
    pos_in = ext("pos", [128, S // 128], I32)
    w_ada = ext("w_ada", [depth, D, 6 * D], F32)
    b_ada = ext("b_ada", [depth, 6 * D], F32)
    g_norm1 = ext("g_norm1", [depth, D], F32)
    w_in = ext("w_in", [depth, D, INW], F32)
    w_sconv = ext("w_sconv", [depth, 3, G], F32)
    w_cconv = ext("w_cconv", [depth, 31, G], F32)
    b_cconv = ext("b_cconv", [depth, G], F32)
    g_cln = ext("g_cln", [depth, G], F32)
    b_cln = ext("b_cln", [depth, G], F32)
    g_q = ext("g_q", [depth, HD], F32)
    g_k = ext("g_k", [depth, HD], F32)
    w_out = ext("w_out", [depth, D, D], F32)
    g_norm2 = ext("g_norm2", [depth, D], F32)
    w_mlp1 = ext("w_mlp1", [depth, D, DFF], F32)
    w_mlp2 = ext("w_mlp2", [depth, DFF, D], F32)
    k_ident = ext("k_ident", [128, 128], F32)
    k_negT = ext("k_negT", [128, 128], F32)
    k_sbmask = ext("k_sbmask", [128, 4, 512], F32)
    k_mbcb = ext("k_mbcb", [128, 4, 512], F32)
    k_blk = ext("k_blk", [128, 32], F32)
    k_invf = ext("k_invf", [128, 32], F32)
    k_csel = ext("k_csel", [33, 128], F32)
    out = P.dram("out", [S, D], F32, kind="ExternalOutput")

    xmid = P.dram("xmid", [S, D], F32)
    x1s = P.dram("x1s", [S, D], F32)
    h2Ts = P.dram("h2Ts", [D, S], BF16)
    mixT = P.dram("mixT", [D, S], BF16)
    sbqT = P.dram("sbqT", [G, S], BF16)
    sbkT = P.dram("sbkT", [G, S], BF16)
    mbqT = P.dram("mbqT", [G, S], BF16)
    mbkT = P.dram("mbkT", [G, S], BF16)
    sbv = P.dram("sbv", [S, G], BF16)
    mbv = P.dram("mbv", [S, G], BF16)
    csd = P.dram("csd", [S, 64], F32)
    modrow = P.dram("modrow", [depth, 6 * D], F32)

    ident_f = P.sb("ident_f", [128, 128], F32)
    ident_b = P.sb("ident_b", [128, 128], BF16)
    ones_f = P.sb("ones_f", [128, 128], F32)
    ones_b = P.sb("ones_b", [128, 128], BF16)
    avg_f = P.sb("avg_f", [128, 128], F32)
    modc = P.sb("modc", [128, 64], F32)
    G1 = P.sb("G1", [128, 8], F32)
    G2 = P.sb("G2", [128, 8], F32)
    prmT = P.sb("prmT", [128, 2, 37], F32)
    gqk_bc = P.sb("gqk_bc", [128, 512], F32)
    negm = P.sb("negm", [128, 1], F32)
    epsc = P.sb("epsc", [128, 1], F32)
    A = Arena(P, "arena", 196 * 1024)
    pb = [P.ps("pb%d" % i, [128, 512], F32) for i in range(8)]

    P.dma(ident_f, k_ident)
    P.copy(ident_b, ident_f)
    P.memset(ones_f, 1.0)
    P.memset(ones_b, 1.0)
    P.memset(avg_f, 1.0 / 256)
    P.memset(epsc, EPS)

    def rstd_from_ss(rs, ss, n):
        P.ts(rs, ss, 1.0 / n, EPS, ALU.mult, ALU.add)
        P.act(rs, rs, AF.Sqrt)
        P.recip(rs, rs)

    A.reset()
    NJ = S // 128
    posi = A.alloc("posi", [128, NJ], I32)
    posf = A.alloc("posf", [128, NJ], F32)
    invf = A.alloc("invf", [128, 32], F32)
    ang = A.alloc("ang", [128, NJ, 32], F32)
    tmpa = A.alloc("tmpa", [128, NJ, 32], F32)
    cst = A.alloc("cst", [128, NJ, 64], F32)
    mpi = A.alloc("mpi", [128, 1], F32)
    P.memset(mpi, -float(np.pi))
    P.dma(posi, pos_in)
    P.dma(invf, k_invf)
    P.copy(posf, posi)
    P.tt(ang, posf.unsq(2).bcast([128, NJ, 32]), invf.unsq(1).bcast([128, NJ, 32]), ALU.mult)
    TWO_PI = float(2 * np.pi)
    C1 = 6.28125
    C2 = TWO_PI - C1
    PI_ = float(np.pi)
    ni = A.alloc("ni", [128, NJ, 32], I32)
    nf = A.alloc("nf", [128, NJ, 32], F32)
    rr = A.alloc("rr", [128, NJ, 32], F32)
    mm_ = A.alloc("mm_", [128, NJ, 32], F32)
    P.ts(tmpa, ang, 1.0 / TWO_PI, None, ALU.mult)
    P.copy(ni, tmpa)
    P.copy(nf, ni)
    P.stt(rr, nf, -C1, ang, ALU.mult, ALU.add)
    P.stt(rr, nf, -C2, rr, ALU.mult, ALU.add)

    def fold(t):
        P.ts(mm_, t, PI_, None, ALU.is_gt)
        P.stt(t, mm_, -TWO_PI, t, ALU.mult, ALU.add)
        P.ts(mm_, t, -PI_, None, ALU.is_lt)
        P.stt(t, mm_, TWO_PI, t, ALU.mult, ALU.add)
        P.ts(t, t, 3.14159, -3.14159, ALU.min, ALU.max)

    fold(rr)
    P.act(cst[:, :, 32:64], rr, AF.Sin)
    P.ts(tmpa, rr, PI_ / 2, None, ALU.add)
    fold(tmpa)
    P.act(cst[:, :, 0:32], tmpa, AF.Sin)
    P.dma(csd.re("(p j) f -> p j f", p=128), cst)
    P.barrier()

    for l in range(depth):
        xin = x_in if l == 0 else xmid
        xout = out if l == depth - 1 else xmid
        A.reset()
        sc = A.alloc("sc", [128, 8], F32)
        cTs = A.alloc("cTs", [128, 8], F32)
        rowbuf = A.alloc("rowbuf", [1, 8 * D], F32)
        brow = A.alloc("brow", [1, 6 * D], F32)
        wst = [A.alloc("wst%d" % i, [128, 8, 512], F32) for i in range(2)]
        prm = A.alloc("prm", [37, G], F32)
        grow = A.alloc("grow", [1, 512], F32)
        gsq = A.alloc("gsq", [1, 128], F32)
        gmx = A.alloc("gmx", [1, 2], F32)
        P.dma(cTs, cT_in)
        P.act(sc, cTs, AF.Sigmoid)
        P.tt(sc, sc, cTs, ALU.mult)
        P.dma(brow, b_ada[l:l + 1, :])
        P.dma(rowbuf[:, 6 * D:7 * D], g_norm1[l:l + 1, :])
        P.dma(rowbuf[:, 7 * D:8 * D], g_norm2[l:l + 1, :])
        P.dma(prm[0:3, :], w_sconv[l])
        P.dma(prm[3:34, :], w_cconv[l])
        P.dma(prm[34:35, :], b_cconv[l:l + 1, :])
        P.dma(prm[35:36, :], g_cln[l:l + 1, :])
        P.dma(prm[36:37, :], b_cln[l:l + 1, :])
        for h in range(NH):
            P.dma(grow[:, h * 64:(h + 1) * 64], g_q[l:l + 1, :])
            P.dma(grow[:, 256 + h * 64:256 + (h + 1) * 64], g_k[l:l + 1, :])
        for n in range(12):
            w = wst[n % 2]
            P.dma(w, w_ada[l][:, n * 512:(n + 1) * 512].re("(j p) n -> p j n", p=128))
            bank = pb[n % 2]
            for j in range(8):
                P.mm(bank[0:1, :], sc[:, j:j + 1], w[:, j, :], start=(j == 0), stop=False)
            P.mm(bank[0:1, :], ones_f[0:1, 0:1], brow[:, n * 512:(n + 1) * 512], start=False, stop=True)
            P.copy(rowbuf[:, n * 512:(n + 1) * 512], bank[0:1, :], eng="act")
        P.dma(modrow[l:l + 1, :], rowbuf[:, 0:6 * D])
        for piece in range(8):
            for j in range(8):
                col = piece * 8 + j
                P.mm(pb[2][:, col:col + 1], rowbuf[:, piece * D + j * 128: piece * D + (j + 1) * 128],
                     ones_f[0:1, 0:1], start=True, stop=True)
        P.copy(modc, pb[2][:, 0:64])
        P.stt(G1, modc[:, 8:16], 1.0, modc[:, 48:56], ALU.add, ALU.mult)
        P.stt(G2, modc[:, 32:40], 1.0, modc[:, 56:64], ALU.add, ALU.mult)
        sh1 = modc[:, 0:8]
        sh2 = modc[:, 24:32]
        for c2 in range(2):
            P.mm(pb[3][:, c2 * 37:(c2 + 1) * 37], prm[:, c2 * 128:(c2 + 1) * 128], ident_f[0:37, 0:37])
        P.copy(prmT.re("p c r -> p (c r)"), pb[3][:, 0:74])
        P.mm(pb[4], ones_f[0:1, :], grow)
        P.copy(gqk_bc, pb[4])
        P.tt(gsq[:, 0:64], grow[:, 0:64], grow[:, 0:64], ALU.mult)
        P.tt(gsq[:, 64:128], grow[:, 256:320], grow[:, 256:320], ALU.mult)
        P.reduce(gmx, gsq.re("p (a b) -> p a b", a=2), ALU.max)
        P.tt(gmx[:, 0:1], gmx[:, 0:1], gmx[:, 1:2], ALU.mult)
        P.act(gmx[:, 0:1], gmx[:, 0:1], AF.Sqrt)
        P.ts(gmx[:, 0:1], gmx[:, 0:1], -8.0, None, ALU.mult)
        P.mm(pb[5][:, 0:1], ones_f[0:1, :], gmx[:, 0:1])
        P.copy(negm, pb[5][:, 0:1])
        P.barrier()

        A.reset()
        Wb = A.alloc("Wb", [128, 8, INW], BF16)
        wst = [A.alloc("wst%d" % i, [128, 8, 512], F32) for i in range(2)]
        ncol = [(n * 512, min(512, INW - n * 512)) for n in range((INW + 511) // 512)]
        for n, (c0, cw) in enumerate(ncol):
            w = wst[n % 2]
            P.dma(w[:, :, 0:cw], w_in[l][:, c0:c0 + cw].re("(j p) n -> p j n", p=128))
            P.copy(Wb[:, :, c0:c0 + cw], w[:, :, 0:cw], eng=("dve", "pool")[n % 2])
        xs = [A.alloc("xs%d" % i, [128, D], F32) for i in range(2)]
        junk = A.alloc("junk", [128, D], BF16)
        ss = A.alloc("ss", [128, 4], F32)
        rs = A.alloc("rs", [128, 4], F32)
        xn = A.alloc("xn", [128, 4, D], BF16)
        hT = A.alloc("hT", [128, 8, 512], BF16)
        projT = A.alloc("projT", [128, 10, 512], F32)
        qkT = A.alloc("qkT", [128, 4, 512], BF16)
        vout = A.alloc("vout", [128, 4, 512], BF16)
        ua = A.alloc("ua", [128, 2, 514], F32)
        acc = A.alloc("acc", [128, 2, 512], F32)
        yaT = A.alloc("yaT", [128, 2, 512], BF16)
        ub = A.alloc("ub", [128, 2, 542], BF16)
        dg = A.alloc("dg", [128, 2, 31, 128], BF16)
        for cc in range(2):
            P.tt(dg[:, cc, :, :], ident_b.unsq(1).bcast([128, 31, 128]),
                 prmT[:, cc, 3:34].unsq(2).bcast([128, 31, 128]), ALU.mult)
        sg = A.alloc("sg", [128, 2, 512], F32)
        usq = A.alloc("usq", [128, 2, 512], F32)
        mean_sb = A.alloc("mean_sb", [128, 512], F32)
        var_sb = A.alloc("var_sb", [128, 512], F32)
        ydT = A.alloc("ydT", [128, 2, 512], BF16)
        cs4 = A.alloc("cs4", [128, 4, 64], F32)
        sq = A.alloc("sq", [128, 512], F32)
        ssh = A.alloc("ssh", [128, 8], F32)
        qn = A.alloc("qn", [128, 512], F32)
        ra = A.alloc("ra", [128, 8, 32], F32)
        rb = A.alloc("rb", [128, 8, 32], F32)
        qr = A.alloc("qr", [128, 4, 512], BF16)
        mbT = A.alloc("mbT", [128, 4, 512], BF16)
        P.memset(ua, 0.0)
        P.memset(ub, 0.0, eng="pool")
        FM = [(0, 0), (1, 128), (2, 256), (3, 384), (4, 512), (5, 640),
              (6, 2304), (7, 2432), (8, 2560), (9, 2688)]
        QK = [(0, 768), (1, 896), (2, 1024), (3, 1152)]
        bi = 0
        for i in range(NT):
            t0 = i * 512
            for st in range(4):
                xt = xs[st % 2]
                P.dma(xt, xin[t0 + st * 128:t0 + (st + 1) * 128, :])
                P.act(junk, xt, AF.Square, accum_out=ss[:, st:st + 1])
                rstd_from_ss(rs[:, st:st + 1], ss[:, st:st + 1], D)
                P.act(xn[:, st, :], xt, AF.Copy, scale=rs[:, st:st + 1])
            P.dma(cs4, csd[t0:t0 + 512, :].re("(s p) f -> p s f", p=128))
            for j in range(8):
                bank = pb[bi % 8]; bi += 1
                bkb = bank.bc(BF16)
                for st in range(4):
                    P.transpose(bkb[:, st * 128:(st + 1) * 128], xn[:, st, j * 128:(j + 1) * 128], ident_b)
                P.act(hT[:, j, :], bkb[:, 0:512], AF.Identity, scale=G1[:, j:j + 1], bias=sh1[:, j:j + 1])
            for idx, c0 in FM:
                bank = pb[bi % 8]; bi += 1
                for j in range(8):
                    P.mm(bank, Wb[:, j, c0:c0 + 128], hT[:, j, :], start=(j == 0), stop=(j == 7))
                P.copy(projT[:, idx, :], bank, eng=("act", "dve")[idx % 2])
            for idx, c0 in QK:
                bank = pb[bi % 8]; bi += 1
                for j in range(8):
                    P.mm(bank, Wb[:, j, c0:c0 + 128], hT[:, j, :], start=(j == 0), stop=(j == 7))
                if idx < 2:
                    P.act(qkT[:, idx, :], bank, AF.Copy, scale=0.125)
                else:
                    P.copy(qkT[:, idx, :], bank, eng="dve")
            P.dma(sbqT[:, t0:t0 + 512].re("(c p) t -> p c t", p=128), qkT[:, 0:2, :])
            P.dma(sbkT[:, t0:t0 + 512].re("(c p) t -> p c t", p=128), qkT[:, 2:4, :])
            for st in range(4):
                bank = pb[bi % 8]; bi += 1
                for j in range(8):
                    P.mm(bank[:, 0:256], hT[:, j, st * 128:(st + 1) * 128], Wb[:, j, 1280:1536],
                         start=(j == 0), stop=(j == 7))
                for j in range(8):
                    P.mm(bank[:, 256:512], hT[:, j, st * 128:(st + 1) * 128], Wb[:, j, 2048:2304],
                         start=(j == 0), stop=(j == 7))
                P.copy(vout[:, st, :], bank, eng="act")
                bank = pb[bi % 8]; bi += 1
                for j in range(8):
                    P.mm(bank, hT[:, j, st * 128:(st + 1) * 128], Wb[:, j, 1536:2048],
                         start=(j == 0), stop=(j == 7))
                P.act(sq, bank, AF.Square)
                P.reduce(ssh, sq.re("p (a b) -> p a b", a=8), ALU.add)
                rstd_from_ss(ssh, ssh, HD)
                P.tt(qn.re("p (a b) -> p a b", a=8), bank.re("p (a b) -> p a b", a=8),
                     ssh.unsq(2).bcast([128, 8, 64]), ALU.mult)
                P.tt(qn, qn, gqk_bc, ALU.mult, eng="pool")
                q4 = qn.re("p (a h b) -> p a h b", a=8, h=2)
                o4 = qr[:, st, :].re("p (a h b) -> p a h b", a=8, h=2)
                cosb = cs4[:, st, 0:32].unsq(1).bcast([128, 8, 32])
                sinb = cs4[:, st, 32:64].unsq(1).bcast([128, 8, 32])
                P.tt(ra, q4[:, :, 0, :], cosb, ALU.mult)
                P.tt(rb, q4[:, :, 1, :], sinb, ALU.mult, eng="pool")
                P.tt(o4[:, :, 0, :], ra, rb, ALU.subtract)
                P.tt(ra, q4[:, :, 1, :], cosb, ALU.mult)
                P.tt(rb, q4[:, :, 0, :], sinb, ALU.mult, eng="pool")
                P.tt(o4[:, :, 1, :], ra, rb, ALU.add)
            P.dma(sbv[t0:t0 + 512, :].re("(s p) f -> p s f", p=128), vout[:, :, 0:256])
            P.dma(mbv[t0:t0 + 512, :].re("(s p) f -> p s f", p=128), vout[:, :, 256:512])
            for blk in range(4):
                bank = pb[bi % 8]; bi += 1
                bkb = bank.bc(BF16)
                for st in range(4):
                    P.transpose(bkb[:, st * 128:(st + 1) * 128], qr[:, st, blk * 128:(blk + 1) * 128], ident_b)
                P.copy(mbT[:, blk, :], bkb[:, 0:512], eng=("act", "dve")[blk % 2])
            P.dma(mbqT[:, t0:t0 + 512].re("(c p) t -> p c t", p=128), mbT[:, 0:2, :])
            P.dma(mbkT[:, t0:t0 + 512].re("(c p) t -> p c t", p=128), mbT[:, 2:4, :])
            for cc in range(2):
                w3 = prmT[:, cc, 0:3]
                P.tt(ua[:, cc, 2:514], projT[:, 2 + cc, :], projT[:, 4 + cc, :], ALU.mult)
                P.ts(acc[:, cc, :], ua[:, cc, 0:512], w3[:, 0:1], None, ALU.mult)
                P.stt(acc[:, cc, :], ua[:, cc, 1:513], w3[:, 1:2], acc[:, cc, :], ALU.mult, ALU.add)
                P.stt(acc[:, cc, :], ua[:, cc, 2:514], w3[:, 2:3], acc[:, cc, :], ALU.mult, ALU.add)
                P.tt(yaT[:, cc, :], projT[:, cc, :], acc[:, cc, :], ALU.mult)
                P.copy(ua[:, cc, 0:2], ua[:, cc, 512:514])
            P.dma(mixT[0:256, t0:t0 + 512].re("(c p) t -> p c t", p=128), yaT)
            for cc in range(2):
                eng = ("dve", "pool")[cc]
                P.act(sg[:, cc, :], projT[:, 8 + cc, :], AF.Sigmoid)
                P.tt(ub[:, cc, 30:542], projT[:, 6 + cc, :], sg[:, cc, :], ALU.mult, eng=eng)
                bank = pb[bi % 8]; bi += 1
                for k in range(31):
                    P.mm(bank, dg[:, cc, k, :], ub[:, cc, k:k + 512], start=(k == 0), stop=(k == 30))
                P.act(acc[:, cc, :], bank, AF.Identity, bias=prmT[:, cc, 34:35])
                P.copy(ub[:, cc, 0:30], ub[:, cc, 512:542], eng="dve")
                P.act(usq[:, cc, :], acc[:, cc, :], AF.Square)
            bm = pb[bi % 8]; bi += 1
            bq = pb[bi % 8]; bi += 1
            for cc in range(2):
                P.mm(bm, avg_f, acc[:, cc, :], start=(cc == 0), stop=(cc == 1))
            for cc in range(2):
                P.mm(bq, avg_f, usq[:, cc, :], start=(cc == 0), stop=(cc == 1))
            P.copy(mean_sb, bm, eng="act")
            P.tt(var_sb, mean_sb, mean_sb, ALU.mult)
            P.tt(var_sb, bq, var_sb, ALU.subtract)
            P.ts(var_sb, var_sb, EPS, None, ALU.add)
            P.act(var_sb, var_sb, AF.Sqrt)
            P.recip(var_sb, var_sb)
            for cc in range(2):
                eng = ("dve", "pool")[cc]
                P.tt(acc[:, cc, :], acc[:, cc, :], mean_sb, ALU.subtract, eng=eng)
                P.tt(acc[:, cc, :], acc[:, cc, :], var_sb, ALU.mult, eng=eng)
                P.act(usq[:, cc, :], acc[:, cc, :], AF.Identity, scale=prmT[:, cc, 35:36], bias=prmT[:, cc, 36:37])
                P.act(sg[:, cc, :], usq[:, cc, :], AF.Sigmoid)
                P.tt(ydT[:, cc, :], usq[:, cc, :], sg[:, cc, :], ALU.mult, eng=eng)
            P.dma(mixT[768:1024, t0:t0 + 512].re("(c p) t -> p c t", p=128), ydT)
        P.barrier()

        A.reset()
        negT = A.alloc("negT", [128, 128], BF16)
        csel = A.alloc("csel", [33, 128], BF16)
        mk = A.alloc("mk", [128, 4, 512], BF16)
        stg = A.alloc("stg", [128, 4, 512], F32)
        P.dma(stg[:, 0, 0:128], k_negT)
        P.copy(negT, stg[:, 0, 0:128])
        P.dma(stg[0:33, 1, 0:128], k_csel)
        P.copy(csel, stg[0:33, 1, 0:128])
        stg2 = A.alloc("stg2", [128, 4, 512], F32)
        P.dma(stg2, k_sbmask)
        P.copy(mk, stg2)
        vall = A.alloc("vall", [128, NS, G], BF16)
        for c0 in range(0, NS, 16):
            c1 = min(NS, c0 + 16)
            P.dma(vall[:, c0:c1, :], sbv[c0 * 128:c1 * 128, :].re("(c p) f -> p c f", p=128))
        qTh = [A.alloc("qTh%d" % i, [64, S], BF16) for i in range(2)]
        kTh = [A.alloc("kTh%d" % i, [64, S], BF16) for i in range(2)]
        R = 3
        negO = A.alloc("negO", [128, 128], BF16)
        P.memset(negO, -1.0)
        e_sb = [A.alloc("e_sb%d" % i, [128, 512], F32) for i in range(2)]
        L_b = [A.alloc("L_b%d" % i, [128, 512], BF16) for i in range(4)]
        A_b = [A.alloc("A_b%d" % i, [128, 512], BF16) for i in range(R)]
        ncb = [A.alloc("ncb%d" % i, [33, 512], BF16) for i in range(R)]
        ncf = A.alloc("ncf", [33, 512], F32)
        yo = [A.alloc("yo%d" % i, [64, 512], BF16) for i in range(2)]
        for r in range(R):
            P.memset(ncb[r], 0.0)
        X = pb[0:4]
        Cs = pb[4:6]
        Ob = pb[6:8]
        tcount = 0
        for h in range(NH):
            qT = qTh[h % 2]
            kT = kTh[h % 2]
            P.dma(qT, sbqT[h * 64:(h + 1) * 64, :])
            P.dma(kT, sbkT[h * 64:(h + 1) * 64, :])
            for qt in range(NT):
                O = Ob[tcount % 2]
                yv = yo[tcount % 2]
                tcount += 1
                steps = list(range(4 * qt + 3, -1, -1))
                n = len(steps)
                P.memset(ncf, 0.0)
                P.memset(ncb[0][0:1, :], 0.0)
                P.memset(ncb[0][32:33, :], 0.0)
                qtile = qT[:, qt * 512:(qt + 1) * 512]

                def stA(s):
                    kc = steps[s]
                    P.mm(X[s % 4], kT[:, kc * 128:(kc + 1) * 128], qtile, start=True, stop=False,
                         skip_group_check=True)

                def stB(s):
                    kc = steps[s]
                    P.act(e_sb[s % 2], X[s % 4], AF.Exp)
                    P.act(L_b[s % 4], e_sb[s % 2], AF.Ln, bias=1.0)
                    if kc >= 4 * qt:
                        P.tt(L_b[s % 4], L_b[s % 4], mk[:, kc - 4 * qt, :], ALU.mult, eng="pool")

                def stC(s):
                    pr = s // 2
                    P.mm(Cs[pr % 2][0:33, :], ones_b[:, 0:33], L_b[s % 4], start=(s % 2 == 0), stop=(s % 2 == 1))
                    if s % 2 == 1 and s + 1 < n:
                        nb_ = ncb[(pr + 1) % R]
                        P.tt(ncf, ncf, Cs[pr % 2][0:33, :], ALU.subtract)
                        P.copy(nb_, ncf)
                        P.tt(nb_[32:33, :], ncf[32:33, :], nb_[32:33, :], ALU.subtract)

                def stD(s):
                    pr = s // 2
                    P.mm(X[s % 4], negT, L_b[s % 4], start=False, stop=False, skip_group_check=True)
                    if s % 2 == 1:
                        P.mm(X[s % 4], negO, L_b[(s - 1) % 4], start=False, stop=False, skip_group_check=True)
                    P.mm(X[s % 4], csel, ncb[pr % R], start=False, stop=True, skip_group_check=True)

                def stE(s):
                    kc = steps[s]
                    P.act(A_b[s % R], X[s % 4], AF.Exp)
                    if kc >= 4 * qt:
                        P.tt(A_b[s % R], A_b[s % R], mk[:, kc - 4 * qt, :], ALU.mult, eng="pool")

                def stF(s):
                    kc = steps[s]
                    P.mm(O[0:64, :], vall[:, kc, h * 64:(h + 1) * 64], A_b[s % R],
                         start=(s == 0), stop=(s == n - 1))

                stA(0)
                for it in range(n + 2):
                    if it + 1 < n:
                        stA(it + 1)
                    if it < n:
                        stB(it)
                        stC(it)
                    if 0 <= it - 1 < n:
                        stD(it - 1)
                    if 0 <= it - 2 < n:
                        stE(it - 2)
                        stF(it - 2)
                P.copy(yv, O[0:64, :], eng="dve")
                P.dma(mixT[256 + h * 64:256 + (h + 1) * 64, qt * 512:(qt + 1) * 512], yv)
        P.barrier()

        A.reset()
        cb = A.alloc("cb", [128, 4, 512], BF16)
        stg2 = A.alloc("stg2", [128, 4, 512], F32)
        P.dma(stg2, k_mbcb)
        P.copy(cb, stg2)
        blk = A.alloc("blk", [128, 32], F32)
        P.dma(blk, k_blk)
        pbias = A.alloc("pbias", [128, 32, 32], F32)
        ownm = A.alloc("ownm", [128, 32, 32], F32)
        for o in range(NB):
            P.ts(pbias[:, o, :], blk, float(o), -1e9, ALU.is_ge, ALU.mult)
            P.ts(ownm[:, o, :], blk, float(o), None, ALU.is_equal, eng="pool")
        oh = A.alloc("oh", [32, 32, 128], BF16)
        P.copy(oh, ident_b[0:32, 0:32].unsq(2).bcast([32, 32, 128]))
        vall = A.alloc("vall", [128, NS, G], BF16)
        for c0 in range(0, NS, 16):
            c1 = min(NS, c0 + 16)
            P.dma(vall[:, c0:c1, :], mbv[c0 * 128:c1 * 128, :].re("(c p) f -> p c f", p=128))
        vaug = [A.alloc("vaug%d" % i, [128, NS, 65], BF16) for i in range(2)]
        qTh = [A.alloc("qTh%d" % i, [64, S], BF16) for i in range(2)]
        kTh = [A.alloc("kTh%d" % i, [64, S], BF16) for i in range(2)]
        kmf = A.alloc("kmf", [64, 32], F32)
        kmh = A.alloc("kmh", [64, 32], BF16)
        kml = A.alloc("kml", [64, 32], BF16)
        kmr = A.alloc("kmr", [64, 32], F32)
        g2 = A.alloc("g2", [128, 32], F32)
        top8 = A.alloc("top8", [128, 8], F32)
        thr = A.alloc("thr", [128, 1], F32)
        sel = A.alloc("sel", [128, 32], F32)
        selb = A.alloc("selb", [128, 4, 32], BF16)
        selbT = [A.alloc("selbT%d" % i, [32, S], BF16) for i in range(2)]
        P_b = [A.alloc("P_b%d" % i, [128, 512], BF16) for i in range(4)]
        O_sb = A.alloc("O_sb", [65, 512], F32)
        rl = A.alloc("rl", [65, 512], F32)
        yo = [A.alloc("yo%d" % i, [64, 512], BF16) for i in range(2)]
        for i in range(2):
            P.memset(vaug[i][:, :, 64:65], 1.0)
        P.memset(kmf, 0.0)
        X = pb[0:4]
        Ob = pb[4:6]
        Gp = pb[6]
        Tp = pb[7]
        Bc = pb[6]
        tcount = 0
        for h in range(NH):
            qT = qTh[h % 2]
            kT = kTh[h % 2]
            va = vaug[h % 2]
            sT = selbT[h % 2]
            P.dma(qT, mbqT[h * 64:(h + 1) * 64, :])
            P.dma(kT, mbkT[h * 64:(h + 1) * 64, :])
            P.copy(va[:, :, 0:64], vall[:, :, h * 64:(h + 1) * 64], eng="pool")
            P.reduce(kmf[:, 0:NB], kT.re("p (n b) -> p n b", b=256), ALU.add)
            P.ts(kmf, kmf, 1.0 / 256, None, ALU.mult)
            P.copy(kmh, kmf)
            P.tt(kmr, kmf, kmh, ALU.subtract)
            P.copy(kml, kmr)
            for qt in range(NT):
                for st in range(4):
                    sub = qt * 4 + st
                    own = sub // 2
                    P.mm(Gp[:, 0:32], qT[:, sub * 128:(sub + 1) * 128], kmh, start=True, stop=False)
                    P.mm(Gp[:, 0:32], qT[:, sub * 128:(sub + 1) * 128], kml, start=False, stop=True)
                    P.tt(g2, Gp[:, 0:32], pbias[:, own, :], ALU.add)
                    P.generic("dve", lambda e, o_=top8, i_=g2: e.max(o_.ap, i_.ap), [g2], [top8])
                    P.ts(thr, top8[:, 2:3], -1e8, None, ALU.max)
                    P.ts(sel, g2, thr, None, ALU.is_ge)
                    P.tt(sel, sel, ownm[:, own, :], ALU.max)
                    P.ts(selb[:, st, :], sel, -1.0, -NEG, ALU.add, ALU.mult)
                Tpb = Tp.bc(BF16)
                for st in range(4):
                    P.transpose(Tpb[0:32, st * 128:(st + 1) * 128], selb[:, st, :], ident_b)
                P.copy(sT[:, qt * 512:(qt + 1) * 512], Tpb[0:32, 0:512], eng="act")
            for qt in range(NT):
                O = Ob[tcount % 2]
                yv = yo[tcount % 2]
                tcount += 1
                n = 4 * qt + 4
                qtile = qT[:, qt * 512:(qt + 1) * 512]

                def mA(kc):
                    jb = kc // 2
                    diag = kc >= 4 * qt
                    P.mm(X[kc % 4], kT[:, kc * 128:(kc + 1) * 128], qtile, start=True, stop=False)
                    P.mm(X[kc % 4], oh[:, jb, :], sT[:, qt * 512:(qt + 1) * 512], start=False, stop=not diag)
                    if diag:
                        P.mm(X[kc % 4], ident_b, cb[:, kc - 4 * qt, :], start=False, stop=True)

                def mB(kc):
                    P.act(P_b[kc % 4], X[kc % 4], AF.Exp, scale=0.125, bias=negm)

                def mC(kc):
                    P.mm(O[0:65, :], va[:, kc, :], P_b[kc % 4], start=(kc == 0), stop=(kc == n - 1))

                mA(0)
                if n > 1:
                    mA(1)
                for kc in range(n + 1):
                    if kc + 2 < n:
                        mA(kc + 2)
                    if kc < n:
                        mB(kc)
                    if kc >= 1:
                        mC(kc - 1)
                P.copy(O_sb, O[0:65, :], eng="act")
                P.recip(rl[64:65, :], O_sb[64:65, :])
                P.mm(Bc[0:64, :], ones_f[64:65, 0:64], rl[64:65, :])
                P.tt(yv, O_sb[0:64, :], Bc[0:64, :], ALU.mult)
                P.dma(mixT[512 + h * 64:512 + (h + 1) * 64, qt * 512:(qt + 1) * 512], yv)
        P.barrier()

        A.reset()
        Wo = A.alloc("Wo", [128, 8, D], BF16)
        gbc = A.alloc("gbc", [128, D], F32)
        wst = [A.alloc("wst%d" % i, [128, 4, D], F32) for i in range(2)]
        P.dma(gbc, modrow[l:l + 1, 2 * D:3 * D].bcast([128, D]))
        for n in range(2):
            w = wst[n % 2]
            P.dma(w, w_out[l][n * 512:(n + 1) * 512, :].re("(j p) n -> p j n", p=128))
            P.tt(Wo[:, n * 4:(n + 1) * 4, :], w, gbc.unsq(1).bcast([128, 4, D]), ALU.mult,
                 eng=("dve", "pool")[n % 2])
        mx = [A.alloc("mx%d" % i, [128, 8, 512], BF16) for i in range(2)]
        xs = [A.alloc("xs%d" % i, [128, D], F32) for i in range(2)]
        x1 = [A.alloc("x1_%d" % i, [128, D], F32) for i in range(2)]
        junk = A.alloc("junk", [128, D], BF16)
        ss = A.alloc("ss", [128, 4], F32)
        rs = A.alloc("rs", [128, 4], F32)
        xn = A.alloc("xn", [128, 4, D], BF16)
        hT = [A.alloc("hT%d" % i, [128, 8, 512], BF16) for i in range(2)]
        bi = 0
        for i in range(NT):
            t0 = i * 512
            m = mx[i % 2]
            P.dma(m, mixT[:, t0:t0 + 512].re("(c p) t -> p c t", p=128))
            for st in range(4):
                xt = xs[st % 2]
                x1t = x1[st % 2]
                P.dma(xt, xin[t0 + st * 128:t0 + (st + 1) * 128, :])
                for n in range(2):
                    bank = pb[bi % 8]; bi += 1
                    for j in range(8):
                        P.mm(bank, m[:, j, st * 128:(st + 1) * 128], Wo[:, j, n * 512:(n + 1) * 512],
                             start=(j == 0), stop=(j == 7))
                    P.tt(x1t[:, n * 512:(n + 1) * 512], bank, xt[:, n * 512:(n + 1) * 512], ALU.add)
                P.dma(x1s[t0 + st * 128:t0 + (st + 1) * 128, :], x1t)
                P.act(junk, x1t, AF.Square, accum_out=ss[:, st:st + 1])
                rstd_from_ss(rs[:, st:st + 1], ss[:, st:st + 1], D)
                P.act(xn[:, st, :], x1t, AF.Copy, scale=rs[:, st:st + 1])
            ht = hT[i % 2]
            for j in range(8):
                bank = pb[bi % 8]; bi += 1
                bkb = bank.bc(BF16)
                for st in range(4):
                    P.transpose(bkb[:, st * 128:(st + 1) * 128], xn[:, st, j * 128:(j + 1) * 128], ident_b)
                P.act(ht[:, j, :], bkb[:, 0:512], AF.Identity, scale=G2[:, j:j + 1], bias=sh2[:, j:j + 1])
            P.dma(h2Ts[:, t0:t0 + 512].re("(c p) t -> p c t", p=128), ht)
        P.barrier()

        A.reset()
        W1 = A.alloc("W1", [128, 8, DFF], BF16)
        W2 = A.alloc("W2", [128, 32, D], BF16)
        mark = A.off
        gbc = A.alloc("gbc", [128, D], F32)
        wst = [A.alloc("wst%d" % i, [128, 8, 512], F32) for i in range(2)]
        P.dma(gbc, modrow[l:l + 1, 5 * D:6 * D].bcast([128, D]))
        for n in range(8):
            w = wst[n % 2]
            P.dma(w, w_mlp1[l][:, n * 512:(n + 1) * 512].re("(j p) n -> p j n", p=128))
            P.copy(W1[:, :, n * 512:(n + 1) * 512], w, eng=("dve", "pool")[n % 2])
        for n in range(8):
            w = wst[n % 2].re("p j n -> p (j n)").re("p (j n) -> p j n", j=4)
            P.dma(w, w_mlp2[l][n * 512:(n + 1) * 512, :].re("(j p) n -> p j n", p=128))
            P.tt(W2[:, n * 4:(n + 1) * 4, :], w, gbc.unsq(1).bcast([128, 4, D]), ALU.mult,
                 eng=("dve", "pool")[n % 2])
        P.barrier()
        A.off = mark
        A.gen += 1
        h2 = [A.alloc("h2_%d" % i, [128, 8, 256], BF16) for i in range(2)]
        f1 = A.alloc("f1", [128, 32, 256], BF16)
        rbuf = [A.alloc("rbuf%d" % i, [128, 256], BF16) for i in range(3)]
        x1 = [A.alloc("x1_%d" % i, [128, D], F32) for i in range(2)]
        x2 = [A.alloc("x2_%d" % i, [128, D], F32) for i in range(2)]
        bi = 0
        for i in range(S // 256):
            t0 = i * 256
            hh = h2[i % 2]
            P.dma(hh, h2Ts[:, t0:t0 + 256].re("(c p) t -> p c t", p=128))
            for fc in range(32):
                bank = pb[bi % 8]; bi += 1
                for j in range(8):
                    P.mm(bank[:, 0:256], W1[:, j, fc * 128:(fc + 1) * 128], hh[:, j, :],
                         start=(j == 0), stop=(j == 7))
                rbf = rbuf[fc % 3]
                P.act(rbf, bank[:, 0:256], AF.Relu)
                P.tt(f1[:, fc, :], rbf, rbf, ALU.mult, eng=("dve", "pool")[fc % 2])
            for st in range(2):
                x1t = x1[st]
                x2t = x2[st]
                P.dma(x1t, x1s[t0 + st * 128:t0 + (st + 1) * 128, :])
                for n in range(2):
                    bank = pb[bi % 8]; bi += 1
                    for fc in range(32):
                        P.mm(bank, f1[:, fc, st * 128:(st + 1) * 128], W2[:, fc, n * 512:(n + 1) * 512],
                             start=(fc == 0), stop=(fc == 31))
                    P.tt(x2t[:, n * 512:(n + 1) * 512], bank, x1t[:, n * 512:(n + 1) * 512], ALU.add)
                P.dma(xout[t0 + st * 128:t0 + (st + 1) * 128, :], x2t)
        P.barrier()

    P.emit()
    return nc, P


_CACHE = {}


def _get_nc(S, depth):
    key = (S, depth)
    if key not in _CACHE:
        _CACHE[key] = build(S, depth)[0]
    return _CACHE[key]


def make_in_maps(inputs, S, depth, nb):
    consts = host_consts()
    maps = []
    shared = {}
    for name in ("w_ada", "b_ada", "g_norm1", "w_in", "w_sconv", "w_cconv", "b_cconv", "g_cln", "b_cln",
                 "g_q", "g_k", "w_out", "g_norm2", "w_mlp1", "w_mlp2"):
        shared[name] = np.ascontiguousarray(inputs[name], dtype=np.float32)
    for k, v in consts.items():
        shared["k_" + k] = v
    x = np.asarray(inputs["x"], dtype=np.float32)
    c = np.asarray(inputs["c"], dtype=np.float32)
    pos = np.asarray(inputs["positions"], dtype=np.int32)
    for b in range(nb):
        m = dict(shared)
        m["x"] = np.ascontiguousarray(x[b])
        m["cT"] = np.ascontiguousarray(c[b].reshape(8, 128).T)
        m["pos"] = np.ascontiguousarray(pos[b].reshape(128, S // 128))
        maps.append(m)
    return maps


def kernel(**inputs):
    x = inputs["x"]
    nb, S, _ = x.shape
    depth = inputs["w_ada"].shape[0]
    nc = _get_nc(S, depth)
    maps = make_in_maps(inputs, S, depth, nb)
    res = run_bass_kernel_spmd(nc, maps, core_ids=list(range(nb)))
    return np.stack([np.asarray(r["out"], dtype=np.float32) for r in res.results], axis=0)
```

```python
import numpy as np
from contextlib import ExitStack
import concourse.bass as bass
import concourse.mybir as mybir
from concourse.bass_utils import run_bass_kernel_spmd

F32 = mybir.dt.float32
BF16 = mybir.dt.bfloat16
I32 = mybir.dt.int32
AF = mybir.ActivationFunctionType
ALU = mybir.AluOpType
AX = mybir.AxisListType

NDSEM = 12
D = 1024
G = 256
NH = 4
HD = 64
DFF = 4096
INW = 11 * G
EPS = 1e-6
NEG = -30000.0


class V:
    __slots__ = ("key", "ap")

    def __init__(self, key, ap):
        self.key = key
        self.ap = ap

    def __getitem__(self, idx):
        return V(self.key, self.ap[idx])

    def k(self, sub):
        return V((self.key, sub), self.ap)

    def re(self, pat, **kw):
        return V(self.key, self.ap.rearrange(pat, **kw))

    def bc(self, dt):
        return V(self.key, self.ap.bitcast(dt))

    def bcast(self, shape):
        return V(self.key, self.ap.broadcast_to(list(shape)))

    def unsq(self, ax):
        return V(self.key, self.ap.unsqueeze(ax))


class Op:
    __slots__ = ("eng", "emit", "deps", "signal", "is_dma", "ev_sem", "ev_val")

    def __init__(self, eng, emit, is_dma=False):
        self.eng = eng
        self.emit = emit
        self.deps = []
        self.signal = False
        self.is_dma = is_dma
        self.ev_sem = None
        self.ev_val = 0


class Prog:
    ENGS = ("sp", "act", "dve", "pool", "pe")

    def __init__(self, nc):
        self.nc = nc
        self.es = ExitStack()
        self.ops = {e: [] for e in self.ENGS}
        self.lastreal = {e: None for e in self.ENGS}
        self.lastw = {}
        self.readers = {}
        self.dma_count = {e: 0 for e in self.ENGS}
        self.dma_last = {}
        self.n = 0

    def sb(self, name, shape, dt):
        t = self.es.enter_context(self.nc.sbuf_tensor(name, list(shape), dt))
        return V(name, t[:])

    def ps(self, name, shape, dt):
        t = self.es.enter_context(self.nc.psum_tensor(name, list(shape), dt))
        return V(name, t[:])

    def dram(self, name, shape, dt, kind="Internal"):
        t = self.nc.dram_tensor(name, list(shape), dt, kind=kind)
        return V(name, t.ap())

    def _add(self, eng, emit, reads, writes, is_dma=False):
        op = Op(eng, emit, is_dma)
        rk = [v.key for v in reads if v is not None]
        wk = [v.key for v in writes if v is not None]
        deps = []
        for k in rk:
            p = self.lastw.get(k)
            if p is not None:
                deps.append((p, True))
        for k in wk:
            p = self.lastw.get(k)
            if p is not None:
                deps.append((p, False))
            for r in self.readers.get(k, ()):
                deps.append((r, False))
        for p, raw in deps:
            if p is op:
                continue
            if p.is_dma:
                need = True
            elif p.eng != eng:
                need = True
            elif eng == "pe":
                need = False
            else:
                need = True
            if need and p not in op.deps:
                p.signal = True
                op.deps.append(p)
        if is_dma:
            op.signal = True
            c = self.dma_count[eng]
            self.dma_count[eng] = c + 1
            slot = (eng, c % NDSEM)
            prev = self.dma_last.get(slot)
            if prev is not None:
                op.deps.append(prev)
            self.dma_last[slot] = op
            op.ev_sem = slot
            op.ev_val = 16 * (c // NDSEM + 1)
        for k in wk:
            self.lastw[k] = op
            self.readers[k] = []
        for k in rk:
            if k not in wk:
                self.readers.setdefault(k, []).append(op)
        self.ops[eng].append(op)
        self.lastreal[eng] = op
        self.n += 1
        return op

    def barrier(self):
        lasts = []
        for e in self.ENGS:
            p = self.lastreal[e]
            if p is not None and not p.is_dma:
                p.signal = True
                lasts.append(p)
        lasts.extend(self.dma_last.values())
        for e in self.ENGS:
            op = Op(e, None)
            op.deps = [p for p in lasts if (p.is_dma or p.eng != e)]
            self.ops[e].append(op)
        self.lastw.clear()
        self.readers.clear()

    def dma(self, out, in_, eng="sp", **kw):
        return self._add(eng, lambda e: e.dma_start(out=out.ap, in_=in_.ap, **kw), [in_], [out], is_dma=True)

    def mm(self, out, lhsT, rhs, start=True, stop=True, **kw):
        return self._add("pe", lambda e: e.matmul(out.ap, lhsT.ap, rhs.ap, start=start, stop=stop, **kw),
                         [lhsT, rhs], [out])

    def transpose(self, out, in_, ident):
        return self._add("pe", lambda e: e.transpose(out.ap, in_.ap, ident.ap), [in_, ident], [out])

    def act(self, out, in_, func, bias=None, scale=None, accum_out=None, eng="act"):
        def emit(e):
            kw = {}
            if bias is not None:
                kw["bias"] = bias.ap if isinstance(bias, V) else bias
            if scale is not None:
                kw["scale"] = scale.ap if isinstance(scale, V) else scale
            if accum_out is not None:
                kw["accum_out"] = accum_out.ap
            return e.activation(out=out.ap, in_=in_.ap, func=func, **kw)
        rd = [in_] + [b for b in (bias, scale) if isinstance(b, V)]
        wr = [out] + ([accum_out] if accum_out is not None else [])
        return self._add(eng, emit, rd, wr)

    def ts(self, out, in0, s1, s2, op0, op1=None, eng="dve"):
        def emit(e):
            a1 = s1.ap if isinstance(s1, V) else s1
            a2 = s2.ap if isinstance(s2, V) else s2
            kw = {}
            if op1 is not None:
                kw["op1"] = op1
            return e.tensor_scalar(out.ap, in0.ap, a1, a2, op0, **kw)
        rd = [in0] + [s for s in (s1, s2) if isinstance(s, V)]
        return self._add(eng, emit, rd, [out])

    def tt(self, out, in0, in1, op, eng="dve"):
        return self._add(eng, lambda e: e.tensor_tensor(out.ap, in0.ap, in1.ap, op), [in0, in1], [out])

    def stt(self, out, in0, scalar, in1, op0, op1, eng="dve"):
        def emit(e):
            s = scalar.ap if isinstance(scalar, V) else scalar
            return e.scalar_tensor_tensor(out.ap, in0.ap, s, in1.ap, op0, op1)
        rd = [in0, in1] + ([scalar] if isinstance(scalar, V) else [])
        return self._add(eng, emit, rd, [out])

    def copy(self, out, in_, eng="dve"):
        if eng == "act":
            return self.act(out, in_, AF.Copy)
        return self._add(eng, lambda e: e.tensor_copy(out.ap, in_.ap), [in_], [out])

    def memset(self, out, val, eng="dve"):
        return self._add(eng, lambda e: e.memset(out.ap, val), [], [out])

    def reduce(self, out, in_, op, eng="dve"):
        return self._add(eng, lambda e: e.tensor_reduce(out.ap, in_.ap, AX.X, op), [in_], [out])

    def recip(self, out, in_):
        return self._add("dve", lambda e: e.reciprocal(out.ap, in_.ap), [in_], [out])

    def generic(self, eng, fn, reads, writes):
        return self._add(eng, fn, reads, writes)

    def emit(self):
        nc = self.nc
        es = self.es
        esem = {e: es.enter_context(nc.semaphore("s_" + e)) for e in self.ENGS}
        dsem = {}
        for e in self.ENGS:
            for i in range(min(NDSEM, self.dma_count[e])):
                dsem[(e, i)] = es.enter_context(nc.semaphore("d_%s_%d" % (e, i)))
        for e in self.ENGS:
            c = 0
            for op in self.ops[e]:
                if op.is_dma or op.emit is None:
                    continue
                if op.signal:
                    c += 1
                    op.ev_sem = e
                    op.ev_val = c
        prog = self

        def run(ename, eng):
            waited = {}
            for op in prog.ops[ename]:
                for p in op.deps:
                    key = p.ev_sem
                    sem = dsem[key] if p.is_dma else esem[key]
                    if waited.get(key, 0) < p.ev_val:
                        eng.wait_ge(sem, p.ev_val)
                        waited[key] = p.ev_val
                if op.emit is None:
                    continue
                ins = op.emit(eng)
                if op.signal:
                    if op.is_dma:
                        ins.then_inc(dsem[op.ev_sem], 16)
                    else:
                        ins.then_inc(esem[ename], 1)
            for (e2, i), last in prog.dma_last.items():
                if e2 == ename and waited.get((e2, i), 0) < last.ev_val:
                    eng.wait_ge(dsem[(e2, i)], last.ev_val)

        with nc.Block() as block:
            @block.sync
            def _(e):
                run("sp", e)

            @block.scalar
            def _(e):
                run("act", e)

            @block.vector
            def _(e):
                run("dve", e)

            @block.gpsimd
            def _(e):
                run("pool", e)

            @block.tensor
            def _(e):
                run("pe", e)
        es.close()


class Arena:
    def __init__(self, P, name, nbytes):
        self.v = P.sb(name, [128, nbytes // 4], F32)
        self.cap = nbytes
        self.off = 0
        self.gen = 0

    def reset(self):
        self.off = 0
        self.gen += 1

    def alloc(self, name, shape, dt):
        esz = 4 if dt in (F32, I32) else 2
        nel = int(np.prod(shape[1:]))
        n4 = (nel * esz + 3) // 4
        o4 = self.off // 4
        assert self.off + n4 * 4 <= self.cap, ("arena overflow", name, self.off, n4 * 4, self.cap)
        ap = self.v.ap[0:shape[0], o4:o4 + n4]
        if dt != F32:
            ap = ap.bitcast(dt)
            if esz == 2 and nel != n4 * 2:
                ap = ap[:, 0:nel]
        if len(shape) == 3:
            ap = ap.rearrange("p (a b) -> p a b", a=shape[1])
        elif len(shape) == 4:
            ap = ap.rearrange("p (a b c) -> p a b c", a=shape[1], b=shape[2])
        self.off += ((n4 * 4 + 63) // 64) * 64
        return V((name, self.gen), ap)


def host_consts():
    c = {}
    c["ident"] = np.eye(128, dtype=np.float32)
    s1 = np.arange(128)[:, None]
    s0 = np.arange(128)[None, :]
    c["negT"] = np.where(s1 >= s0, -1.0, 0.0).astype(np.float32)
    t = np.arange(512)[None, :]
    sb = np.zeros((128, 4, 512), np.float32)
    cb = np.zeros((128, 4, 512), np.float32)
    for cc in range(4):
        sb[:, cc, :] = (t > 128 * cc + s1).astype(np.float32)
        cb[:, cc, :] = np.where(t >= 128 * cc + s1, 0.0, NEG)
    c["sbmask"] = sb
    c["mbcb"] = cb
    c["blk"] = np.tile(np.arange(32, dtype=np.float32)[None, :], (128, 1))
    half = HD // 2
    inv = np.exp(-np.log(10000.0) * np.arange(half, dtype=np.float32) / half).astype(np.float32)
    c["invf"] = np.tile(inv[None, :], (128, 1)).astype(np.float32)
    cs = np.zeros((128, 128), np.float32)
    cs[0, :] = 1.0
    cs[32, :] = 1.0
    c["csel"] = cs
    return c


def build(S, depth):
    assert S % 512 == 0
    nc = bass.Bass("TRN2", target_bir_lowering=False)
    P = Prog(nc)
    NT = S // 512
    NS = S // 128
    NB = S // 256
    ext = lambda n, s, d: P.dram(n, s, d, kind="ExternalInput")
    x_in = ext("x", [S, D], F32)
    cT_in = ext("cT", [128, 8], F32)
    pos_in = ext("pos", [128, S // 128], I32)
    w_ada = ext("w_ada", [depth, D, 6 * D], F32)
    b_ada = ext("b_ada", [depth, 6 * D], F32)
    g_norm1 = ext("g_norm1", [depth, D], F32)
    w_in = ext("w_in", [depth, D, INW], F32)
    w_sconv = ext("w_sconv", [depth, 3, G], F32)
    w_cconv = ext("w_cconv", [depth, 31, G], F32)
    b_cconv = ext("b_cconv", [depth, G], F32)
    g_cln = ext("g_cln", [depth, G], F32)
    b_cln = ext("b_cln", [depth, G], F32)
    g_q = ext("g_q", [depth, HD], F32)
    g_k = ext("g_k", [depth, HD], F32)
    w_out = ext("w_out", [depth, D, D], F32)
    g_norm2 = ext("g_norm2", [depth, D], F32)
    w_mlp1 = ext("w_mlp1", [depth, D, DFF], F32)
    w_mlp2 = ext("w_mlp2", [depth, DFF, D], F32)
    k_ident = ext("k_ident", [128, 128], F32)
    k_negT = ext("k_negT", [128, 128], F32)
    k_sbmask = ext("k_sbmask", [128, 4, 512], F32)
    k_mbcb = ext("k_mbcb", [128, 4, 512], F32)
    k_blk = ext("k_blk", [128, 32], F32)
    k_invf = ext("k_invf", [128, 32], F32)
    k_csel = ext("k_csel", [128, 128], F32)
    out = P.dram("out", [S, D], F32, kind="ExternalOutput")

    xmid = P.dram("xmid", [S, D], F32)
    x1s = P.dram("x1s", [S, D], F32)
    h2Ts = P.dram("h2Ts", [D, S], BF16)
    mixT = P.dram("mixT", [D, S], BF16)
    sbqT = P.dram("sbqT", [G, S], BF16)
    sbkT = P.dram("sbkT", [G, S], BF16)
    mbqT = P.dram("mbqT", [G, S], BF16)
    mbkT = P.dram("mbkT", [G, S], BF16)
    sbv = P.dram("sbv", [S, G], BF16)
    mbv = P.dram("mbv", [S, G], BF16)
    csd = P.dram("csd", [S, 64], F32)
    modrow = P.dram("modrow", [depth, 6 * D], F32)

    ident_f = P.sb("ident_f", [128, 128], F32)
    ident_b = P.sb("ident_b", [128, 128], BF16)
    ones_f = P.sb("ones_f", [128, 128], F32)
    ones_b = P.sb("ones_b", [128, 128], BF16)
    avg_f = P.sb("avg_f", [128, 128], F32)
    modc = P.sb("modc", [128, 64], F32)
    G1 = P.sb("G1", [128, 8], F32)
    G2 = P.sb("G2", [128, 8], F32)
    prmT = P.sb("prmT", [128, 2, 37], F32)
    gqk_bc = P.sb("gqk_bc", [128, 512], F32)
    negm = P.sb("negm", [128, 1], F32)
    epsc = P.sb("epsc", [128, 1], F32)
    A = Arena(P, "arena", 203 * 1024)
    pb = [P.ps("pb%d" % i, [128, 512], F32) for i in range(8)]

    P.dma(ident_f, k_ident)
    P.copy(ident_b, ident_f)
    P.memset(ones_f, 1.0)
    P.memset(ones_b, 1.0)
    P.memset(avg_f, 1.0 / 256)
    P.memset(epsc, EPS)

    def rstd_from_ss(rs, ss, n):
        P.ts(rs, ss, 1.0 / n, EPS, ALU.mult, ALU.add)
        P.act(rs, rs, AF.Sqrt)
        P.recip(rs, rs)

    A.reset()
    NJ = S // 128
    posi = A.alloc("posi", [128, NJ], I32)
    posf = A.alloc("posf", [128, NJ], F32)
    invf = A.alloc("invf", [128, 32], F32)
    ang = A.alloc("ang", [128, NJ, 32], F32)
    tmpa = A.alloc("tmpa", [128, NJ, 32], F32)
    cst = A.alloc("cst", [128, NJ, 64], F32)
    mpi = A.alloc("mpi", [128, 1], F32)
    P.memset(mpi, -float(np.pi))
    P.dma(posi, pos_in)
    P.dma(invf, k_invf)
    P.copy(posf, posi)
    P.tt(ang, posf.unsq(2).bcast([128, NJ, 32]), invf.unsq(1).bcast([128, NJ, 32]), ALU.mult)
    TWO_PI = float(2 * np.pi)
    C1 = 6.28125
    C2 = TWO_PI - C1
    PI_ = float(np.pi)
    ni = A.alloc("ni", [128, NJ, 32], I32)
    nf = A.alloc("nf", [128, NJ, 32], F32)
    rr = A.alloc("rr", [128, NJ, 32], F32)
    mm_ = A.alloc("mm_", [128, NJ, 32], F32)
    P.ts(tmpa, ang, 1.0 / TWO_PI, None, ALU.mult)
    P.copy(ni, tmpa)
    P.copy(nf, ni)
    P.stt(rr, nf, -C1, ang, ALU.mult, ALU.add)
    P.stt(rr, nf, -C2, rr, ALU.mult, ALU.add)

    def fold(t):
        P.ts(mm_, t, PI_, None, ALU.is_gt)
        P.stt(t, mm_, -TWO_PI, t, ALU.mult, ALU.add)
        P.ts(mm_, t, -PI_, None, ALU.is_lt)
        P.stt(t, mm_, TWO_PI, t, ALU.mult, ALU.add)
        P.ts(t, t, 3.14159, -3.14159, ALU.min, ALU.max)

    fold(rr)
    P.act(cst[:, :, 32:64], rr, AF.Sin)
    P.ts(tmpa, rr, PI_ / 2, None, ALU.add)
    fold(tmpa)
    P.act(cst[:, :, 0:32], tmpa, AF.Sin)
    P.dma(csd.re("(p j) f -> p j f", p=128), cst)
    P.barrier()

    for l in range(depth):
        xin = x_in if l == 0 else xmid
        xout = out if l == depth - 1 else xmid
        A.reset()
        sc = A.alloc("sc", [128, 8], F32)
        cTs = A.alloc("cTs", [128, 8], F32)
        rowbuf = A.alloc("rowbuf", [1, 8 * D], F32)
        brow = A.alloc("brow", [1, 6 * D], F32)
        wst = [A.alloc("wst%d" % i, [128, 8, 512], F32) for i in range(2)]
        prm = A.alloc("prm", [37, G], F32)
        grow = A.alloc("grow", [1, 512], F32)
        gsq = A.alloc("gsq", [1, 128], F32)
        gmx = A.alloc("gmx", [1, 2], F32)
        P.dma(cTs, cT_in)
        P.act(sc, cTs, AF.Sigmoid)
        P.tt(sc, sc, cTs, ALU.mult)
        P.dma(brow, b_ada[l:l + 1, :])
        P.dma(rowbuf[:, 6 * D:7 * D], g_norm1[l:l + 1, :])
        P.dma(rowbuf[:, 7 * D:8 * D], g_norm2[l:l + 1, :])
        P.dma(prm[0:3, :], w_sconv[l])
        P.dma(prm[3:34, :], w_cconv[l])
        P.dma(prm[34:35, :], b_cconv[l:l + 1, :])
        P.dma(prm[35:36, :], g_cln[l:l + 1, :])
        P.dma(prm[36:37, :], b_cln[l:l + 1, :])
        for h in range(NH):
            P.dma(grow[:, h * 64:(h + 1) * 64], g_q[l:l + 1, :])
            P.dma(grow[:, 256 + h * 64:256 + (h + 1) * 64], g_k[l:l + 1, :])
        for n in range(12):
            w = wst[n % 2]
            P.dma(w, w_ada[l][:, n * 512:(n + 1) * 512].re("(j p) n -> p j n", p=128))
            bank = pb[n % 2]
            for j in range(8):
                P.mm(bank[0:1, :], sc[:, j:j + 1], w[:, j, :], start=(j == 0), stop=False)
            P.mm(bank[0:1, :], ones_f[0:1, 0:1], brow[:, n * 512:(n + 1) * 512], start=False, stop=True)
            P.copy(rowbuf[:, n * 512:(n + 1) * 512], bank[0:1, :], eng="act")
        P.dma(modrow[l:l + 1, :], rowbuf[:, 0:6 * D])
        for piece in range(8):
            for j in range(8):
                col = piece * 8 + j
                P.mm(pb[2][:, col:col + 1], rowbuf[:, piece * D + j * 128: piece * D + (j + 1) * 128],
                     ones_f[0:1, 0:1], start=True, stop=True)
        P.copy(modc, pb[2][:, 0:64])
        P.stt(G1, modc[:, 8:16], 1.0, modc[:, 48:56], ALU.add, ALU.mult)
        P.stt(G2, modc[:, 32:40], 1.0, modc[:, 56:64], ALU.add, ALU.mult)
        sh1 = modc[:, 0:8]
        sh2 = modc[:, 24:32]
        for c2 in range(2):
            P.mm(pb[3][:, c2 * 37:(c2 + 1) * 37], prm[:, c2 * 128:(c2 + 1) * 128], ident_f[0:37, 0:37])
        P.copy(prmT.re("p c r -> p (c r)"), pb[3][:, 0:74])
        P.mm(pb[4], ones_f[0:1, :], grow)
        P.copy(gqk_bc, pb[4])
        P.tt(gsq[:, 0:64], grow[:, 0:64], grow[:, 0:64], ALU.mult)
        P.tt(gsq[:, 64:128], grow[:, 256:320], grow[:, 256:320], ALU.mult)
        P.reduce(gmx, gsq.re("p (a b) -> p a b", a=2), ALU.max)
        P.tt(gmx[:, 0:1], gmx[:, 0:1], gmx[:, 1:2], ALU.mult)
        P.act(gmx[:, 0:1], gmx[:, 0:1], AF.Sqrt)
        P.ts(gmx[:, 0:1], gmx[:, 0:1], -8.0, None, ALU.mult)
        P.mm(pb[5][:, 0:1], ones_f[0:1, :], gmx[:, 0:1])
        P.copy(negm, pb[5][:, 0:1])
        P.barrier()

        A.reset()
        Wb = A.alloc("Wb", [128, 8, INW], BF16)
        wst = [A.alloc("wst%d" % i, [128, 8, 512], F32) for i in range(2)]
        ncol = [(n * 512, min(512, INW - n * 512)) for n in range((INW + 511) // 512)]
        for n, (c0, cw) in enumerate(ncol):
            w = wst[n % 2]
            P.dma(w[:, :, 0:cw], w_in[l][:, c0:c0 + cw].re("(j p) n -> p j n", p=128))
            P.copy(Wb[:, :, c0:c0 + cw], w[:, :, 0:cw], eng=("dve", "pool")[n % 2])
        xs = [A.alloc("xs%d" % i, [128, D], F32) for i in range(2)]
        junk = A.alloc("junk", [128, D], BF16)
        ss = A.alloc("ss", [128, 4], F32)
        rs = A.alloc("rs", [128, 4], F32)
        xn = A.alloc("xn", [128, 4, D], BF16)
        hT = A.alloc("hT", [128, 8, 512], BF16)
        projT = A.alloc("projT", [128, 10, 512], F32)
        qkT = A.alloc("qkT", [128, 4, 512], BF16)
        vout = A.alloc("vout", [128, 4, 512], BF16)
        ua = A.alloc("ua", [128, 2, 514], F32)
        acc = A.alloc("acc", [128, 2, 512], F32)
        yaT = A.alloc("yaT", [128, 2, 512], BF16)
        ub = A.alloc("ub", [128, 2, 542], BF16)
        dg = A.alloc("dg", [128, 2, 31, 128], BF16)
        for cc in range(2):
            P.tt(dg[:, cc, :, :], ident_b.unsq(1).bcast([128, 31, 128]),
                 prmT[:, cc, 3:34].unsq(2).bcast([128, 31, 128]), ALU.mult)
        sg = A.alloc("sg", [128, 2, 512], F32)
        usq = A.alloc("usq", [128, 2, 512], F32)
        mean_sb = A.alloc("mean_sb", [128, 512], F32)
        var_sb = A.alloc("var_sb", [128, 512], F32)
        ydT = A.alloc("ydT", [128, 2, 512], BF16)
        cs4 = A.alloc("cs4", [128, 4, 64], F32)
        sq = A.alloc("sq", [128, 512], F32)
        ssh = A.alloc("ssh", [128, 8], F32)
        qn = A.alloc("qn", [128, 512], F32)
        ra = A.alloc("ra", [128, 8, 32], F32)
        rb = A.alloc("rb", [128, 8, 32], F32)
        qr = A.alloc("qr", [128, 4, 512], BF16)
        mbT = A.alloc("mbT", [128, 4, 512], BF16)
        P.memset(ua, 0.0)
        P.memset(ub, 0.0, eng="pool")
        FM = [(0, 0), (1, 128), (2, 256), (3, 384), (4, 512), (5, 640),
              (6, 2304), (7, 2432), (8, 2560), (9, 2688)]
        QK = [(0, 768), (1, 896), (2, 1024), (3, 1152)]
        bi = 0
        for i in range(NT):
            t0 = i * 512
            for st in range(4):
                xt = xs[st % 2]
                P.dma(xt, xin[t0 + st * 128:t0 + (st + 1) * 128, :])
                P.act(junk, xt, AF.Square, accum_out=ss[:, st:st + 1])
                rstd_from_ss(rs[:, st:st + 1], ss[:, st:st + 1], D)
                P.act(xn[:, st, :], xt, AF.Copy, scale=rs[:, st:st + 1])
            P.dma(cs4, csd[t0:t0 + 512, :].re("(s p) f -> p s f", p=128))
            for j in range(8):
                bank = pb[bi % 8]; bi += 1
                bkb = bank.bc(BF16)
                for st in range(4):
                    P.transpose(bkb[:, st * 128:(st + 1) * 128], xn[:, st, j * 128:(j + 1) * 128], ident_b)
                P.act(hT[:, j, :], bkb[:, 0:512], AF.Identity, scale=G1[:, j:j + 1], bias=sh1[:, j:j + 1])
            for idx, c0 in FM:
                bank = pb[bi % 8]; bi += 1
                for j in range(8):
                    P.mm(bank, Wb[:, j, c0:c0 + 128], hT[:, j, :], start=(j == 0), stop=(j == 7))
                P.copy(projT[:, idx, :], bank, eng=("act", "dve")[idx % 2])
            for idx, c0 in QK:
                bank = pb[bi % 8]; bi += 1
                for j in range(8):
                    P.mm(bank, Wb[:, j, c0:c0 + 128], hT[:, j, :], start=(j == 0), stop=(j == 7))
                if idx < 2:
                    P.act(qkT[:, idx, :], bank, AF.Copy, scale=0.125)
                else:
                    P.copy(qkT[:, idx, :], bank, eng="dve")
            P.dma(sbqT[:, t0:t0 + 512].re("(c p) t -> p c t", p=128), qkT[:, 0:2, :])
            P.dma(sbkT[:, t0:t0 + 512].re("(c p) t -> p c t", p=128), qkT[:, 2:4, :])
            for st in range(4):
                bank = pb[bi % 8]; bi += 1
                for j in range(8):
                    P.mm(bank[:, 0:256], hT[:, j, st * 128:(st + 1) * 128], Wb[:, j, 1280:1536],
                         start=(j == 0), stop=(j == 7))
                for j in range(8):
                    P.mm(bank[:, 256:512], hT[:, j, st * 128:(st + 1) * 128], Wb[:, j, 2048:2304],
                         start=(j == 0), stop=(j == 7))
                P.copy(vout[:, st, :], bank, eng="act")
                bank = pb[bi % 8]; bi += 1
                for j in range(8):
                    P.mm(bank, hT[:, j, st * 128:(st + 1) * 128], Wb[:, j, 1536:2048],
                         start=(j == 0), stop=(j == 7))
                P.act(sq, bank, AF.Square)
                P.reduce(ssh, sq.re("p (a b) -> p a b", a=8), ALU.add)
                rstd_from_ss(ssh, ssh, HD)
                P.tt(qn.re("p (a b) -> p a b", a=8), bank.re("p (a b) -> p a b", a=8),
                     ssh.unsq(2).bcast([128, 8, 64]), ALU.mult)
                P.tt(qn, qn, gqk_bc, ALU.mult, eng="pool")
                q4 = qn.re("p (a h b) -> p a h b", a=8, h=2)
                o4 = qr[:, st, :].re("p (a h b) -> p a h b", a=8, h=2)
                cosb = cs4[:, st, 0:32].unsq(1).bcast([128, 8, 32])
                sinb = cs4[:, st, 32:64].unsq(1).bcast([128, 8, 32])
                P.tt(ra, q4[:, :, 0, :], cosb, ALU.mult)
                P.tt(rb, q4[:, :, 1, :], sinb, ALU.mult, eng="pool")
                P.tt(o4[:, :, 0, :], ra, rb, ALU.subtract)
                P.tt(ra, q4[:, :, 1, :], cosb, ALU.mult)
                P.tt(rb, q4[:, :, 0, :], sinb, ALU.mult, eng="pool")
                P.tt(o4[:, :, 1, :], ra, rb, ALU.add)
            P.dma(sbv[t0:t0 + 512, :].re("(s p) f -> p s f", p=128), vout[:, :, 0:256])
            P.dma(mbv[t0:t0 + 512, :].re("(s p) f -> p s f", p=128), vout[:, :, 256:512])
            for blk in range(4):
                bank = pb[bi % 8]; bi += 1
                bkb = bank.bc(BF16)
                for st in range(4):
                    P.transpose(bkb[:, st * 128:(st + 1) * 128], qr[:, st, blk * 128:(blk + 1) * 128], ident_b)
                P.copy(mbT[:, blk, :], bkb[:, 0:512], eng=("act", "dve")[blk % 2])
            P.dma(mbqT[:, t0:t0 + 512].re("(c p) t -> p c t", p=128), mbT[:, 0:2, :])
            P.dma(mbkT[:, t0:t0 + 512].re("(c p) t -> p c t", p=128), mbT[:, 2:4, :])
            for cc in range(2):
                w3 = prmT[:, cc, 0:3]
                P.tt(ua[:, cc, 2:514], projT[:, 2 + cc, :], projT[:, 4 + cc, :], ALU.mult)
                P.ts(acc[:, cc, :], ua[:, cc, 0:512], w3[:, 0:1], None, ALU.mult)
                P.stt(acc[:, cc, :], ua[:, cc, 1:513], w3[:, 1:2], acc[:, cc, :], ALU.mult, ALU.add)
                P.stt(acc[:, cc, :], ua[:, cc, 2:514], w3[:, 2:3], acc[:, cc, :], ALU.mult, ALU.add)
                P.tt(yaT[:, cc, :], projT[:, cc, :], acc[:, cc, :], ALU.mult)
                P.copy(ua[:, cc, 0:2], ua[:, cc, 512:514])
            P.dma(mixT[0:256, t0:t0 + 512].re("(c p) t -> p c t", p=128), yaT)
            for cc in range(2):
                eng = ("dve", "pool")[cc]
                P.act(sg[:, cc, :], projT[:, 8 + cc, :], AF.Sigmoid)
                P.tt(ub[:, cc, 30:542], projT[:, 6 + cc, :], sg[:, cc, :], ALU.mult, eng=eng)
                bank = pb[bi % 8]; bi += 1
                for k in range(31):
                    P.mm(bank, dg[:, cc, k, :], ub[:, cc, k:k + 512], start=(k == 0), stop=(k == 30))
                P.act(acc[:, cc, :], bank, AF.Identity, bias=prmT[:, cc, 34:35])
                P.copy(ub[:, cc, 0:30], ub[:, cc, 512:542], eng="dve")
                P.act(usq[:, cc, :], acc[:, cc, :], AF.Square)
            bm = pb[bi % 8]; bi += 1
            bq = pb[bi % 8]; bi += 1
            for cc in range(2):
                P.mm(bm, avg_f, acc[:, cc, :], start=(cc == 0), stop=(cc == 1))
            for cc in range(2):
                P.mm(bq, avg_f, usq[:, cc, :], start=(cc == 0), stop=(cc == 1))
            P.copy(mean_sb, bm, eng="act")
            P.tt(var_sb, mean_sb, mean_sb, ALU.mult)
            P.tt(var_sb, bq, var_sb, ALU.subtract)
            P.ts(var_sb, var_sb, EPS, None, ALU.add)
            P.act(var_sb, var_sb, AF.Sqrt)
            P.recip(var_sb, var_sb)
            for cc in range(2):
                eng = ("dve", "pool")[cc]
                P.tt(acc[:, cc, :], acc[:, cc, :], mean_sb, ALU.subtract, eng=eng)
                P.tt(acc[:, cc, :], acc[:, cc, :], var_sb, ALU.mult, eng=eng)
                P.act(usq[:, cc, :], acc[:, cc, :], AF.Identity, scale=prmT[:, cc, 35:36], bias=prmT[:, cc, 36:37])
                P.act(sg[:, cc, :], usq[:, cc, :], AF.Sigmoid)
                P.tt(ydT[:, cc, :], usq[:, cc, :], sg[:, cc, :], ALU.mult, eng=eng)
            P.dma(mixT[768:1024, t0:t0 + 512].re("(c p) t -> p c t", p=128), ydT)
        P.barrier()

        A.reset()
        negT = A.alloc("negT", [128, 128], BF16)
        csel = A.alloc("csel", [128, 128], BF16)
        mk = A.alloc("mk", [128, 4, 512], BF16)
        stg = A.alloc("stg", [128, 4, 512], F32)
        P.dma(stg[:, 0, 0:128], k_negT)
        P.copy(negT, stg[:, 0, 0:128])
        P.dma(stg[:, 1, 0:128], k_csel)
        P.copy(csel, stg[:, 1, 0:128])
        stg2 = A.alloc("stg2", [128, 4, 512], F32)
        P.dma(stg2, k_sbmask)
        P.copy(mk, stg2)
        vall = A.alloc("vall", [128, NS, G], BF16)
        for c0 in range(0, NS, 16):
            c1 = min(NS, c0 + 16)
            P.dma(vall[:, c0:c1, :], sbv[c0 * 128:c1 * 128, :].re("(c p) f -> p c f", p=128))
        qTh = [A.alloc("qTh%d" % i, [128, S], BF16) for i in range(2)]
        kTh = [A.alloc("kTh%d" % i, [128, S], BF16) for i in range(2)]
        for i in range(2):
            P.memset(qTh[i][64:128, :], 0.0)
            P.memset(kTh[i][64:128, :], 0.0, eng="pool")
        R = 3
        negO = A.alloc("negO", [128, 128], BF16)
        P.memset(negO, -1.0)
        e_sb = [A.alloc("e_sb%d" % i, [128, 512], F32) for i in range(2)]
        L_b = [A.alloc("L_b%d" % i, [128, 512], BF16) for i in range(4)]
        A_b = [A.alloc("A_b%d" % i, [128, 512], BF16) for i in range(R)]
        ncb = [A.alloc("ncb%d" % i, [128, 512], BF16) for i in range(R)]
        ncf = A.alloc("ncf", [33, 512], F32)
        yo = [A.alloc("yo%d" % i, [128, 512], BF16) for i in range(2)]
        for r in range(R):
            P.memset(ncb[r], 0.0)
        X = pb[0:4]
        Cs = pb[4:6]
        Ob = pb[6:8]
        tcount = 0
        for h in range(NH):
            qT = qTh[h % 2]
            kT = kTh[h % 2]
            P.dma(qT[0:64, :], sbqT[h * 64:(h + 1) * 64, :])
            P.dma(kT[0:64, :], sbkT[h * 64:(h + 1) * 64, :])
            for qt in range(NT):
                O = Ob[tcount % 2]
                yv = yo[tcount % 2]
                tcount += 1
                steps = list(range(4 * qt + 3, -1, -1))
                n = len(steps)
                P.memset(ncf, 0.0)
                P.memset(ncb[0][0:1, :], 0.0)
                P.memset(ncb[0][32:33, :], 0.0)
                qtile = qT[:, qt * 512:(qt + 1) * 512]

                def stA(s):
                    kc = steps[s]
                    P.mm(X[s % 4], kT[:, kc * 128:(kc + 1) * 128], qtile, start=True, stop=False,
                         skip_group_check=True)

                def stB(s):
                    kc = steps[s]
                    P.act(e_sb[s % 2], X[s % 4], AF.Exp)
                    P.act(L_b[s % 4], e_sb[s % 2], AF.Ln, bias=1.0)
                    if kc >= 4 * qt:
                        P.tt(L_b[s % 4], L_b[s % 4], mk[:, kc - 4 * qt, :], ALU.mult, eng="pool")

                def stC(s):
                    pr = s // 2
                    P.mm(Cs[pr % 2], ones_b, L_b[s % 4], start=(s % 2 == 0), stop=(s % 2 == 1))
                    if s % 2 == 1 and s + 1 < n:
                        nb_ = ncb[(pr + 1) % R]
                        P.tt(ncf, ncf, Cs[pr % 2][0:33, :], ALU.subtract)
                        P.copy(nb_[0:33, :], ncf)
                        P.tt(nb_[32:33, :], ncf[32:33, :], nb_[32:33, :], ALU.subtract)

                def stD(s):
                    pr = s // 2
                    P.mm(X[s % 4], negT, L_b[s % 4], start=False, stop=False, skip_group_check=True)
                    if s % 2 == 1:
                        P.mm(X[s % 4], negO, L_b[(s - 1) % 4], start=False, stop=False, skip_group_check=True)
                    P.mm(X[s % 4], csel, ncb[pr % R], start=False, stop=True, skip_group_check=True)

                def stE(s):
                    kc = steps[s]
                    P.act(A_b[s % R], X[s % 4], AF.Exp)
                    if kc >= 4 * qt:
                        P.tt(A_b[s % R], A_b[s % R], mk[:, kc - 4 * qt, :], ALU.mult, eng="pool")

                def stF(s):
                    kc = steps[s]
                    P.mm(O, vall[:, kc, (h // 2) * 128:(h // 2 + 1) * 128], A_b[s % R],
                         start=(s == 0), stop=(s == n - 1))

                stA(0)
                for it in range(n + 2):
                    if it + 1 < n:
                        stA(it + 1)
                    if it < n:
                        stB(it)
                        stC(it)
                    if 0 <= it - 1 < n:
                        stD(it - 1)
                    if 0 <= it - 2 < n:
                        stE(it - 2)
                        stF(it - 2)
                r0 = (h % 2) * 64
                P.copy(yv[r0:r0 + 64, :], O[r0:r0 + 64, :], eng="dve")
                P.dma(mixT[256 + h * 64:256 + (h + 1) * 64, qt * 512:(qt + 1) * 512], yv[r0:r0 + 64, :])
        P.barrier()

        A.reset()
        cb = A.alloc("cb", [128, 4, 512], BF16)
        stg2 = A.alloc("stg2", [128, 4, 512], F32)
        P.dma(stg2, k_mbcb)
        P.copy(cb, stg2)
        blk = A.alloc("blk", [128, 32], F32)
        P.dma(blk, k_blk)
        pbias = A.alloc("pbias", [128, 32, 32], F32)
        ownm = A.alloc("ownm", [128, 32, 32], F32)
        for o in range(NB):
            P.ts(pbias[:, o, :], blk, float(o), -1e9, ALU.is_ge, ALU.mult)
            P.ts(ownm[:, o, :], blk, float(o), None, ALU.is_equal, eng="pool")
        oh = A.alloc("oh", [128, 32, 128], BF16)
        P.memset(oh, 0.0)
        P.copy(oh[0:32], ident_b[0:32, 0:32].unsq(2).bcast([32, 32, 128]))
        vall = A.alloc("vall", [128, NS, G], BF16)
        for c0 in range(0, NS, 16):
            c1 = min(NS, c0 + 16)
            P.dma(vall[:, c0:c1, :], mbv[c0 * 128:c1 * 128, :].re("(c p) f -> p c f", p=128))
        vaug = [A.alloc("vaug%d" % i, [128, NS, 128], BF16) for i in range(2)]
        qTh = [A.alloc("qTh%d" % i, [128, S], BF16) for i in range(2)]
        kTh = [A.alloc("kTh%d" % i, [128, S], BF16) for i in range(2)]
        for i in range(2):
            P.memset(qTh[i][64:128, :], 0.0)
            P.memset(kTh[i][64:128, :], 0.0, eng="pool")
        kmf = A.alloc("kmf", [64, 32], F32)
        kmh = A.alloc("kmh", [64, 32], BF16)
        kml = A.alloc("kml", [64, 32], BF16)
        kmr = A.alloc("kmr", [64, 32], F32)
        g2 = A.alloc("g2", [128, 32], F32)
        top8 = A.alloc("top8", [128, 8], F32)
        thr = A.alloc("thr", [128, 1], F32)
        sel = A.alloc("sel", [128, 32], F32)
        selb = A.alloc("selb", [128, 4, 32], BF16)
        selbT = [A.alloc("selbT%d" % i, [128, S], BF16) for i in range(2)]
        for i in range(2):
            P.memset(selbT[i], 0.0, eng=("dve", "pool")[i])
        P_b = [A.alloc("P_b%d" % i, [128, 512], BF16) for i in range(4)]
        O_sb = A.alloc("O_sb", [65, 512], F32)
        rl = A.alloc("rl", [65, 512], F32)
        yo = [A.alloc("yo%d" % i, [64, 512], BF16) for i in range(2)]
        for i in range(2):
            P.memset(vaug[i], 0.0, eng=("dve", "pool")[i])
            P.memset(vaug[i][:, :, 64:65], 1.0, eng=("dve", "pool")[i])
        P.memset(kmf, 0.0)
        X = pb[0:4]
        Ob = pb[4:6]
        Gp = pb[6]
        Tp = pb[7]
        Bc = pb[6]
        tcount = 0
        for h in range(NH):
            qT = qTh[h % 2]
            kT = kTh[h % 2]
            va = vaug[h % 2]
            sT = selbT[h % 2]
            P.dma(qT[0:64, :], mbqT[h * 64:(h + 1) * 64, :])
            P.dma(kT[0:64, :], mbkT[h * 64:(h + 1) * 64, :])
            P.copy(va[:, :, 0:64], vall[:, :, h * 64:(h + 1) * 64], eng="pool")
            P.reduce(kmf[:, 0:NB], kT[0:64, :].re("p (n b) -> p n b", b=256), ALU.add)
            P.ts(kmf, kmf, 1.0 / 256, None, ALU.mult)
            P.copy(kmh, kmf)
            P.tt(kmr, kmf, kmh, ALU.subtract)
            P.copy(kml, kmr)
            for qt in range(NT):
                for st in range(4):
                    sub = qt * 4 + st
                    own = sub // 2
                    P.mm(Gp[:, 0:32], qT[0:64, sub * 128:(sub + 1) * 128], kmh, start=True, stop=False)
                    P.mm(Gp[:, 0:32], qT[0:64, sub * 128:(sub + 1) * 128], kml, start=False, stop=True)
                    P.tt(g2, Gp[:, 0:32], pbias[:, own, :], ALU.add)
                    P.generic("dve", lambda e, o_=top8, i_=g2: e.max(o_.ap, i_.ap), [g2], [top8])
                    P.ts(thr, top8[:, 2:3], -1e8, None, ALU.max)
                    P.ts(sel, g2, thr, None, ALU.is_ge)
                    P.tt(sel, sel, ownm[:, own, :], ALU.max)
                    P.ts(selb[:, st, :], sel, -1.0, -NEG, ALU.add, ALU.mult)
                Tpb = Tp.bc(BF16)
                for st in range(4):
                    P.transpose(Tpb[0:32, st * 128:(st + 1) * 128], selb[:, st, :], ident_b)
                P.copy(sT[0:32, qt * 512:(qt + 1) * 512], Tpb[0:32, 0:512], eng="act")
            for qt in range(NT):
                O = Ob[tcount % 2]
                yv = yo[tcount % 2]
                tcount += 1
                n = 4 * qt + 4
                qtile = qT[:, qt * 512:(qt + 1) * 512]

                def mA(kc):
                    jb = kc // 2
                    diag = kc >= 4 * qt
                    P.mm(X[kc % 4], kT[:, kc * 128:(kc + 1) * 128], qtile, start=True, stop=False)
                    P.mm(X[kc % 4], oh[:, jb, :], sT[:, qt * 512:(qt + 1) * 512], start=False, stop=not diag)
                    if diag:
                        P.mm(X[kc % 4], ident_b, cb[:, kc - 4 * qt, :], start=False, stop=True)

                def mB(kc):
                    P.act(P_b[kc % 4], X[kc % 4], AF.Exp, scale=0.125, bias=negm)

                def mC(kc):
                    P.mm(O, va[:, kc, :], P_b[kc % 4], start=(kc == 0), stop=(kc == n - 1))

                mA(0)
                if n > 1:
                    mA(1)
                for kc in range(n + 1):
                    if kc + 2 < n:
                        mA(kc + 2)
                    if kc < n:
                        mB(kc)
                    if kc >= 1:
                        mC(kc - 1)
                P.copy(O_sb, O[0:65, :], eng="act")
                P.recip(rl[64:65, :], O_sb[64:65, :])
                P.mm(Bc[0:64, :], ones_f[64:65, 0:64], rl[64:65, :])
                P.tt(yv, O_sb[0:64, :], Bc[0:64, :], ALU.mult)
                P.dma(mixT[512 + h * 64:512 + (h + 1) * 64, qt * 512:(qt + 1) * 512], yv)
        P.barrier()

        A.reset()
        Wo = A.alloc("Wo", [128, 8, D], BF16)
        gbc = A.alloc("gbc", [128, D], F32)
        wst = [A.alloc("wst%d" % i, [128, 4, D], F32) for i in range(2)]
        P.dma(gbc, modrow[l:l + 1, 2 * D:3 * D].bcast([128, D]))
        for n in range(2):
            w = wst[n % 2]
            P.dma(w, w_out[l][n * 512:(n + 1) * 512, :].re("(j p) n -> p j n", p=128))
            P.tt(Wo[:, n * 4:(n + 1) * 4, :], w, gbc.unsq(1).bcast([128, 4, D]), ALU.mult,
                 eng=("dve", "pool")[n % 2])
        mx = [A.alloc("mx%d" % i, [128, 8, 512], BF16) for i in range(2)]
        xs = [A.alloc("xs%d" % i, [128, D], F32) for i in range(2)]
        x1 = [A.alloc("x1_%d" % i, [128, D], F32) for i in range(2)]
        junk = A.alloc("junk", [128, D], BF16)
        ss = A.alloc("ss", [128, 4], F32)
        rs = A.alloc("rs", [128, 4], F32)
        xn = A.alloc("xn", [128, 4, D], BF16)
        hT = [A.alloc("hT%d" % i, [128, 8, 512], BF16) for i in range(2)]
        bi = 0
        for i in range(NT):
            t0 = i * 512
            m = mx[i % 2]
            P.dma(m, mixT[:, t0:t0 + 512].re("(c p) t -> p c t", p=128))
            for st in range(4):
                xt = xs[st % 2]
                x1t = x1[st % 2]
                P.dma(xt, xin[t0 + st * 128:t0 + (st + 1) * 128, :])
                for n in range(2):
                    bank = pb[bi % 8]; bi += 1
                    for j in range(8):
                        P.mm(bank, m[:, j, st * 128:(st + 1) * 128], Wo[:, j, n * 512:(n + 1) * 512],
                             start=(j == 0), stop=(j == 7))
                    P.tt(x1t[:, n * 512:(n + 1) * 512], bank, xt[:, n * 512:(n + 1) * 512], ALU.add)
                P.dma(x1s[t0 + st * 128:t0 + (st + 1) * 128, :], x1t)
                P.act(junk, x1t, AF.Square, accum_out=ss[:, st:st + 1])
                rstd_from_ss(rs[:, st:st + 1], ss[:, st:st + 1], D)
                P.act(xn[:, st, :], x1t, AF.Copy, scale=rs[:, st:st + 1])
            ht = hT[i % 2]
            for j in range(8):
                bank = pb[bi % 8]; bi += 1
                bkb = bank.bc(BF16)
                for st in range(4):
                    P.transpose(bkb[:, st * 128:(st + 1) * 128], xn[:, st, j * 128:(j + 1) * 128], ident_b)
                P.act(ht[:, j, :], bkb[:, 0:512], AF.Identity, scale=G2[:, j:j + 1], bias=sh2[:, j:j + 1])
            P.dma(h2Ts[:, t0:t0 + 512].re("(c p) t -> p c t", p=128), ht)
        P.barrier()

        A.reset()
        W1 = A.alloc("W1", [128, 8, DFF], BF16)
        W2 = A.alloc("W2", [128, 32, D], BF16)
        mark = A.off
        gbc = A.alloc("gbc", [128, D], F32)
        wst = [A.alloc("wst%d" % i, [128, 8, 512], F32) for i in range(2)]
        P.dma(gbc, modrow[l:l + 1, 5 * D:6 * D].bcast([128, D]))
        for n in range(8):
            w = wst[n % 2]
            P.dma(w, w_mlp1[l][:, n * 512:(n + 1) * 512].re("(j p) n -> p j n", p=128))
            P.copy(W1[:, :, n * 512:(n + 1) * 512], w, eng=("dve", "pool")[n % 2])
        for n in range(8):
            w = wst[n % 2].re("p j n -> p (j n)").re("p (j n) -> p j n", j=4)
            P.dma(w, w_mlp2[l][n * 512:(n + 1) * 512, :].re("(j p) n -> p j n", p=128))
            P.tt(W2[:, n * 4:(n + 1) * 4, :], w, gbc.unsq(1).bcast([128, 4, D]), ALU.mult,
                 eng=("dve", "pool")[n % 2])
        P.barrier()
        A.off = mark
        A.gen += 1
        h2 = [A.alloc("h2_%d" % i, [128, 8, 256], BF16) for i in range(2)]
        f1 = A.alloc("f1", [128, 32, 256], BF16)
        rbuf = [A.alloc("rbuf%d" % i, [128, 256], BF16) for i in range(3)]
        x1 = [A.alloc("x1_%d" % i, [128, D], F32) for i in range(2)]
        x2 = [A.alloc("x2_%d" % i, [128, D], F32) for i in range(2)]
        bi = 0
        for i in range(S // 256):
            t0 = i * 256
            hh = h2[i % 2]
            P.dma(hh, h2Ts[:, t0:t0 + 256].re("(c p) t -> p c t", p=128))
            for fc in range(32):
                bank = pb[bi % 8]; bi += 1
                for j in range(8):
                    P.mm(bank[:, 0:256], W1[:, j, fc * 128:(fc + 1) * 128], hh[:, j, :],
                         start=(j == 0), stop=(j == 7))
                rbf = rbuf[fc % 3]
                P.act(rbf, bank[:, 0:256], AF.Relu)
                P.tt(f1[:, fc, :], rbf, rbf, ALU.mult, eng=("dve", "pool")[fc % 2])
            for st in range(2):
                x1t = x1[st]
                x2t = x2[st]
                P.dma(x1t, x1s[t0 + st * 128:t0 + (st + 1) * 128, :])
                for n in range(2):
                    bank = pb[bi % 8]; bi += 1
                    for fc in range(32):
                        P.mm(bank, f1[:, fc, st * 128:(st + 1) * 128], W2[:, fc, n * 512:(n + 1) * 512],
                             start=(fc == 0), stop=(fc == 31))
                    P.tt(x2t[:, n * 512:(n + 1) * 512], bank, x1t[:, n * 512:(n + 1) * 512], ALU.add)
                P.dma(xout[t0 + st * 128:t0 + (st + 1) * 128, :], x2t)
        P.barrier()

    P.emit()
    return nc, P


_CACHE = {}


def _get_nc(S, depth):
    key = (S, depth)
    if key not in _CACHE:
        _CACHE[key] = build(S, depth)[0]
    return _CACHE[key]


def make_in_maps(inputs, S, depth, nb):
    consts = host_consts()
    maps = []
    shared = {}
    for name in ("w_ada", "b_ada", "g_norm1", "w_in", "w_sconv", "w_cconv", "b_cconv", "g_cln", "b_cln",
                 "g_q", "g_k", "w_out", "g_norm2", "w_mlp1", "w_mlp2"):
        shared[name] = np.ascontiguousarray(inputs[name], dtype=np.float32)
    for k, v in consts.items():
        shared["k_" + k] = v
    x = np.asarray(inputs["x"], dtype=np.float32)
    c = np.asarray(inputs["c"], dtype=np.float32)
    pos = np.asarray(inputs["positions"], dtype=np.int32)
    for b in range(nb):
        m = dict(shared)
        m["x"] = np.ascontiguousarray(x[b])
        m["cT"] = np.ascontiguousarray(c[b].reshape(8, 128).T)
        m["pos"] = np.ascontiguousarray(pos[b].reshape(128, S // 128))
        maps.append(m)
    return maps


def kernel(**inputs):
    x = inputs["x"]
    nb, S, _ = x.shape
    depth = inputs["w_ada"].shape[0]
    nc = _get_nc(S, depth)
    maps = make_in_maps(inputs, S, depth, nb)
    res = run_bass_kernel_spmd(nc, maps, core_ids=list(range(nb)))
    return np.stack([np.asarray(r["out"], dtype=np.float32) for r in res.results], axis=0)
```

```python
import numpy as np
from contextlib import ExitStack
import concourse.bass as bass
import concourse.mybir as mybir
from concourse.bass_utils import run_bass_kernel_spmd

F32 = mybir.dt.float32
BF16 = mybir.dt.bfloat16
I32 = mybir.dt.int32
AF = mybir.ActivationFunctionType
ALU = mybir.AluOpType
AX = mybir.AxisListType

NDSEM = 12
D = 1024
G = 256
NH = 4
HD = 64
DFF = 4096
INW = 11 * G
EPS = 1e-6
NEG = -30000.0


class V:
    __slots__ = ("key", "ap")

    def __init__(self, key, ap):
        self.key = key
        self.ap = ap

    def __getitem__(self, idx):
        return V(self.key, self.ap[idx])

    def k(self, sub):
        return V((self.key, sub), self.ap)

    def re(self, pat, **kw):
        return V(self.key, self.ap.rearrange(pat, **kw))

    def bc(self, dt):
        return V(self.key, self.ap.bitcast(dt))

    def bcast(self, shape):
        return V(self.key, self.ap.broadcast_to(list(shape)))

    def unsq(self, ax):
        return V(self.key, self.ap.unsqueeze(ax))


class Op:
    __slots__ = ("eng", "emit", "deps", "signal", "is_dma", "ev_sem", "ev_val")

    def __init__(self, eng, emit, is_dma=False):
        self.eng = eng
        self.emit = emit
        self.deps = []
        self.signal = False
        self.is_dma = is_dma
        self.ev_sem = None
        self.ev_val = 0


class Prog:
    ENGS = ("sp", "act", "dve", "pool", "pe")

    def __init__(self, nc):
        self.nc = nc
        self.es = ExitStack()
        self.ops = {e: [] for e in self.ENGS}
        self.lastreal = {e: None for e in self.ENGS}
        self.lastw = {}
        self.readers = {}
        self.dma_count = {e: 0 for e in self.ENGS}
        self.dma_last = {}
        self.n = 0

    def sb(self, name, shape, dt):
        t = self.es.enter_context(self.nc.sbuf_tensor(name, list(shape), dt))
        return V(name, t[:])

    def ps(self, name, shape, dt):
        t = self.es.enter_context(self.nc.psum_tensor(name, list(shape), dt))
        return V(name, t[:])

    def dram(self, name, shape, dt, kind="Internal"):
        t = self.nc.dram_tensor(name, list(shape), dt, kind=kind)
        return V(name, t.ap())

    def _add(self, eng, emit, reads, writes, is_dma=False):
        op = Op(eng, emit, is_dma)
        rk = [v.key for v in reads if v is not None]
        wk = [v.key for v in writes if v is not None]
        deps = []
        for k in rk:
            p = self.lastw.get(k)
            if p is not None:
                deps.append((p, True))
        for k in wk:
            p = self.lastw.get(k)
            if p is not None:
                deps.append((p, False))
            for r in self.readers.get(k, ()):
                deps.append((r, False))
        for p, raw in deps:
            if p is op:
                continue
            if p.is_dma:
                need = True
            elif p.eng != eng:
                need = True
            elif eng == "pe":
                need = False
            else:
                need = True
            if need and p not in op.deps:
                p.signal = True
                op.deps.append(p)
        if is_dma:
            op.signal = True
            c = self.dma_count[eng]
            self.dma_count[eng] = c + 1
            slot = (eng, c % NDSEM)
            prev = self.dma_last.get(slot)
            if prev is not None:
                op.deps.append(prev)
            self.dma_last[slot] = op
            op.ev_sem = slot
            op.ev_val = 16 * (c // NDSEM + 1)
        for k in wk:
            self.lastw[k] = op
            self.readers[k] = []
        for k in rk:
            if k not in wk:
                self.readers.setdefault(k, []).append(op)
        self.ops[eng].append(op)
        self.lastreal[eng] = op
        self.n += 1
        return op

    def barrier(self):
        lasts = []
        for e in self.ENGS:
            p = self.lastreal[e]
            if p is not None and not p.is_dma:
                p.signal = True
                lasts.append(p)
        lasts.extend(self.dma_last.values())
        for e in self.ENGS:
            op = Op(e, None)
            op.deps = [p for p in lasts if (p.is_dma or p.eng != e)]
            self.ops[e].append(op)
        self.lastw.clear()
        self.readers.clear()

    def dma(self, out, in_, eng="sp", **kw):
        return self._add(eng, lambda e: e.dma_start(out=out.ap, in_=in_.ap, **kw), [in_], [out], is_dma=True)

    def mm(self, out, lhsT, rhs, start=True, stop=True, **kw):
        return self._add("pe", lambda e: e.matmul(out.ap, lhsT.ap, rhs.ap, start=start, stop=stop, **kw),
                         [lhsT, rhs], [out])

    def transpose(self, out, in_, ident):
        return self._add("pe", lambda e: e.transpose(out.ap, in_.ap, ident.ap), [in_, ident], [out])

    def act(self, out, in_, func, bias=None, scale=None, accum_out=None, eng="act"):
        def emit(e):
            kw = {}
            if bias is not None:
                kw["bias"] = bias.ap if isinstance(bias, V) else bias
            if scale is not None:
                kw["scale"] = scale.ap if isinstance(scale, V) else scale
            if accum_out is not None:
                kw["accum_out"] = accum_out.ap
            return e.activation(out=out.ap, in_=in_.ap, func=func, **kw)
        rd = [in_] + [b for b in (bias, scale) if isinstance(b, V)]
        wr = [out] + ([accum_out] if accum_out is not None else [])
        return self._add(eng, emit, rd, wr)

    def ts(self, out, in0, s1, s2, op0, op1=None, eng="dve"):
        def emit(e):
            a1 = s1.ap if isinstance(s1, V) else s1
            a2 = s2.ap if isinstance(s2, V) else s2
            kw = {}
            if op1 is not None:
                kw["op1"] = op1
            return e.tensor_scalar(out.ap, in0.ap, a1, a2, op0, **kw)
        rd = [in0] + [s for s in (s1, s2) if isinstance(s, V)]
        return self._add(eng, emit, rd, [out])

    def tt(self, out, in0, in1, op, eng="dve"):
        return self._add(eng, lambda e: e.tensor_tensor(out.ap, in0.ap, in1.ap, op), [in0, in1], [out])

    def stt(self, out, in0, scalar, in1, op0, op1, eng="dve"):
        def emit(e):
            s = scalar.ap if isinstance(scalar, V) else scalar
            return e.scalar_tensor_tensor(out.ap, in0.ap, s, in1.ap, op0, op1)
        rd = [in0, in1] + ([scalar] if isinstance(scalar, V) else [])
        return self._add(eng, emit, rd, [out])

    def copy(self, out, in_, eng="dve"):
        if eng == "act":
            return self.act(out, in_, AF.Copy)
        return self._add(eng, lambda e: e.tensor_copy(out.ap, in_.ap), [in_], [out])

    def memset(self, out, val, eng="dve"):
        return self._add(eng, lambda e: e.memset(out.ap, val), [], [out])

    def reduce(self, out, in_, op, eng="dve"):
        return self._add(eng, lambda e: e.tensor_reduce(out.ap, in_.ap, AX.X, op), [in_], [out])

    def recip(self, out, in_):
        return self._add("dve", lambda e: e.reciprocal(out.ap, in_.ap), [in_], [out])

    def generic(self, eng, fn, reads, writes):
        return self._add(eng, fn, reads, writes)

    def emit(self):
        nc = self.nc
        es = self.es
        esem = {e: es.enter_context(nc.semaphore("s_" + e)) for e in self.ENGS}
        dsem = {}
        for e in self.ENGS:
            for i in range(min(NDSEM, self.dma_count[e])):
                dsem[(e, i)] = es.enter_context(nc.semaphore("d_%s_%d" % (e, i)))
        for e in self.ENGS:
            c = 0
            for op in self.ops[e]:
                if op.is_dma or op.emit is None:
                    continue
                if op.signal:
                    c += 1
                    op.ev_sem = e
                    op.ev_val = c
        prog = self

        def run(ename, eng):
            waited = {}
            for op in prog.ops[ename]:
                for p in op.deps:
                    key = p.ev_sem
                    sem = dsem[key] if p.is_dma else esem[key]
                    if waited.get(key, 0) < p.ev_val:
                        eng.wait_ge(sem, p.ev_val)
                        waited[key] = p.ev_val
                if op.emit is None:
                    continue
                ins = op.emit(eng)
                if op.signal:
                    if op.is_dma:
                        ins.then_inc(dsem[op.ev_sem], 16)
                    else:
                        ins.then_inc(esem[ename], 1)
            for (e2, i), last in prog.dma_last.items():
                if e2 == ename and waited.get((e2, i), 0) < last.ev_val:
                    eng.wait_ge(dsem[(e2, i)], last.ev_val)

        with nc.Block() as block:
            @block.sync
            def _(e):
                run("sp", e)

            @block.scalar
            def _(e):
                run("act", e)

            @block.vector
            def _(e):
                run("dve", e)

            @block.gpsimd
            def _(e):
                run("pool", e)

            @block.tensor
            def _(e):
                run("pe", e)
        es.close()


class Arena:
    def __init__(self, P, name, nbytes):
        self.v = P.sb(name, [128, nbytes // 4], F32)
        self.cap = nbytes
        self.off = 0
        self.gen = 0

    def reset(self):
        self.off = 0
        self.gen += 1

    def alloc(self, name, shape, dt):
        esz = 4 if dt in (F32, I32) else 2
        nel = int(np.prod(shape[1:]))
        n4 = (nel * esz + 3) // 4
        o4 = self.off // 4
        assert self.off + n4 * 4 <= self.cap, ("arena overflow", name, self.off, n4 * 4, self.cap)
        ap = self.v.ap[0:shape[0], o4:o4 + n4]
        if dt != F32:
            ap = ap.bitcast(dt)
            if esz == 2 and nel != n4 * 2:
                ap = ap[:, 0:nel]
        if len(shape) == 3:
            ap = ap.rearrange("p (a b) -> p a b", a=shape[1])
        elif len(shape) == 4:
            ap = ap.rearrange("p (a b c) -> p a b c", a=shape[1], b=shape[2])
        self.off += ((n4 * 4 + 63) // 64) * 64
        return V((name, self.gen), ap)


def host_consts():
    c = {}
    c["ident"] = np.eye(128, dtype=np.float32)
    s1 = np.arange(128)[:, None]
    s0 = np.arange(128)[None, :]
    c["negT"] = np.where(s1 >= s0, -1.0, 0.0).astype(np.float32)
    t = np.arange(512)[None, :]
    sb = np.zeros((128, 4, 512), np.float32)
    cb = np.zeros((128, 4, 512), np.float32)
    for cc in range(4):
        sb[:, cc, :] = (t > 128 * cc + s1).astype(np.float32)
        cb[:, cc, :] = np.where(t >= 128 * cc + s1, 0.0, NEG)
    c["sbmask"] = sb
    c["mbcb"] = cb
    c["blk"] = np.tile(np.arange(32, dtype=np.float32)[None, :], (128, 1))
    half = HD // 2
    inv = np.exp(-np.log(10000.0) * np.arange(half, dtype=np.float32) / half).astype(np.float32)
    c["invf"] = np.tile(inv[None, :], (128, 1)).astype(np.float32)
    cs = np.zeros((128, 128), np.float32)
    cs[0, :] = 1.0
    cs[32, :] = 1.0
    c["csel"] = cs
    return c


def build(S, depth):
    assert S % 512 == 0
    nc = bass.Bass("TRN2", target_bir_lowering=False)
    P = Prog(nc)
    NT = S // 512
    NS = S // 128
    NB = S // 256
    ext = lambda n, s, d: P.dram(n, s, d, kind="ExternalInput")
    x_in = ext("x", [S, D], F32)
    cT_in = ext("cT", [128, 8], F32)
    pos_in = ext("pos", [128, S // 128], I32)
    w_ada = ext("w_ada", [depth, D, 6 * D], F32)
    b_ada = ext("b_ada", [depth, 6 * D], F32)
    g_norm1 = ext("g_norm1", [depth, D], F32)
    w_in = ext("w_in", [depth, D, INW], F32)
    w_sconv = ext("w_sconv", [depth, 3, G], F32)
    w_cconv = ext("w_cconv", [depth, 31, G], F32)
    b_cconv = ext("b_cconv", [depth, G], F32)
    g_cln = ext("g_cln", [depth, G], F32)
    b_cln = ext("b_cln", [depth, G], F32)
    g_q = ext("g_q", [depth, HD], F32)
    g_k = ext("g_k", [depth, HD], F32)
    w_out = ext("w_out", [depth, D, D], F32)
    g_norm2 = ext("g_norm2", [depth, D], F32)
    w_mlp1 = ext("w_mlp1", [depth, D, DFF], F32)
    w_mlp2 = ext("w_mlp2", [depth, DFF, D], F32)
    k_ident = ext("k_ident", [128, 128], F32)
    k_negT = ext("k_negT", [128, 128], F32)
    k_sbmask = ext("k_sbmask", [128, 4, 512], F32)
    k_mbcb = ext("k_mbcb", [128, 4, 512], F32)
    k_blk = ext("k_blk", [128, 32], F32)
    k_invf = ext("k_invf", [128, 32], F32)
    k_csel = ext("k_csel", [128, 128], F32)
    out = P.dram("out", [S, D], F32, kind="ExternalOutput")

    xmid = P.dram("xmid", [S, D], F32)
    x1s = P.dram("x1s", [S, D], F32)
    h2Ts = P.dram("h2Ts", [D, S], BF16)
    mixT = P.dram("mixT", [D, S], BF16)
    sbqT = P.dram("sbqT", [G, S], BF16)
    sbkT = P.dram("sbkT", [G, S], BF16)
    mbqT = P.dram("mbqT", [G, S], BF16)
    mbkT = P.dram("mbkT", [G, S], BF16)
    sbv = P.dram("sbv", [S, G], BF16)
    mbv = P.dram("mbv", [S, G], BF16)
    csd = P.dram("csd", [S, 64], F32)
    modrow = P.dram("modrow", [depth, 6 * D], F32)

    ident_f = P.sb("ident_f", [128, 128], F32)
    ident_b = P.sb("ident_b", [128, 128], BF16)
    ones_f = P.sb("ones_f", [128, 128], F32)
    ones_b = P.sb("ones_b", [128, 128], BF16)
    avg_f = P.sb("avg_f", [128, 128], F32)
    modc = P.sb("modc", [128, 64], F32)
    G1 = P.sb("G1", [128, 8], F32)
    G2 = P.sb("G2", [128, 8], F32)
    prmT = P.sb("prmT", [128, 2, 37], F32)
    gqk_bc = P.sb("gqk_bc", [128, 512], F32)
    negm = P.sb("negm", [128, 1], F32)
    epsc = P.sb("epsc", [128, 1], F32)
    A = Arena(P, "arena", 203 * 1024)
    pb = [P.ps("pb%d" % i, [128, 512], F32) for i in range(8)]

    P.dma(ident_f, k_ident)
    P.copy(ident_b, ident_f)
    P.memset(ones_f, 1.0)
    P.memset(ones_b, 1.0)
    P.memset(avg_f, 1.0 / 256)
    P.memset(epsc, EPS)

    def rstd_from_ss(rs, ss, n):
        P.ts(rs, ss, 1.0 / n, EPS, ALU.mult, ALU.add)
        P.act(rs, rs, AF.Sqrt)
        P.recip(rs, rs)

    A.reset()
    NJ = S // 128
    posi = A.alloc("posi", [128, NJ], I32)
    posf = A.alloc("posf", [128, NJ], F32)
    invf = A.alloc("invf", [128, 32], F32)
    ang = A.alloc("ang", [128, NJ, 32], F32)
    tmpa = A.alloc("tmpa", [128, NJ, 32], F32)
    cst = A.alloc("cst", [128, NJ, 64], F32)
    mpi = A.alloc("mpi", [128, 1], F32)
    P.memset(mpi, -float(np.pi))
    P.dma(posi, pos_in)
    P.dma(invf, k_invf)
    P.copy(posf, posi)
    P.tt(ang, posf.unsq(2).bcast([128, NJ, 32]), invf.unsq(1).bcast([128, NJ, 32]), ALU.mult)
    TWO_PI = float(2 * np.pi)
    C1 = 6.28125
    C2 = TWO_PI - C1
    PI_ = float(np.pi)
    ni = A.alloc("ni", [128, NJ, 32], I32)
    nf = A.alloc("nf", [128, NJ, 32], F32)
    rr = A.alloc("rr", [128, NJ, 32], F32)
    mm_ = A.alloc("mm_", [128, NJ, 32], F32)
    P.ts(tmpa, ang, 1.0 / TWO_PI, None, ALU.mult)
    P.copy(ni, tmpa)
    P.copy(nf, ni)
    P.stt(rr, nf, -C1, ang, ALU.mult, ALU.add)
    P.stt(rr, nf, -C2, rr, ALU.mult, ALU.add)

    def fold(t):
        P.ts(mm_, t, PI_, None, ALU.is_gt)
        P.stt(t, mm_, -TWO_PI, t, ALU.mult, ALU.add)
        P.ts(mm_, t, -PI_, None, ALU.is_lt)
        P.stt(t, mm_, TWO_PI, t, ALU.mult, ALU.add)
        P.ts(t, t, 3.14159, -3.14159, ALU.min, ALU.max)

    fold(rr)
    P.act(cst[:, :, 32:64], rr, AF.Sin)
    P.ts(tmpa, rr, PI_ / 2, None, ALU.add)
    fold(tmpa)
    P.act(cst[:, :, 0:32], tmpa, AF.Sin)
    P.dma(csd.re("(p j) f -> p j f", p=128), cst)
    P.barrier()

    for l in range(depth):
        xin = x_in if l == 0 else xmid
        xout = out if l == depth - 1 else xmid
        A.reset()
        sc = A.alloc("sc", [128, 8], F32)
        cTs = A.alloc("cTs", [128, 8], F32)
        rowbuf = A.alloc("rowbuf", [1, 8 * D], F32)
        brow = A.alloc("brow", [1, 6 * D], F32)
        wst = [A.alloc("wst%d" % i, [128, 8, 512], F32) for i in range(2)]
        prm = A.alloc("prm", [37, G], F32)
        grow = A.alloc("grow", [1, 512], F32)
        gsq = A.alloc("gsq", [1, 128], F32)
        gmx = A.alloc("gmx", [1, 2], F32)
        P.dma(cTs, cT_in)
        P.act(sc, cTs, AF.Sigmoid)
        P.tt(sc, sc, cTs, ALU.mult)
        P.dma(brow, b_ada[l:l + 1, :])
        P.dma(rowbuf[:, 6 * D:7 * D], g_norm1[l:l + 1, :])
        P.dma(rowbuf[:, 7 * D:8 * D], g_norm2[l:l + 1, :])
        P.dma(prm[0:3, :], w_sconv[l])
        P.dma(prm[3:34, :], w_cconv[l])
        P.dma(prm[34:35, :], b_cconv[l:l + 1, :])
        P.dma(prm[35:36, :], g_cln[l:l + 1, :])
        P.dma(prm[36:37, :], b_cln[l:l + 1, :])
        for h in range(NH):
            P.dma(grow[:, h * 64:(h + 1) * 64], g_q[l:l + 1, :])
            P.dma(grow[:, 256 + h * 64:256 + (h + 1) * 64], g_k[l:l + 1, :])
        for n in range(12):
            w = wst[n % 2]
            P.dma(w, w_ada[l][:, n * 512:(n + 1) * 512].re("(j p) n -> p j n", p=128))
            bank = pb[n % 2]
            for j in range(8):
                P.mm(bank[0:1, :], sc[:, j:j + 1], w[:, j, :], start=(j == 0), stop=False)
            P.mm(bank[0:1, :], ones_f[0:1, 0:1], brow[:, n * 512:(n + 1) * 512], start=False, stop=True)
            P.copy(rowbuf[:, n * 512:(n + 1) * 512], bank[0:1, :], eng="act")
        P.dma(modrow[l:l + 1, :], rowbuf[:, 0:6 * D])
        for piece in range(8):
            for j in range(8):
                col = piece * 8 + j
                P.mm(pb[2][:, col:col + 1], rowbuf[:, piece * D + j * 128: piece * D + (j + 1) * 128],
                     ones_f[0:1, 0:1], start=True, stop=True)
        P.copy(modc, pb[2][:, 0:64])
        P.stt(G1, modc[:, 8:16], 1.0, modc[:, 48:56], ALU.add, ALU.mult)
        P.stt(G2, modc[:, 32:40], 1.0, modc[:, 56:64], ALU.add, ALU.mult)
        sh1 = modc[:, 0:8]
        sh2 = modc[:, 24:32]
        for c2 in range(2):
            P.mm(pb[3][:, c2 * 37:(c2 + 1) * 37], prm[:, c2 * 128:(c2 + 1) * 128], ident_f[0:37, 0:37])
        P.copy(prmT.re("p c r -> p (c r)"), pb[3][:, 0:74])
        P.mm(pb[4], ones_f[0:1, :], grow)
        P.copy(gqk_bc, pb[4])
        P.tt(gsq[:, 0:64], grow[:, 0:64], grow[:, 0:64], ALU.mult)
        P.tt(gsq[:, 64:128], grow[:, 256:320], grow[:, 256:320], ALU.mult)
        P.reduce(gmx, gsq.re("p (a b) -> p a b", a=2), ALU.max)
        P.tt(gmx[:, 0:1], gmx[:, 0:1], gmx[:, 1:2], ALU.mult)
        P.act(gmx[:, 0:1], gmx[:, 0:1], AF.Sqrt)
        P.ts(gmx[:, 0:1], gmx[:, 0:1], -8.0, None, ALU.mult)
        P.mm(pb[5][:, 0:1], ones_f[0:1, :], gmx[:, 0:1])
        P.copy(negm, pb[5][:, 0:1])
        P.barrier()

        A.reset()
        Wb = A.alloc("Wb", [128, 8, INW], BF16)
        mark = A.off
        wst = [A.alloc("wst%d" % i, [128, 8, 512], F32) for i in range(2)]
        ncol = [(n * 512, min(512, INW - n * 512)) for n in range((INW + 511) // 512)]
        for n, (c0, cw) in enumerate(ncol):
            w = wst[n % 2]
            P.dma(w[:, :, 0:cw], w_in[l][:, c0:c0 + cw].re("(j p) n -> p j n", p=128))
            P.copy(Wb[:, :, c0:c0 + cw], w[:, :, 0:cw], eng=("dve", "pool")[n % 2])
        P.barrier()
        A.off = mark
        A.gen += 1
        xs = [A.alloc("xs%d" % i, [128, D], F32) for i in range(2)]
        junk = A.alloc("junk", [128, D], BF16)
        ss = A.alloc("ss", [128, 4], F32)
        rs = A.alloc("rs", [128, 4], F32)
        xn = A.alloc("xn", [128, 4, D], BF16)
        hT = A.alloc("hT", [128, 8, 512], BF16)
        projT2 = [A.alloc("projT%d" % i, [128, 10, 512], F32) for i in range(2)]
        qkT = A.alloc("qkT", [128, 4, 512], BF16)
        vout = A.alloc("vout", [128, 4, 512], BF16)
        ua = A.alloc("ua", [128, 2, 514], F32)
        acc = A.alloc("acc", [128, 2, 512], F32)
        yaT = A.alloc("yaT", [128, 2, 512], BF16)
        ub = A.alloc("ub", [128, 2, 542], BF16)
        dg = A.alloc("dg", [128, 2, 31, 128], BF16)
        for cc in range(2):
            P.tt(dg[:, cc, :, :], ident_b.unsq(1).bcast([128, 31, 128]),
                 prmT[:, cc, 3:34].unsq(2).bcast([128, 31, 128]), ALU.mult)
        sg = A.alloc("sg", [128, 2, 512], F32)
        usq = A.alloc("usq", [128, 2, 512], F32)
        mean_sb = A.alloc("mean_sb", [128, 512], F32)
        var_sb = A.alloc("var_sb", [128, 512], F32)
        ydT = A.alloc("ydT", [128, 2, 512], BF16)
        cs4 = A.alloc("cs4", [128, 4, 64], F32)
        sq = A.alloc("sq", [128, 512], F32)
        ssh = A.alloc("ssh", [128, 8], F32)
        qn = A.alloc("qn", [128, 512], F32)
        ra = A.alloc("ra", [128, 8, 32], F32)
        rb = A.alloc("rb", [128, 8, 32], F32)
        qr = A.alloc("qr", [128, 4, 512], BF16)
        mbT = A.alloc("mbT", [128, 4, 512], BF16)
        P.memset(ua, 0.0)
        P.memset(ub, 0.0, eng="pool")
        FM = [(0, 0), (1, 128), (2, 256), (3, 384), (4, 512), (5, 640),
              (6, 2304), (7, 2432), (8, 2560), (9, 2688)]
        QK = [(0, 768), (1, 896), (2, 1024), (3, 1152)]
        bic = [0]

        def nbank():
            b_ = pb[bic[0] % 8]
            bic[0] += 1
            return b_

        def head_stages(i):
            t0 = i * 512
            projT = projT2[i % 2]

            def h1():
                for st in range(4):
                    xt = xs[st % 2]
                    P.dma(xt, xin[t0 + st * 128:t0 + (st + 1) * 128, :])
                    P.act(junk, xt, AF.Square, accum_out=ss[:, st:st + 1])
                    rstd_from_ss(rs[:, st:st + 1], ss[:, st:st + 1], D)
                    P.act(xn[:, st, :], xt, AF.Copy, scale=rs[:, st:st + 1])
                P.dma(cs4, csd[t0:t0 + 512, :].re("(s p) f -> p s f", p=128))

            def h2():
                for j in range(8):
                    bank = nbank()
                    bkb = bank.bc(BF16)
                    for st in range(4):
                        P.transpose(bkb[:, st * 128:(st + 1) * 128], xn[:, st, j * 128:(j + 1) * 128], ident_b)
                    P.act(hT[:, j, :], bkb[:, 0:512], AF.Identity, scale=G1[:, j:j + 1], bias=sh1[:, j:j + 1])

            def h3(lo, hi):
                def f():
                    for idx, c0 in FM[lo:hi]:
                        bank = nbank()
                        for j in range(8):
                            P.mm(bank, Wb[:, j, c0:c0 + 128], hT[:, j, :], start=(j == 0), stop=(j == 7))
                        P.copy(projT[:, idx, :], bank, eng=("act", "dve")[idx % 2])
                return f

            def h4():
                for idx, c0 in QK:
                    bank = nbank()
                    for j in range(8):
                        P.mm(bank, Wb[:, j, c0:c0 + 128], hT[:, j, :], start=(j == 0), stop=(j == 7))
                    if idx < 2:
                        P.act(qkT[:, idx, :], bank, AF.Copy, scale=0.125)
                    else:
                        P.copy(qkT[:, idx, :], bank, eng="dve")
                P.dma(sbqT[:, t0:t0 + 512].re("(c p) t -> p c t", p=128), qkT[:, 0:2, :])
                P.dma(sbkT[:, t0:t0 + 512].re("(c p) t -> p c t", p=128), qkT[:, 2:4, :])

            def h5(st):
                def f():
                    bank = nbank()
                    for j in range(8):
                        P.mm(bank[:, 0:256], hT[:, j, st * 128:(st + 1) * 128], Wb[:, j, 1280:1536],
                             start=(j == 0), stop=(j == 7))
                    for j in range(8):
                        P.mm(bank[:, 256:512], hT[:, j, st * 128:(st + 1) * 128], Wb[:, j, 2048:2304],
                             start=(j == 0), stop=(j == 7))
                    P.copy(vout[:, st, :], bank, eng="act")
                    bank = nbank()
                    for j in range(8):
                        P.mm(bank, hT[:, j, st * 128:(st + 1) * 128], Wb[:, j, 1536:2048],
                             start=(j == 0), stop=(j == 7))
                    P.act(sq, bank, AF.Square)
                    P.reduce(ssh, sq.re("p (a b) -> p a b", a=8), ALU.add)
                    rstd_from_ss(ssh, ssh, HD)
                    P.tt(qn.re("p (a b) -> p a b", a=8), bank.re("p (a b) -> p a b", a=8),
                         ssh.unsq(2).bcast([128, 8, 64]), ALU.mult)
                    P.tt(qn, qn, gqk_bc, ALU.mult, eng="pool")
                    q4 = qn.re("p (a h b) -> p a h b", a=8, h=2)
                    o4 = qr[:, st, :].re("p (a h b) -> p a h b", a=8, h=2)
                    cosb = cs4[:, st, 0:32].unsq(1).bcast([128, 8, 32])
                    sinb = cs4[:, st, 32:64].unsq(1).bcast([128, 8, 32])
                    P.tt(ra, q4[:, :, 0, :], cosb, ALU.mult)
                    P.tt(rb, q4[:, :, 1, :], sinb, ALU.mult, eng="pool")
                    P.tt(o4[:, :, 0, :], ra, rb, ALU.subtract)
                    P.tt(ra, q4[:, :, 1, :], cosb, ALU.mult)
                    P.tt(rb, q4[:, :, 0, :], sinb, ALU.mult, eng="pool")
                    P.tt(o4[:, :, 1, :], ra, rb, ALU.add)
                return f

            def h6():
                P.dma(sbv[t0:t0 + 512, :].re("(s p) f -> p s f", p=128), vout[:, :, 0:256])
                P.dma(mbv[t0:t0 + 512, :].re("(s p) f -> p s f", p=128), vout[:, :, 256:512])
                for blk in range(4):
                    bank = nbank()
                    bkb = bank.bc(BF16)
                    for st in range(4):
                        P.transpose(bkb[:, st * 128:(st + 1) * 128], qr[:, st, blk * 128:(blk + 1) * 128], ident_b)
                    P.copy(mbT[:, blk, :], bkb[:, 0:512], eng=("act", "dve")[blk % 2])
                P.dma(mbqT[:, t0:t0 + 512].re("(c p) t -> p c t", p=128), mbT[:, 0:2, :])
                P.dma(mbkT[:, t0:t0 + 512].re("(c p) t -> p c t", p=128), mbT[:, 2:4, :])

            return [h1, h2, h3(0, 5), h3(5, 10), h4, h5(0), h5(1), h5(2), h5(3), h6]

        def tail_stages(i):
            t0 = i * 512
            projT = projT2[i % 2]

            def t1():
                for cc in range(2):
                    w3 = prmT[:, cc, 0:3]
                    P.tt(ua[:, cc, 2:514], projT[:, 2 + cc, :], projT[:, 4 + cc, :], ALU.mult)
                    P.ts(acc[:, cc, :], ua[:, cc, 0:512], w3[:, 0:1], None, ALU.mult)
                    P.stt(acc[:, cc, :], ua[:, cc, 1:513], w3[:, 1:2], acc[:, cc, :], ALU.mult, ALU.add)
                    P.stt(acc[:, cc, :], ua[:, cc, 2:514], w3[:, 2:3], acc[:, cc, :], ALU.mult, ALU.add)
                    P.tt(yaT[:, cc, :], projT[:, cc, :], acc[:, cc, :], ALU.mult)
                    P.copy(ua[:, cc, 0:2], ua[:, cc, 512:514])
                P.dma(mixT[0:256, t0:t0 + 512].re("(c p) t -> p c t", p=128), yaT)

            def t2(cc):
                def f():
                    eng = ("dve", "pool")[cc]
                    P.act(sg[:, cc, :], projT[:, 8 + cc, :], AF.Sigmoid)
                    P.tt(ub[:, cc, 30:542], projT[:, 6 + cc, :], sg[:, cc, :], ALU.mult, eng=eng)
                    bank = nbank()
                    for k in range(31):
                        P.mm(bank, dg[:, cc, k, :], ub[:, cc, k:k + 512], start=(k == 0), stop=(k == 30))
                    P.act(acc[:, cc, :], bank, AF.Identity, bias=prmT[:, cc, 34:35])
                    P.copy(ub[:, cc, 0:30], ub[:, cc, 512:542], eng="dve")
                    P.act(usq[:, cc, :], acc[:, cc, :], AF.Square)
                return f

            def t3():
                bm = nbank()
                bq = nbank()
                for cc in range(2):
                    P.mm(bm, avg_f, acc[:, cc, :], start=(cc == 0), stop=(cc == 1))
                for cc in range(2):
                    P.mm(bq, avg_f, usq[:, cc, :], start=(cc == 0), stop=(cc == 1))
                P.copy(mean_sb, bm, eng="act")
                P.tt(var_sb, mean_sb, mean_sb, ALU.mult)
                P.tt(var_sb, bq, var_sb, ALU.subtract)
                P.ts(var_sb, var_sb, EPS, None, ALU.add)
                P.act(var_sb, var_sb, AF.Sqrt)
                P.recip(var_sb, var_sb)

            def t4():
                for cc in range(2):
                    eng = ("dve", "pool")[cc]
                    P.tt(acc[:, cc, :], acc[:, cc, :], mean_sb, ALU.subtract, eng=eng)
                    P.tt(acc[:, cc, :], acc[:, cc, :], var_sb, ALU.mult, eng=eng)
                    P.act(usq[:, cc, :], acc[:, cc, :], AF.Identity, scale=prmT[:, cc, 35:36], bias=prmT[:, cc, 36:37])
                    P.act(sg[:, cc, :], usq[:, cc, :], AF.Sigmoid)
                    P.tt(ydT[:, cc, :], usq[:, cc, :], sg[:, cc, :], ALU.mult, eng=eng)
                P.dma(mixT[768:1024, t0:t0 + 512].re("(c p) t -> p c t", p=128), ydT)

            return [t1, t2(0), t2(1), t3, t4]

        for f in head_stages(0):
            f()
        for i in range(NT):
            hs = head_stages(i + 1) if i + 1 < NT else []
            tl = tail_stages(i)
            order = []
            hi_, ti_ = 0, 0
            while hi_ < len(hs) or ti_ < len(tl):
                if hi_ < len(hs):
                    order.append(hs[hi_]); hi_ += 1
                if ti_ < len(tl):
                    order.append(tl[ti_]); ti_ += 1
            for f in order:
                f()
        P.barrier()

        A.reset()
        negT = A.alloc("negT", [128, 128], BF16)
        csel = A.alloc("csel", [128, 128], BF16)
        mk = A.alloc("mk", [128, 4, 512], BF16)
        stg = A.alloc("stg", [128, 4, 512], F32)
        P.dma(stg[:, 0, 0:128], k_negT)
        P.copy(negT, stg[:, 0, 0:128])
        P.dma(stg[:, 1, 0:128], k_csel)
        P.copy(csel, stg[:, 1, 0:128])
        stg2 = A.alloc("stg2", [128, 4, 512], F32)
        P.dma(stg2, k_sbmask)
        P.copy(mk, stg2)
        vall = A.alloc("vall", [128, NS, G], BF16)
        for c0 in range(0, NS, 16):
            c1 = min(NS, c0 + 16)
            P.dma(vall[:, c0:c1, :], sbv[c0 * 128:c1 * 128, :].re("(c p) f -> p c f", p=128))
        qTh = [A.alloc("qTh%d" % i, [128, S], BF16) for i in range(2)]
        kTh = [A.alloc("kTh%d" % i, [128, S], BF16) for i in range(2)]
        for i in range(2):
            P.memset(qTh[i][64:128, :], 0.0)
            P.memset(kTh[i][64:128, :], 0.0, eng="pool")
        R = 3
        negO = A.alloc("negO", [128, 128], BF16)
        P.memset(negO, -1.0)
        e_sb = [A.alloc("e_sb%d" % i, [128, 512], F32) for i in range(2)]
        L_b = [A.alloc("L_b%d" % i, [128, 512], BF16) for i in range(4)]
        A_b = [A.alloc("A_b%d" % i, [128, 512], BF16) for i in range(R)]
        ncb = [A.alloc("ncb%d" % i, [128, 512], BF16) for i in range(R)]
        ncf = A.alloc("ncf", [33, 512], F32)
        yo = [A.alloc("yo%d" % i, [128, 512], BF16) for i in range(2)]
        for r in range(R):
            P.memset(ncb[r], 0.0)
        X = pb[0:4]
        Cs = pb[4:6]
        Ob = pb[6:8]
        tcount = 0
        for h in range(NH):
            qT = qTh[h % 2]
            kT = kTh[h % 2]
            P.dma(qT[0:64, :], sbqT[h * 64:(h + 1) * 64, :])
            P.dma(kT[0:64, :], sbkT[h * 64:(h + 1) * 64, :])
            for qt in range(NT):
                O = Ob[tcount % 2]
                yv = yo[tcount % 2]
                tcount += 1
                steps = list(range(4 * qt + 3, -1, -1))
                n = len(steps)
                P.memset(ncf, 0.0)
                P.memset(ncb[0][0:1, :], 0.0)
                P.memset(ncb[0][32:33, :], 0.0)
                qtile = qT[:, qt * 512:(qt + 1) * 512]

                def stA(s):
                    kc = steps[s]
                    P.mm(X[s % 4], kT[:, kc * 128:(kc + 1) * 128], qtile, start=True, stop=False,
                         skip_group_check=True)

                def stB(s):
                    kc = steps[s]
                    P.act(e_sb[s % 2], X[s % 4], AF.Exp)
                    P.act(L_b[s % 4], e_sb[s % 2], AF.Ln, bias=1.0)
                    if kc >= 4 * qt:
                        P.tt(L_b[s % 4], L_b[s % 4], mk[:, kc - 4 * qt, :], ALU.mult, eng="pool")

                def stC(s):
                    pr = s // 2
                    P.mm(Cs[pr % 2], ones_b, L_b[s % 4], start=(s % 2 == 0), stop=(s % 2 == 1))
                    if s % 2 == 1 and s + 1 < n:
                        nb_ = ncb[(pr + 1) % R]
                        P.tt(ncf, ncf, Cs[pr % 2][0:33, :], ALU.subtract)
                        P.copy(nb_[0:33, :], ncf)
                        P.tt(nb_[32:33, :], ncf[32:33, :], nb_[32:33, :], ALU.subtract)

                def stD(s):
                    pr = s // 2
                    P.mm(X[s % 4], negT, L_b[s % 4], start=False, stop=False, skip_group_check=True)
                    if s % 2 == 1:
                        P.mm(X[s % 4], negO, L_b[(s - 1) % 4], start=False, stop=False, skip_group_check=True)
                    P.mm(X[s % 4], csel, ncb[pr % R], start=False, stop=True, skip_group_check=True)

                def stE(s):
                    kc = steps[s]
                    P.act(A_b[s % R], X[s % 4], AF.Exp)
                    if kc >= 4 * qt:
                        P.tt(A_b[s % R], A_b[s % R], mk[:, kc - 4 * qt, :], ALU.mult, eng="pool")

                def stF(s):
                    kc = steps[s]
                    P.mm(O, vall[:, kc, (h // 2) * 128:(h // 2 + 1) * 128], A_b[s % R],
                         start=(s == 0), stop=(s == n - 1))

                stA(0)
                for it in range(n + 2):
                    if it + 1 < n:
                        stA(it + 1)
                    if it < n:
                        stB(it)
                        stC(it)
                    if 0 <= it - 1 < n:
                        stD(it - 1)
                    if 0 <= it - 2 < n:
                        stE(it - 2)
                        stF(it - 2)
                r0 = (h % 2) * 64
                P.copy(yv[r0:r0 + 64, :], O[r0:r0 + 64, :], eng="dve")
                P.dma(mixT[256 + h * 64:256 + (h + 1) * 64, qt * 512:(qt + 1) * 512], yv[r0:r0 + 64, :])
        P.barrier()

        A.reset()
        cb = A.alloc("cb", [128, 4, 512], BF16)
        stg2 = A.alloc("stg2", [128, 4, 512], F32)
        P.dma(stg2, k_mbcb)
        P.copy(cb, stg2)
        blk = A.alloc("blk", [128, 32], F32)
        P.dma(blk, k_blk)
        pbias = A.alloc("pbias", [128, 32, 32], F32)
        ownm = A.alloc("ownm", [128, 32, 32], F32)
        for o in range(NB):
            P.ts(pbias[:, o, :], blk, float(o), -1e9, ALU.is_ge, ALU.mult)
            P.ts(ownm[:, o, :], blk, float(o), None, ALU.is_equal, eng="pool")
        oh = A.alloc("oh", [128, 32, 128], BF16)
        P.memset(oh, 0.0)
        P.copy(oh[0:32], ident_b[0:32, 0:32].unsq(2).bcast([32, 32, 128]))
        vall = A.alloc("vall", [128, NS, G], BF16)
        for c0 in range(0, NS, 16):
            c1 = min(NS, c0 + 16)
            P.dma(vall[:, c0:c1, :], mbv[c0 * 128:c1 * 128, :].re("(c p) f -> p c f", p=128))
        vaug = [A.alloc("vaug%d" % i, [128, NS, 128], BF16) for i in range(2)]
        qTh = [A.alloc("qTh%d" % i, [128, S], BF16) for i in range(2)]
        kTh = [A.alloc("kTh%d" % i, [128, S], BF16) for i in range(2)]
        for i in range(2):
            P.memset(qTh[i][64:128, :], 0.0)
            P.memset(kTh[i][64:128, :], 0.0, eng="pool")
        kmf = A.alloc("kmf", [64, 32], F32)
        kmh = A.alloc("kmh", [64, 32], BF16)
        kml = A.alloc("kml", [64, 32], BF16)
        kmr = A.alloc("kmr", [64, 32], F32)
        g2 = A.alloc("g2", [128, 32], F32)
        top8 = A.alloc("top8", [128, 8], F32)
        thr = A.alloc("thr", [128, 1], F32)
        sel = A.alloc("sel", [128, 32], F32)
        selb = A.alloc("selb", [128, 4, 32], BF16)
        selbT = [A.alloc("selbT%d" % i, [128, S], BF16) for i in range(2)]
        for i in range(2):
            P.memset(selbT[i], 0.0, eng=("dve", "pool")[i])
        P_b = [A.alloc("P_b%d" % i, [128, 512], BF16) for i in range(4)]
        O_sb = A.alloc("O_sb", [65, 512], F32)
        rl = A.alloc("rl", [65, 512], F32)
        yo = [A.alloc("yo%d" % i, [64, 512], BF16) for i in range(2)]
        for i in range(2):
            P.memset(vaug[i], 0.0, eng=("dve", "pool")[i])
            P.memset(vaug[i][:, :, 64:65], 1.0, eng=("dve", "pool")[i])
        P.memset(kmf, 0.0)
        X = pb[0:4]
        Ob = pb[4:6]
        Gp = pb[6]
        Tp = pb[7]
        Bc = pb[6]
        tcount = 0
        for h in range(NH):
            qT = qTh[h % 2]
            kT = kTh[h % 2]
            va = vaug[h % 2]
            sT = selbT[h % 2]
            P.dma(qT[0:64, :], mbqT[h * 64:(h + 1) * 64, :])
            P.dma(kT[0:64, :], mbkT[h * 64:(h + 1) * 64, :])
            P.copy(va[:, :, 0:64], vall[:, :, h * 64:(h + 1) * 64], eng="pool")
            P.reduce(kmf[:, 0:NB], kT[0:64, :].re("p (n b) -> p n b", b=256), ALU.add)
            P.ts(kmf, kmf, 1.0 / 256, None, ALU.mult)
            P.copy(kmh, kmf)
            P.tt(kmr, kmf, kmh, ALU.subtract)
            P.copy(kml, kmr)
            for qt in range(NT):
                for st in range(4):
                    sub = qt * 4 + st
                    own = sub // 2
                    P.mm(Gp[:, 0:32], qT[0:64, sub * 128:(sub + 1) * 128], kmh, start=True, stop=False)
                    P.mm(Gp[:, 0:32], qT[0:64, sub * 128:(sub + 1) * 128], kml, start=False, stop=True)
                    P.tt(g2, Gp[:, 0:32], pbias[:, own, :], ALU.add)
                    P.generic("dve", lambda e, o_=top8, i_=g2: e.max(o_.ap, i_.ap), [g2], [top8])
                    P.ts(thr, top8[:, 2:3], -1e8, None, ALU.max)
                    P.ts(sel, g2, thr, None, ALU.is_ge)
                    P.tt(sel, sel, ownm[:, own, :], ALU.max)
                    P.ts(selb[:, st, :], sel, -1.0, -NEG, ALU.add, ALU.mult)
                Tpb = Tp.bc(BF16)
                for st in range(4):
                    P.transpose(Tpb[0:32, st * 128:(st + 1) * 128], selb[:, st, :], ident_b)
                P.copy(sT[0:32, qt * 512:(qt + 1) * 512], Tpb[0:32, 0:512], eng="act")
            for qt in range(NT):
                O = Ob[tcount % 2]
                yv = yo[tcount % 2]
                tcount += 1
                n = 4 * qt + 4
                qtile = qT[:, qt * 512:(qt + 1) * 512]

                def mA(kc):
                    jb = kc // 2
                    diag = kc >= 4 * qt
                    P.mm(X[kc % 4], kT[:, kc * 128:(kc + 1) * 128], qtile, start=True, stop=False)
                    P.mm(X[kc % 4], oh[:, jb, :], sT[:, qt * 512:(qt + 1) * 512], start=False, stop=not diag)
                    if diag:
                        P.mm(X[kc % 4], ident_b, cb[:, kc - 4 * qt, :], start=False, stop=True)

                def mB(kc):
                    P.act(P_b[kc % 4], X[kc % 4], AF.Exp, scale=0.125, bias=negm)

                def mC(kc):
                    P.mm(O, va[:, kc, :], P_b[kc % 4], start=(kc == 0), stop=(kc == n - 1))

                mA(0)
                if n > 1:
                    mA(1)
                for kc in range(n + 1):
                    if kc + 2 < n:
                        mA(kc + 2)
                    if kc < n:
                        mB(kc)
                    if kc >= 1:
                        mC(kc - 1)
                P.copy(O_sb, O[0:65, :], eng="act")
                P.recip(rl[64:65, :], O_sb[64:65, :])
                P.mm(Bc[0:64, :], ones_f[64:65, 0:64], rl[64:65, :])
                P.tt(yv, O_sb[0:64, :], Bc[0:64, :], ALU.mult)
                P.dma(mixT[512 + h * 64:512 + (h + 1) * 64, qt * 512:(qt + 1) * 512], yv)
        P.barrier()

        A.reset()
        Wo = A.alloc("Wo", [128, 8, D], BF16)
        gbc = A.alloc("gbc", [128, D], F32)
        wst = [A.alloc("wst%d" % i, [128, 4, D], F32) for i in range(2)]
        P.dma(gbc, modrow[l:l + 1, 2 * D:3 * D].bcast([128, D]))
        for n in range(2):
            w = wst[n % 2]
            P.dma(w, w_out[l][n * 512:(n + 1) * 512, :].re("(j p) n -> p j n", p=128))
            P.tt(Wo[:, n * 4:(n + 1) * 4, :], w, gbc.unsq(1).bcast([128, 4, D]), ALU.mult,
                 eng=("dve", "pool")[n % 2])
        mx = [A.alloc("mx%d" % i, [128, 8, 512], BF16) for i in range(2)]
        xs = [A.alloc("xs%d" % i, [128, D], F32) for i in range(2)]
        x1 = [A.alloc("x1_%d" % i, [128, D], F32) for i in range(2)]
        junk = A.alloc("junk", [128, D], BF16)
        ss = A.alloc("ss", [128, 4], F32)
        rs = A.alloc("rs", [128, 4], F32)
        xn = A.alloc("xn", [128, 4, D], BF16)
        hT = [A.alloc("hT%d" % i, [128, 8, 512], BF16) for i in range(2)]
        bi = 0
        for i in range(NT):
            t0 = i * 512
            m = mx[i % 2]
            P.dma(m, mixT[:, t0:t0 + 512].re("(c p) t -> p c t", p=128))
            for st in range(4):
                xt = xs[st % 2]
                x1t = x1[st % 2]
                P.dma(xt, xin[t0 + st * 128:t0 + (st + 1) * 128, :])
                for n in range(2):
                    bank = pb[bi % 8]; bi += 1
                    for j in range(8):
                        P.mm(bank, m[:, j, st * 128:(st + 1) * 128], Wo[:, j, n * 512:(n + 1) * 512],
                             start=(j == 0), stop=(j == 7))
                    P.tt(x1t[:, n * 512:(n + 1) * 512], bank, xt[:, n * 512:(n + 1) * 512], ALU.add)
                P.dma(x1s[t0 + st * 128:t0 + (st + 1) * 128, :], x1t)
                P.act(junk, x1t, AF.Square, accum_out=ss[:, st:st + 1])
                rstd_from_ss(rs[:, st:st + 1], ss[:, st:st + 1], D)
                P.act(xn[:, st, :], x1t, AF.Copy, scale=rs[:, st:st + 1])
            ht = hT[i % 2]
            for j in range(8):
                bank = pb[bi % 8]; bi += 1
                bkb = bank.bc(BF16)
                for st in range(4):
                    P.transpose(bkb[:, st * 128:(st + 1) * 128], xn[:, st, j * 128:(j + 1) * 128], ident_b)
                P.act(ht[:, j, :], bkb[:, 0:512], AF.Identity, scale=G2[:, j:j + 1], bias=sh2[:, j:j + 1])
            P.dma(h2Ts[:, t0:t0 + 512].re("(c p) t -> p c t", p=128), ht)
        P.barrier()

        A.reset()
        W1 = A.alloc("W1", [128, 8, DFF], BF16)
        W2 = A.alloc("W2", [128, 32, D], BF16)
        mark = A.off
        gbc = A.alloc("gbc", [128, D], F32)
        wst = [A.alloc("wst%d" % i, [128, 8, 512], F32) for i in range(2)]
        P.dma(gbc, modrow[l:l + 1, 5 * D:6 * D].bcast([128, D]))
        for n in range(8):
            w = wst[n % 2]
            P.dma(w, w_mlp1[l][:, n * 512:(n + 1) * 512].re("(j p) n -> p j n", p=128))
            P.copy(W1[:, :, n * 512:(n + 1) * 512], w, eng=("dve", "pool")[n % 2])
        for n in range(8):
            w = wst[n % 2].re("p j n -> p (j n)").re("p (j n) -> p j n", j=4)
            P.dma(w, w_mlp2[l][n * 512:(n + 1) * 512, :].re("(j p) n -> p j n", p=128))
            P.tt(W2[:, n * 4:(n + 1) * 4, :], w, gbc.unsq(1).bcast([128, 4, D]), ALU.mult,
                 eng=("dve", "pool")[n % 2])
        P.barrier()
        A.off = mark
        A.gen += 1
        h2 = [A.alloc("h2_%d" % i, [128, 8, 256], BF16) for i in range(2)]
        f1 = A.alloc("f1", [128, 32, 256], BF16)
        rbuf = [A.alloc("rbuf%d" % i, [128, 256], BF16) for i in range(3)]
        x1 = [A.alloc("x1_%d" % i, [128, D], F32) for i in range(2)]
        x2 = [A.alloc("x2_%d" % i, [128, D], F32) for i in range(2)]
        bi = 0
        for i in range(S // 256):
            t0 = i * 256
            hh = h2[i % 2]
            P.dma(hh, h2Ts[:, t0:t0 + 256].re("(c p) t -> p c t", p=128))
            for fc in range(32):
                bank = pb[bi % 8]; bi += 1
                for j in range(8):
                    P.mm(bank[:, 0:256], W1[:, j, fc * 128:(fc + 1) * 128], hh[:, j, :],
                         start=(j == 0), stop=(j == 7))
                rbf = rbuf[fc % 3]
                P.act(rbf, bank[:, 0:256], AF.Relu)
                P.tt(f1[:, fc, :], rbf, rbf, ALU.mult, eng=("dve", "pool")[fc % 2])
            for st in range(2):
                x1t = x1[st]
                x2t = x2[st]
                P.dma(x1t, x1s[t0 + st * 128:t0 + (st + 1) * 128, :])
                for n in range(2):
                    bank = pb[bi % 8]; bi += 1
                    for fc in range(32):
                        P.mm(bank, f1[:, fc, st * 128:(st + 1) * 128], W2[:, fc, n * 512:(n + 1) * 512],
                             start=(fc == 0), stop=(fc == 31))
                    P.tt(x2t[:, n * 512:(n + 1) * 512], bank, x1t[:, n * 512:(n + 1) * 512], ALU.add)
                P.dma(xout[t0 + st * 128:t0 + (st + 1) * 128, :], x2t)
        P.barrier()

    P.emit()
    return nc, P


_CACHE = {}


def _get_nc(S, depth):
    key = (S, depth)
    if key not in _CACHE:
        _CACHE[key] = build(S, depth)[0]
    return _CACHE[key]


def make_in_maps(inputs, S, depth, nb):
    consts = host_consts()
    maps = []
    shared = {}
    for name in ("w_ada", "b_ada", "g_norm1", "w_in", "w_sconv", "w_cconv", "b_cconv", "g_cln", "b_cln",
                 "g_q", "g_k", "w_out", "g_norm2", "w_mlp1", "w_mlp2"):
        shared[name] = np.ascontiguousarray(inputs[name], dtype=np.float32)
    for k, v in consts.items():
        shared["k_" + k] = v
    x = np.asarray(inputs["x"], dtype=np.float32)
    c = np.asarray(inputs["c"], dtype=np.float32)
    pos = np.asarray(inputs["positions"], dtype=np.int32)
    for b in range(nb):
        m = dict(shared)
        m["x"] = np.ascontiguousarray(x[b])
        m["cT"] = np.ascontiguousarray(c[b].reshape(8, 128).T)
        m["pos"] = np.ascontiguousarray(pos[b].reshape(128, S // 128))
        maps.append(m)
    return maps


def kernel(**inputs):
    x = inputs["x"]
    nb, S, _ = x.shape
    depth = inputs["w_ada"].shape[0]
    nc = _get_nc(S, depth)
    maps = make_in_maps(inputs, S, depth, nb)
    res = run_bass_kernel_spmd(nc, maps, core_ids=list(range(nb)))
    return np.stack([np.asarray(r["out"], dtype=np.float32) for r in res.results], axis=0)
```

```python
import numpy as np
from contextlib import ExitStack
import concourse.bass as bass
import concourse.mybir as mybir
from concourse.bass_utils import run_bass_kernel_spmd

F32 = mybir.dt.float32
BF16 = mybir.dt.bfloat16
I32 = mybir.dt.int32
AF = mybir.ActivationFunctionType
ALU = mybir.AluOpType
AX = mybir.AxisListType

NDSEM = 12
D = 1024
G = 256
NH = 4
HD = 64
DFF = 4096
INW = 11 * G
EPS = 1e-6
NEG = -30000.0


class V:
    __slots__ = ("key", "ap")

    def __init__(self, key, ap):
        self.key = key
        self.ap = ap

    def __getitem__(self, idx):
        return V(self.key, self.ap[idx])

    def k(self, sub):
        return V((self.key, sub), self.ap)

    def re(self, pat, **kw):
        return V(self.key, self.ap.rearrange(pat, **kw))

    def bc(self, dt):
        return V(self.key, self.ap.bitcast(dt))

    def bcast(self, shape):
        return V(self.key, self.ap.broadcast_to(list(shape)))

    def unsq(self, ax):
        return V(self.key, self.ap.unsqueeze(ax))


class Op:
    __slots__ = ("eng", "emit", "deps", "signal", "is_dma", "ev_sem", "ev_val")

    def __init__(self, eng, emit, is_dma=False):
        self.eng = eng
        self.emit = emit
        self.deps = []
        self.signal = False
        self.is_dma = is_dma
        self.ev_sem = None
        self.ev_val = 0


class Prog:
    ENGS = ("sp", "act", "dve", "pool", "pe")

    def __init__(self, nc):
        self.nc = nc
        self.es = ExitStack()
        self.ops = {e: [] for e in self.ENGS}
        self.lastreal = {e: None for e in self.ENGS}
        self.lastw = {}
        self.readers = {}
        self.dma_count = {e: 0 for e in self.ENGS}
        self.dma_last = {}
        self.n = 0

    def sb(self, name, shape, dt):
        t = self.es.enter_context(self.nc.sbuf_tensor(name, list(shape), dt))
        return V(name, t[:])

    def ps(self, name, shape, dt):
        t = self.es.enter_context(self.nc.psum_tensor(name, list(shape), dt))
        return V(name, t[:])

    def dram(self, name, shape, dt, kind="Internal"):
        t = self.nc.dram_tensor(name, list(shape), dt, kind=kind)
        return V(name, t.ap())

    def _add(self, eng, emit, reads, writes, is_dma=False):
        op = Op(eng, emit, is_dma)
        rk = [v.key for v in reads if v is not None]
        wk = [v.key for v in writes if v is not None]
        deps = []
        for k in rk:
            p = self.lastw.get(k)
            if p is not None:
                deps.append((p, True))
        for k in wk:
            p = self.lastw.get(k)
            if p is not None:
                deps.append((p, False))
            for r in self.readers.get(k, ()):
                deps.append((r, False))
        for p, raw in deps:
            if p is op:
                continue
            if p.is_dma:
                need = True
            elif p.eng != eng:
                need = True
            elif eng == "pe":
                need = False
            else:
                need = True
            if need and p not in op.deps:
                p.signal = True
                op.deps.append(p)
        if is_dma:
            op.signal = True
            c = self.dma_count[eng]
            self.dma_count[eng] = c + 1
            slot = (eng, c % NDSEM)
            prev = self.dma_last.get(slot)
            if prev is not None:
                op.deps.append(prev)
            self.dma_last[slot] = op
            op.ev_sem = slot
            op.ev_val = 16 * (c // NDSEM + 1)
        for k in wk:
            self.lastw[k] = op
            self.readers[k] = []
        for k in rk:
            if k not in wk:
                self.readers.setdefault(k, []).append(op)
        self.ops[eng].append(op)
        self.lastreal[eng] = op
        self.n += 1
        return op

    def barrier(self):
        lasts = []
        for e in self.ENGS:
            p = self.lastreal[e]
            if p is not None and not p.is_dma:
                p.signal = True
                lasts.append(p)
        lasts.extend(self.dma_last.values())
        for e in self.ENGS:
            op = Op(e, None)
            op.deps = [p for p in lasts if (p.is_dma or p.eng != e)]
            self.ops[e].append(op)
        self.lastw.clear()
        self.readers.clear()

    def dma(self, out, in_, eng="sp", **kw):
        return self._add(eng, lambda e: e.dma_start(out=out.ap, in_=in_.ap, **kw), [in_], [out], is_dma=True)

    def mm(self, out, lhsT, rhs, start=True, stop=True, **kw):
        return self._add("pe", lambda e: e.matmul(out.ap, lhsT.ap, rhs.ap, start=start, stop=stop, **kw),
                         [lhsT, rhs], [out])

    def transpose(self, out, in_, ident):
        return self._add("pe", lambda e: e.transpose(out.ap, in_.ap, ident.ap), [in_, ident], [out])

    def act(self, out, in_, func, bias=None, scale=None, accum_out=None, eng="act"):
        def emit(e):
            kw = {}
            if bias is not None:
                kw["bias"] = bias.ap if isinstance(bias, V) else bias
            if scale is not None:
                kw["scale"] = scale.ap if isinstance(scale, V) else scale
            if accum_out is not None:
                kw["accum_out"] = accum_out.ap
            return e.activation(out=out.ap, in_=in_.ap, func=func, **kw)
        rd = [in_] + [b for b in (bias, scale) if isinstance(b, V)]
        wr = [out] + ([accum_out] if accum_out is not None else [])
        return self._add(eng, emit, rd, wr)

    def ts(self, out, in0, s1, s2, op0, op1=None, eng="dve"):
        def emit(e):
            a1 = s1.ap if isinstance(s1, V) else s1
            a2 = s2.ap if isinstance(s2, V) else s2
            kw = {}
            if op1 is not None:
                kw["op1"] = op1
            return e.tensor_scalar(out.ap, in0.ap, a1, a2, op0, **kw)
        rd = [in0] + [s for s in (s1, s2) if isinstance(s, V)]
        return self._add(eng, emit, rd, [out])

    def tt(self, out, in0, in1, op, eng="dve"):
        return self._add(eng, lambda e: e.tensor_tensor(out.ap, in0.ap, in1.ap, op), [in0, in1], [out])

    def stt(self, out, in0, scalar, in1, op0, op1, eng="dve"):
        def emit(e):
            s = scalar.ap if isinstance(scalar, V) else scalar
            return e.scalar_tensor_tensor(out.ap, in0.ap, s, in1.ap, op0, op1)
        rd = [in0, in1] + ([scalar] if isinstance(scalar, V) else [])
        return self._add(eng, emit, rd, [out])

    def copy(self, out, in_, eng="dve"):
        if eng == "act":
            return self.act(out, in_, AF.Copy)
        return self._add(eng, lambda e: e.tensor_copy(out.ap, in_.ap), [in_], [out])

    def memset(self, out, val, eng="dve"):
        return self._add(eng, lambda e: e.memset(out.ap, val), [], [out])

    def reduce(self, out, in_, op, eng="dve"):
        return self._add(eng, lambda e: e.tensor_reduce(out.ap, in_.ap, AX.X, op), [in_], [out])

    def recip(self, out, in_):
        return self._add("dve", lambda e: e.reciprocal(out.ap, in_.ap), [in_], [out])

    def generic(self, eng, fn, reads, writes):
        return self._add(eng, fn, reads, writes)

    def emit(self):
        nc = self.nc
        es = self.es
        esem = {e: es.enter_context(nc.semaphore("s_" + e)) for e in self.ENGS}
        dsem = {}
        for e in self.ENGS:
            for i in range(min(NDSEM, self.dma_count[e])):
                dsem[(e, i)] = es.enter_context(nc.semaphore("d_%s_%d" % (e, i)))
        for e in self.ENGS:
            c = 0
            for op in self.ops[e]:
                if op.is_dma or op.emit is None:
                    continue
                if op.signal:
                    c += 1
                    op.ev_sem = e
                    op.ev_val = c
        prog = self

        def run(ename, eng):
            waited = {}
            for op in prog.ops[ename]:
                for p in op.deps:
                    key = p.ev_sem
                    sem = dsem[key] if p.is_dma else esem[key]
                    if waited.get(key, 0) < p.ev_val:
                        eng.wait_ge(sem, p.ev_val)
                        waited[key] = p.ev_val
                if op.emit is None:
                    continue
                ins = op.emit(eng)
                if op.signal:
                    if op.is_dma:
                        ins.then_inc(dsem[op.ev_sem], 16)
                    else:
                        ins.then_inc(esem[ename], 1)
            for (e2, i), last in prog.dma_last.items():
                if e2 == ename and waited.get((e2, i), 0) < last.ev_val:
                    eng.wait_ge(dsem[(e2, i)], last.ev_val)

        with nc.Block() as block:
            @block.sync
            def _(e):
                run("sp", e)

            @block.scalar
            def _(e):
                run("act", e)

            @block.vector
            def _(e):
                run("dve", e)

            @block.gpsimd
            def _(e):
                run("pool", e)

            @block.tensor
            def _(e):
                run("pe", e)
        es.close()


class Arena:
    def __init__(self, P, name, nbytes):
        self.v = P.sb(name, [128, nbytes // 4], F32)
        self.cap = nbytes
        self.off = 0
        self.gen = 0

    def reset(self):
        self.off = 0
        self.gen += 1

    def alloc(self, name, shape, dt):
        esz = 4 if dt in (F32, I32) else 2
        nel = int(np.prod(shape[1:]))
        n4 = (nel * esz + 3) // 4
        o4 = self.off // 4
        assert self.off + n4 * 4 <= self.cap, ("arena overflow", name, self.off, n4 * 4, self.cap)
        ap = self.v.ap[0:shape[0], o4:o4 + n4]
        if dt != F32:
            ap = ap.bitcast(dt)
            if esz == 2 and nel != n4 * 2:
                ap = ap[:, 0:nel]
        if len(shape) == 3:
            ap = ap.rearrange("p (a b) -> p a b", a=shape[1])
        elif len(shape) == 4:
            ap = ap.rearrange("p (a b c) -> p a b c", a=shape[1], b=shape[2])
        self.off += ((n4 * 4 + 63) // 64) * 64
        return V((name, self.gen), ap)


def host_consts():
    c = {}
    c["ident"] = np.eye(128, dtype=np.float32)
    s1 = np.arange(128)[:, None]
    s0 = np.arange(128)[None, :]
    c["negT"] = np.where(s1 >= s0, -1.0, 0.0).astype(np.float32)
    t = np.arange(512)[None, :]
    sb = np.zeros((128, 4, 512), np.float32)
    cb = np.zeros((128, 4, 512), np.float32)
    for cc in range(4):
        sb[:, cc, :] = (t > 128 * cc + s1).astype(np.float32)
        cb[:, cc, :] = np.where(t >= 128 * cc + s1, 0.0, NEG)
    c["sbmask"] = sb
    c["mbcb"] = cb
    c["blk"] = np.tile(np.arange(32, dtype=np.float32)[None, :], (128, 1))
    half = HD // 2
    inv = np.exp(-np.log(10000.0) * np.arange(half, dtype=np.float32) / half).astype(np.float32)
    c["invf"] = np.tile(inv[None, :], (128, 1)).astype(np.float32)
    cs = np.zeros((128, 128), np.float32)
    cs[0, :] = 1.0
    cs[32, :] = 1.0
    c["csel"] = cs
    return c


def build(S, depth):
    assert S % 512 == 0
    nc = bass.Bass("TRN2", target_bir_lowering=False)
    P = Prog(nc)
    NT = S // 512
    NS = S // 128
    NB = S // 256
    ext = lambda n, s, d: P.dram(n, s, d, kind="ExternalInput")
    x_in = ext("x", [S, D], F32)
    cT_in = ext("cT", [128, 8], F32)
    pos_in = ext("pos", [128, S // 128], I32)
    w_ada = ext("w_ada", [depth, D, 6 * D], F32)
    b_ada = ext("b_ada", [depth, 6 * D], F32)
    g_norm1 = ext("g_norm1", [depth, D], F32)
    w_in = ext("w_in", [depth, D, INW], F32)
    w_sconv = ext("w_sconv", [depth, 3, G], F32)
    w_cconv = ext("w_cconv", [depth, 31, G], F32)
    b_cconv = ext("b_cconv", [depth, G], F32)
    g_cln = ext("g_cln", [depth, G], F32)
    b_cln = ext("b_cln", [depth, G], F32)
    g_q = ext("g_q", [depth, HD], F32)
    g_k = ext("g_k", [depth, HD], F32)
    w_out = ext("w_out", [depth, D, D], F32)
    g_norm2 = ext("g_norm2", [depth, D], F32)
    w_mlp1 = ext("w_mlp1", [depth, D, DFF], F32)
    w_mlp2 = ext("w_mlp2", [depth, DFF, D], F32)
    k_ident = ext("k_ident", [128, 128], F32)
    k_negT = ext("k_negT", [128, 128], F32)
    k_sbmask = ext("k_sbmask", [128, 4, 512], F32)
    k_mbcb = ext("k_mbcb", [128, 4, 512], F32)
    k_blk = ext("k_blk", [128, 32], F32)
    k_invf = ext("k_invf", [128, 32], F32)
    k_csel = ext("k_csel", [128, 128], F32)
    out = P.dram("out", [S, D], F32, kind="ExternalOutput")

    xmid = P.dram("xmid", [S, D], F32)
    x1s = P.dram("x1s", [S, D], F32)
    h2Ts = P.dram("h2Ts", [D, S], BF16)
    mixT = P.dram("mixT", [D, S], BF16)
    sbqT = P.dram("sbqT", [G, S], BF16)
    sbkT = P.dram("sbkT", [G, S], BF16)
    mbqT = P.dram("mbqT", [G, S], BF16)
    mbkT = P.dram("mbkT", [G, S], BF16)
    sbv = P.dram("sbv", [S, G], BF16)
    mbv = P.dram("mbv", [S, G], BF16)
    csd = P.dram("csd", [S, 64], F32)
    modrow = P.dram("modrow", [depth, 6 * D], F32)

    ident_f = P.sb("ident_f", [128, 128], F32)
    ident_b = P.sb("ident_b", [128, 128], BF16)
    ones_f = P.sb("ones_f", [128, 128], F32)
    ones_b = P.sb("ones_b", [128, 128], BF16)
    avg_f = P.sb("avg_f", [128, 128], F32)
    modc = P.sb("modc", [128, 64], F32)
    G1 = P.sb("G1", [128, 8], F32)
    G2 = P.sb("G2", [128, 8], F32)
    prmT = P.sb("prmT", [128, 2, 37], F32)
    gqk_bc = P.sb("gqk_bc", [128, 512], F32)
    negm = P.sb("negm", [128, 1], F32)
    epsc = P.sb("epsc", [128, 1], F32)
    A = Arena(P, "arena", 203 * 1024)
    pb = [P.ps("pb%d" % i, [128, 512], F32) for i in range(8)]

    P.dma(ident_f, k_ident)
    P.copy(ident_b, ident_f)
    P.memset(ones_f, 1.0)
    P.memset(ones_b, 1.0)
    P.memset(avg_f, 1.0 / 256)
    P.memset(epsc, EPS)

    def rstd_from_ss(rs, ss, n):
        P.ts(rs, ss, 1.0 / n, EPS, ALU.mult, ALU.add)
        P.act(rs, rs, AF.Sqrt)
        P.recip(rs, rs)

    A.reset()
    NJ = S // 128
    posi = A.alloc("posi", [128, NJ], I32)
    posf = A.alloc("posf", [128, NJ], F32)
    invf = A.alloc("invf", [128, 32], F32)
    ang = A.alloc("ang", [128, NJ, 32], F32)
    tmpa = A.alloc("tmpa", [128, NJ, 32], F32)
    cst = A.alloc("cst", [128, NJ, 64], F32)
    mpi = A.alloc("mpi", [128, 1], F32)
    P.memset(mpi, -float(np.pi))
    P.dma(posi, pos_in)
    P.dma(invf, k_invf)
    P.copy(posf, posi)
    P.tt(ang, posf.unsq(2).bcast([128, NJ, 32]), invf.unsq(1).bcast([128, NJ, 32]), ALU.mult)
    TWO_PI = float(2 * np.pi)
    C1 = 6.28125
    C2 = TWO_PI - C1
    PI_ = float(np.pi)
    ni = A.alloc("ni", [128, NJ, 32], I32)
    nf = A.alloc("nf", [128, NJ, 32], F32)
    rr = A.alloc("rr", [128, NJ, 32], F32)
    mm_ = A.alloc("mm_", [128, NJ, 32], F32)
    P.ts(tmpa, ang, 1.0 / TWO_PI, None, ALU.mult)
    P.copy(ni, tmpa)
    P.copy(nf, ni)
    P.stt(rr, nf, -C1, ang, ALU.mult, ALU.add)
    P.stt(rr, nf, -C2, rr, ALU.mult, ALU.add)

    def fold(t):
        P.ts(mm_, t, PI_, None, ALU.is_gt)
        P.stt(t, mm_, -TWO_PI, t, ALU.mult, ALU.add)
        P.ts(mm_, t, -PI_, None, ALU.is_lt)
        P.stt(t, mm_, TWO_PI, t, ALU.mult, ALU.add)
        P.ts(t, t, 3.14159, -3.14159, ALU.min, ALU.max)

    fold(rr)
    P.act(cst[:, :, 32:64], rr, AF.Sin)
    P.ts(tmpa, rr, PI_ / 2, None, ALU.add)
    fold(tmpa)
    P.act(cst[:, :, 0:32], tmpa, AF.Sin)
    P.dma(csd.re("(p j) f -> p j f", p=128), cst)
    P.barrier()

    for l in range(depth):
        xin = x_in if l == 0 else xmid
        xout = out if l == depth - 1 else xmid
        A.reset()
        sc = A.alloc("sc", [128, 8], F32)
        cTs = A.alloc("cTs", [128, 8], F32)
        rowbuf = A.alloc("rowbuf", [1, 8 * D], F32)
        brow = A.alloc("brow", [1, 6 * D], F32)
        wst = [A.alloc("wst%d" % i, [128, 8, 512], F32) for i in range(2)]
        prm = A.alloc("prm", [37, G], F32)
        grow = A.alloc("grow", [1, 512], F32)
        gsq = A.alloc("gsq", [1, 128], F32)
        gmx = A.alloc("gmx", [1, 2], F32)
        P.dma(cTs, cT_in)
        P.act(sc, cTs, AF.Sigmoid)
        P.tt(sc, sc, cTs, ALU.mult)
        P.dma(brow, b_ada[l:l + 1, :])
        P.dma(rowbuf[:, 6 * D:7 * D], g_norm1[l:l + 1, :])
        P.dma(rowbuf[:, 7 * D:8 * D], g_norm2[l:l + 1, :])
        P.dma(prm[0:3, :], w_sconv[l])
        P.dma(prm[3:34, :], w_cconv[l])
        P.dma(prm[34:35, :], b_cconv[l:l + 1, :])
        P.dma(prm[35:36, :], g_cln[l:l + 1, :])
        P.dma(prm[36:37, :], b_cln[l:l + 1, :])
        for h in range(NH):
            P.dma(grow[:, h * 64:(h + 1) * 64], g_q[l:l + 1, :])
            P.dma(grow[:, 256 + h * 64:256 + (h + 1) * 64], g_k[l:l + 1, :])
        for n in range(12):
            w = wst[n % 2]
            P.dma(w, w_ada[l][:, n * 512:(n + 1) * 512].re("(j p) n -> p j n", p=128))
            bank = pb[n % 2]
            for j in range(8):
                P.mm(bank[0:1, :], sc[:, j:j + 1], w[:, j, :], start=(j == 0), stop=False)
            P.mm(bank[0:1, :], ones_f[0:1, 0:1], brow[:, n * 512:(n + 1) * 512], start=False, stop=True)
            P.copy(rowbuf[:, n * 512:(n + 1) * 512], bank[0:1, :], eng="act")
        P.dma(modrow[l:l + 1, :], rowbuf[:, 0:6 * D])
        for piece in range(8):
            for j in range(8):
                col = piece * 8 + j
                P.mm(pb[2][:, col:col + 1], rowbuf[:, piece * D + j * 128: piece * D + (j + 1) * 128],
                     ones_f[0:1, 0:1], start=True, stop=True)
        P.copy(modc, pb[2][:, 0:64])
        P.stt(G1, modc[:, 8:16], 1.0, modc[:, 48:56], ALU.add, ALU.mult)
        P.stt(G2, modc[:, 32:40], 1.0, modc[:, 56:64], ALU.add, ALU.mult)
        sh1 = modc[:, 0:8]
        sh2 = modc[:, 24:32]
        for c2 in range(2):
            P.mm(pb[3][:, c2 * 37:(c2 + 1) * 37], prm[:, c2 * 128:(c2 + 1) * 128], ident_f[0:37, 0:37])
        P.copy(prmT.re("p c r -> p (c r)"), pb[3][:, 0:74])
        P.mm(pb[4], ones_f[0:1, :], grow)
        P.copy(gqk_bc, pb[4])
        P.tt(gsq[:, 0:64], grow[:, 0:64], grow[:, 0:64], ALU.mult)
        P.tt(gsq[:, 64:128], grow[:, 256:320], grow[:, 256:320], ALU.mult)
        P.reduce(gmx, gsq.re("p (a b) -> p a b", a=2), ALU.max)
        P.tt(gmx[:, 0:1], gmx[:, 0:1], gmx[:, 1:2], ALU.mult)
        P.act(gmx[:, 0:1], gmx[:, 0:1], AF.Sqrt)
        P.ts(gmx[:, 0:1], gmx[:, 0:1], -8.0, None, ALU.mult)
        P.mm(pb[5][:, 0:1], ones_f[0:1, :], gmx[:, 0:1])
        P.copy(negm, pb[5][:, 0:1])
        P.barrier()

        A.reset()
        Wb = A.alloc("Wb", [128, 8, INW], BF16)
        mark = A.off
        wst = [A.alloc("wst%d" % i, [128, 8, 512], F32) for i in range(2)]
        ncol = [(n * 512, min(512, INW - n * 512)) for n in range((INW + 511) // 512)]
        for n, (c0, cw) in enumerate(ncol):
            w = wst[n % 2]
            P.dma(w[:, :, 0:cw], w_in[l][:, c0:c0 + cw].re("(j p) n -> p j n", p=128))
            P.copy(Wb[:, :, c0:c0 + cw], w[:, :, 0:cw], eng=("dve", "pool")[n % 2])
        P.barrier()
        A.off = mark
        A.gen += 1
        xs = [A.alloc("xs%d" % i, [128, D], F32) for i in range(2)]
        junk = A.alloc("junk", [128, D], BF16)
        ss = A.alloc("ss", [128, 4], F32)
        rs = A.alloc("rs", [128, 4], F32)
        xn = A.alloc("xn", [128, 4, D], BF16)
        hT = A.alloc("hT", [128, 8, 512], BF16)
        projT2 = [A.alloc("projT%d" % i, [128, 10, 512], F32) for i in range(2)]
        qkT = A.alloc("qkT", [128, 4, 512], BF16)
        vout = A.alloc("vout", [128, 4, 512], BF16)
        ua = A.alloc("ua", [128, 2, 514], F32)
        acc = A.alloc("acc", [128, 2, 512], F32)
        yaT = A.alloc("yaT", [128, 2, 512], BF16)
        ub = A.alloc("ub", [128, 2, 542], BF16)
        dg = A.alloc("dg", [128, 2, 31, 128], BF16)
        for cc in range(2):
            P.tt(dg[:, cc, :, :], ident_b.unsq(1).bcast([128, 31, 128]),
                 prmT[:, cc, 3:34].unsq(2).bcast([128, 31, 128]), ALU.mult)
        sg = A.alloc("sg", [128, 2, 512], F32)
        usq = A.alloc("usq", [128, 2, 512], F32)
        mean_sb = A.alloc("mean_sb", [128, 512], F32)
        var_sb = A.alloc("var_sb", [128, 512], F32)
        ydT = A.alloc("ydT", [128, 2, 512], BF16)
        cs4 = A.alloc("cs4", [128, 4, 64], F32)
        sq = A.alloc("sq", [128, 512], F32)
        ssh = A.alloc("ssh", [128, 8], F32)
        qn = A.alloc("qn", [128, 512], F32)
        ra = A.alloc("ra", [128, 8, 32], F32)
        rb = A.alloc("rb", [128, 8, 32], F32)
        qr = A.alloc("qr", [128, 4, 512], BF16)
        mbT = A.alloc("mbT", [128, 4, 512], BF16)
        P.memset(ua, 0.0)
        P.memset(ub, 0.0, eng="pool")
        FM = [(0, 0), (1, 128), (2, 256), (3, 384), (4, 512), (5, 640),
              (6, 2304), (7, 2432), (8, 2560), (9, 2688)]
        QK = [(0, 768), (1, 896), (2, 1024), (3, 1152)]
        bic = [0]

        def nbank():
            b_ = pb[bic[0] % 8]
            bic[0] += 1
            return b_

        def head_stages(i):
            t0 = i * 512
            projT = projT2[i % 2]

            def h1():
                for st in range(4):
                    xt = xs[st % 2]
                    P.dma(xt, xin[t0 + st * 128:t0 + (st + 1) * 128, :])
                    P.act(junk, xt, AF.Square, accum_out=ss[:, st:st + 1])
                    rstd_from_ss(rs[:, st:st + 1], ss[:, st:st + 1], D)
                    P.act(xn[:, st, :], xt, AF.Copy, scale=rs[:, st:st + 1])
                P.dma(cs4, csd[t0:t0 + 512, :].re("(s p) f -> p s f", p=128))

            def h2():
                for j in range(8):
                    bank = nbank()
                    bkb = bank.bc(BF16)
                    for st in range(4):
                        P.transpose(bkb[:, st * 128:(st + 1) * 128], xn[:, st, j * 128:(j + 1) * 128], ident_b)
                    P.act(hT[:, j, :], bkb[:, 0:512], AF.Identity, scale=G1[:, j:j + 1], bias=sh1[:, j:j + 1])

            def h3(lo, hi):
                def f():
                    for idx, c0 in FM[lo:hi]:
                        bank = nbank()
                        for j in range(8):
                            P.mm(bank, Wb[:, j, c0:c0 + 128], hT[:, j, :], start=(j == 0), stop=(j == 7))
                        P.copy(projT[:, idx, :], bank, eng=("act", "dve")[idx % 2])
                return f

            def h4():
                for idx, c0 in QK:
                    bank = nbank()
                    for j in range(8):
                        P.mm(bank, Wb[:, j, c0:c0 + 128], hT[:, j, :], start=(j == 0), stop=(j == 7))
                    if idx < 2:
                        P.act(qkT[:, idx, :], bank, AF.Copy, scale=0.125)
                    else:
                        P.copy(qkT[:, idx, :], bank, eng="dve")
                P.dma(sbqT[:, t0:t0 + 512].re("(c p) t -> p c t", p=128), qkT[:, 0:2, :])
                P.dma(sbkT[:, t0:t0 + 512].re("(c p) t -> p c t", p=128), qkT[:, 2:4, :])

            def h5(st):
                def f():
                    bank = nbank()
                    for j in range(8):
                        P.mm(bank[:, 0:256], hT[:, j, st * 128:(st + 1) * 128], Wb[:, j, 1280:1536],
                             start=(j == 0), stop=(j == 7))
                    for j in range(8):
                        P.mm(bank[:, 256:512], hT[:, j, st * 128:(st + 1) * 128], Wb[:, j, 2048:2304],
                             start=(j == 0), stop=(j == 7))
                    P.copy(vout[:, st, :], bank, eng="act")
                    bank = nbank()
                    for j in range(8):
                        P.mm(bank, hT[:, j, st * 128:(st + 1) * 128], Wb[:, j, 1536:2048],
                             start=(j == 0), stop=(j == 7))
                    P.act(sq, bank, AF.Square)
                    P.reduce(ssh, sq.re("p (a b) -> p a b", a=8), ALU.add)
                    rstd_from_ss(ssh, ssh, HD)
                    P.tt(qn.re("p (a b) -> p a b", a=8), bank.re("p (a b) -> p a b", a=8),
                         ssh.unsq(2).bcast([128, 8, 64]), ALU.mult)
                    P.tt(qn, qn, gqk_bc, ALU.mult, eng="pool")
                    q4 = qn.re("p (a h b) -> p a h b", a=8, h=2)
                    o4 = qr[:, st, :].re("p (a h b) -> p a h b", a=8, h=2)
                    cosb = cs4[:, st, 0:32].unsq(1).bcast([128, 8, 32])
                    sinb = cs4[:, st, 32:64].unsq(1).bcast([128, 8, 32])
                    P.tt(ra, q4[:, :, 0, :], cosb, ALU.mult)
                    P.tt(rb, q4[:, :, 1, :], sinb, ALU.mult, eng="pool")
                    P.tt(o4[:, :, 0, :], ra, rb, ALU.subtract)
                    P.tt(ra, q4[:, :, 1, :], cosb, ALU.mult)
                    P.tt(rb, q4[:, :, 0, :], sinb, ALU.mult, eng="pool")
                    P.tt(o4[:, :, 1, :], ra, rb, ALU.add)
                return f

            def h6():
                P.dma(sbv[t0:t0 + 512, :].re("(s p) f -> p s f", p=128), vout[:, :, 0:256])
                P.dma(mbv[t0:t0 + 512, :].re("(s p) f -> p s f", p=128), vout[:, :, 256:512])
                for blk in range(4):
                    bank = nbank()
                    bkb = bank.bc(BF16)
                    for st in range(4):
                        P.transpose(bkb[:, st * 128:(st + 1) * 128], qr[:, st, blk * 128:(blk + 1) * 128], ident_b)
                    P.copy(mbT[:, blk, :], bkb[:, 0:512], eng=("act", "dve")[blk % 2])
                P.dma(mbqT[:, t0:t0 + 512].re("(c p) t -> p c t", p=128), mbT[:, 0:2, :])
                P.dma(mbkT[:, t0:t0 + 512].re("(c p) t -> p c t", p=128), mbT[:, 2:4, :])

            return [h1, h2, h3(0, 5), h3(5, 10), h4, h5(0), h5(1), h5(2), h5(3), h6]

        def tail_stages(i):
            t0 = i * 512
            projT = projT2[i % 2]

            def t1():
                for cc in range(2):
                    w3 = prmT[:, cc, 0:3]
                    P.tt(ua[:, cc, 2:514], projT[:, 2 + cc, :], projT[:, 4 + cc, :], ALU.mult)
                    P.ts(acc[:, cc, :], ua[:, cc, 0:512], w3[:, 0:1], None, ALU.mult)
                    P.stt(acc[:, cc, :], ua[:, cc, 1:513], w3[:, 1:2], acc[:, cc, :], ALU.mult, ALU.add)
                    P.stt(acc[:, cc, :], ua[:, cc, 2:514], w3[:, 2:3], acc[:, cc, :], ALU.mult, ALU.add)
                    P.tt(yaT[:, cc, :], projT[:, cc, :], acc[:, cc, :], ALU.mult)
                    P.copy(ua[:, cc, 0:2], ua[:, cc, 512:514])
                P.dma(mixT[0:256, t0:t0 + 512].re("(c p) t -> p c t", p=128), yaT)

            def t2(cc):
                def f():
                    eng = ("dve", "pool")[cc]
                    P.act(sg[:, cc, :], projT[:, 8 + cc, :], AF.Sigmoid)
                    P.tt(ub[:, cc, 30:542], projT[:, 6 + cc, :], sg[:, cc, :], ALU.mult, eng=eng)
                    bank = nbank()
                    for k in range(31):
                        P.mm(bank, dg[:, cc, k, :], ub[:, cc, k:k + 512], start=(k == 0), stop=(k == 30))
                    P.act(acc[:, cc, :], bank, AF.Identity, bias=prmT[:, cc, 34:35])
                    P.copy(ub[:, cc, 0:30], ub[:, cc, 512:542], eng="dve")
                    P.act(usq[:, cc, :], acc[:, cc, :], AF.Square)
                return f

            def t3():
                bm = nbank()
                bq = nbank()
                for cc in range(2):
                    P.mm(bm, avg_f, acc[:, cc, :], start=(cc == 0), stop=(cc == 1))
                for cc in range(2):
                    P.mm(bq, avg_f, usq[:, cc, :], start=(cc == 0), stop=(cc == 1))
                P.copy(mean_sb, bm, eng="act")
                P.tt(var_sb, mean_sb, mean_sb, ALU.mult)
                P.tt(var_sb, bq, var_sb, ALU.subtract)
                P.ts(var_sb, var_sb, EPS, None, ALU.add)
                P.act(var_sb, var_sb, AF.Sqrt)
                P.recip(var_sb, var_sb)

            def t4():
                for cc in range(2):
                    eng = ("dve", "pool")[cc]
                    P.tt(acc[:, cc, :], acc[:, cc, :], mean_sb, ALU.subtract, eng=eng)
                    P.tt(acc[:, cc, :], acc[:, cc, :], var_sb, ALU.mult, eng=eng)
                    P.act(usq[:, cc, :], acc[:, cc, :], AF.Identity, scale=prmT[:, cc, 35:36], bias=prmT[:, cc, 36:37])
                    P.act(sg[:, cc, :], usq[:, cc, :], AF.Sigmoid)
                    P.tt(ydT[:, cc, :], usq[:, cc, :], sg[:, cc, :], ALU.mult, eng=eng)
                P.dma(mixT[768:1024, t0:t0 + 512].re("(c p) t -> p c t", p=128), ydT)

            return [t1, t2(0), t2(1), t3, t4]

        for f in head_stages(0):
            f()
        for i in range(NT):
            hs = head_stages(i + 1) if i + 1 < NT else []
            tl = tail_stages(i)
            order = []
            hi_, ti_ = 0, 0
            while hi_ < len(hs) or ti_ < len(tl):
                if hi_ < len(hs):
                    order.append(hs[hi_]); hi_ += 1
                if ti_ < len(tl):
                    order.append(tl[ti_]); ti_ += 1
            for f in order:
                f()
        P.barrier()

        A.reset()
        negT = A.alloc("negT", [128, 128], BF16)
        csel = A.alloc("csel", [128, 128], BF16)
        mk = A.alloc("mk", [128, 4, 512], BF16)
        stg = A.alloc("stg", [128, 4, 512], F32)
        P.dma(stg[:, 0, 0:128], k_negT)
        P.copy(negT, stg[:, 0, 0:128])
        P.dma(stg[:, 1, 0:128], k_csel)
        P.copy(csel, stg[:, 1, 0:128])
        stg2 = A.alloc("stg2", [128, 4, 512], F32)
        P.dma(stg2, k_sbmask)
        P.copy(mk, stg2)
        vall = A.alloc("vall", [128, NS, G], BF16)
        for c0 in range(0, NS, 16):
            c1 = min(NS, c0 + 16)
            P.dma(vall[:, c0:c1, :], sbv[c0 * 128:c1 * 128, :].re("(c p) f -> p c f", p=128))
        qTh = [A.alloc("qTh%d" % i, [128, S], BF16) for i in range(2)]
        kTh = [A.alloc("kTh%d" % i, [128, S], BF16) for i in range(2)]
        for i in range(2):
            P.memset(qTh[i][64:128, :], 0.0)
            P.memset(kTh[i][64:128, :], 0.0, eng="pool")
        R = 3
        negO = A.alloc("negO", [128, 128], BF16)
        P.memset(negO, -1.0)
        e_sb = [A.alloc("e_sb%d" % i, [128, 512], F32) for i in range(2)]
        L_b = [A.alloc("L_b%d" % i, [128, 512], BF16) for i in range(4)]
        A_b = [A.alloc("A_b%d" % i, [128, 512], BF16) for i in range(R)]
        ncb = [A.alloc("ncb%d" % i, [128, 512], BF16) for i in range(R)]
        ncf = A.alloc("ncf", [33, 512], F32)
        yo = [A.alloc("yo%d" % i, [128, 512], BF16) for i in range(2)]
        for r in range(R):
            P.memset(ncb[r], 0.0)
        X = pb[0:4]
        Cs = pb[4:6]
        Ob = pb[6:8]
        tcount = 0
        for h in range(NH):
            qT = qTh[h % 2]
            kT = kTh[h % 2]
            P.dma(qT[0:64, :], sbqT[h * 64:(h + 1) * 64, :])
            P.dma(kT[0:64, :], sbkT[h * 64:(h + 1) * 64, :])
            for qt in range(NT):
                O = Ob[tcount % 2]
                yv = yo[tcount % 2]
                tcount += 1
                steps = list(range(4 * qt + 3, -1, -1))
                n = len(steps)
                P.memset(ncf, 0.0)
                P.memset(ncb[0][0:1, :], 0.0)
                P.memset(ncb[0][32:33, :], 0.0)
                qtile = qT[:, qt * 512:(qt + 1) * 512]

                def stA(s):
                    kc = steps[s]
                    P.mm(X[s % 4], kT[:, kc * 128:(kc + 1) * 128], qtile, start=True, stop=False,
                         skip_group_check=True)

                def stB(s):
                    kc = steps[s]
                    P.act(e_sb[s % 2], X[s % 4], AF.Exp)
                    P.act(L_b[s % 4], e_sb[s % 2], AF.Ln, bias=1.0)
                    if kc >= 4 * qt:
                        P.tt(L_b[s % 4], L_b[s % 4], mk[:, kc - 4 * qt, :], ALU.mult, eng="pool")

                def stC(s):
                    pr = s // 2
                    P.mm(Cs[pr % 2], ones_b, L_b[s % 4], start=(s % 2 == 0), stop=(s % 2 == 1))
                    if s % 2 == 1 and s + 1 < n:
                        nb_ = ncb[(pr + 1) % R]
                        P.tt(ncf, ncf, Cs[pr % 2][0:33, :], ALU.subtract)
                        P.copy(nb_[0:33, :], ncf)
                        P.tt(nb_[32:33, :], ncf[32:33, :], nb_[32:33, :], ALU.subtract)

                def stD(s):
                    pr = s // 2
                    P.mm(X[s % 4], negT, L_b[s % 4], start=False, stop=False, skip_group_check=True)
                    if s % 2 == 1:
                        P.mm(X[s % 4], negO, L_b[(s - 1) % 4], start=False, stop=False, skip_group_check=True)
                    P.mm(X[s % 4], csel, ncb[pr % R], start=False, stop=True, skip_group_check=True)

                def stE(s):
                    kc = steps[s]
                    P.act(A_b[s % R], X[s % 4], AF.Exp)
                    if kc >= 4 * qt:
                        P.tt(A_b[s % R], A_b[s % R], mk[:, kc - 4 * qt, :], ALU.mult, eng="pool")

                def stF(s):
                    kc = steps[s]
                    P.mm(O, vall[:, kc, (h // 2) * 128:(h // 2 + 1) * 128], A_b[s % R],
                         start=(s == 0), stop=(s == n - 1))

                stA(0)
                for it in range(n + 2):
                    if it + 1 < n:
                        stA(it + 1)
                    if it < n:
                        stB(it)
                        stC(it)
                    if 0 <= it - 1 < n:
                        stD(it - 1)
                    if 0 <= it - 2 < n:
                        stE(it - 2)
                        stF(it - 2)
                r0 = (h % 2) * 64
                P.copy(yv[r0:r0 + 64, :], O[r0:r0 + 64, :], eng="dve")
                P.dma(mixT[256 + h * 64:256 + (h + 1) * 64, qt * 512:(qt + 1) * 512], yv[r0:r0 + 64, :])
        P.barrier()

        A.reset()
        cb = A.alloc("cb", [128, 4, 512], BF16)
        stg2 = A.alloc("stg2", [128, 4, 512], F32)
        P.dma(stg2, k_mbcb)
        P.copy(cb, stg2)
        blk = A.alloc("blk", [128, 32], F32)
        P.dma(blk, k_blk)
        pbias = A.alloc("pbias", [128, 32, 32], F32)
        ownm = A.alloc("ownm", [128, 32, 32], F32)
        for o in range(NB):
            P.ts(pbias[:, o, :], blk, float(o), -1e9, ALU.is_ge, ALU.mult)
            P.ts(ownm[:, o, :], blk, float(o), None, ALU.is_equal, eng="pool")
        oh = A.alloc("oh", [128, 32, 128], BF16)
        P.memset(oh, 0.0)
        P.copy(oh[0:32], ident_b[0:32, 0:32].unsq(2).bcast([32, 32, 128]))
        vall = A.alloc("vall", [128, NS, G], BF16)
        for c0 in range(0, NS, 16):
            c1 = min(NS, c0 + 16)
            P.dma(vall[:, c0:c1, :], mbv[c0 * 128:c1 * 128, :].re("(c p) f -> p c f", p=128))
        vaug = [A.alloc("vaug%d" % i, [128, NS, 128], BF16) for i in range(2)]
        qTh = [A.alloc("qTh%d" % i, [128, S], BF16) for i in range(2)]
        kTh = [A.alloc("kTh%d" % i, [128, S], BF16) for i in range(2)]
        for i in range(2):
            P.memset(qTh[i][64:128, :], 0.0)
            P.memset(kTh[i][64:128, :], 0.0, eng="pool")
        kmf = A.alloc("kmf", [64, 32], F32)
        kmh = A.alloc("kmh", [64, 32], BF16)
        kml = A.alloc("kml", [64, 32], BF16)
        kmr = A.alloc("kmr", [64, 32], F32)
        g2 = A.alloc("g2", [128, 128], F32)
        top8 = A.alloc("top8", [128, 4, 8], F32)
        thr = A.alloc("thr", [128, 4], F32)
        sel = A.alloc("sel", [128, 128], F32)
        selb = A.alloc("selb", [128, 4, 32], BF16)
        selbT = [A.alloc("selbT%d" % i, [128, S], BF16) for i in range(2)]
        for i in range(2):
            P.memset(selbT[i], 0.0, eng=("dve", "pool")[i])
        P_b = [A.alloc("P_b%d" % i, [128, 512], BF16) for i in range(4)]
        O_sb = A.alloc("O_sb", [65, 512], F32)
        rl = A.alloc("rl", [65, 512], F32)
        yo = [A.alloc("yo%d" % i, [64, 512], BF16) for i in range(2)]
        for i in range(2):
            P.memset(vaug[i], 0.0, eng=("dve", "pool")[i])
            P.memset(vaug[i][:, :, 64:65], 1.0, eng=("dve", "pool")[i])
        P.memset(kmf, 0.0)
        X = pb[0:4]
        Ob = pb[4:6]
        Gp = pb[6]
        Tp = pb[7]
        Bc = pb[6]
        tcount = 0
        for h in range(NH):
            qT = qTh[h % 2]
            kT = kTh[h % 2]
            va = vaug[h % 2]
            sT = selbT[h % 2]
            P.dma(qT[0:64, :], mbqT[h * 64:(h + 1) * 64, :])
            P.dma(kT[0:64, :], mbkT[h * 64:(h + 1) * 64, :])
            P.copy(va[:, :, 0:64], vall[:, :, h * 64:(h + 1) * 64], eng="pool")
            P.reduce(kmf[:, 0:NB], kT[0:64, :].re("p (n b) -> p n b", b=256), ALU.add)
            P.ts(kmf, kmf, 1.0 / 256, None, ALU.mult)
            P.copy(kmh, kmf)
            P.tt(kmr, kmf, kmh, ALU.subtract)
            P.copy(kml, kmr)
            for qt in range(NT):
                for st in range(4):
                    sub = qt * 4 + st
                    P.mm(Gp[:, st * 32:(st + 1) * 32], qT[0:64, sub * 128:(sub + 1) * 128], kmh, start=True, stop=False)
                    P.mm(Gp[:, st * 32:(st + 1) * 32], qT[0:64, sub * 128:(sub + 1) * 128], kml, start=False, stop=True)
                own0 = 2 * qt
                pbv = pbias[:, own0:own0 + 2, :].unsq(2).bcast([128, 2, 2, 32])
                omv = ownm[:, own0:own0 + 2, :].unsq(2).bcast([128, 2, 2, 32])
                P.tt(g2.re("p (a b c) -> p a b c", a=2, b=2), Gp[:, 0:128].re("p (a b c) -> p a b c", a=2, b=2),
                     pbv, ALU.add)
                for st in range(4):
                    P.generic("dve", lambda e, o_=top8[:, st, :], i_=g2[:, st * 32:(st + 1) * 32]: e.max(o_.ap, i_.ap),
                              [g2], [top8])
                P.ts(thr, top8[:, :, 2], -1e8, None, ALU.max)
                for st in range(4):
                    P.ts(sel[:, st * 32:(st + 1) * 32], g2[:, st * 32:(st + 1) * 32], thr[:, st:st + 1], None, ALU.is_ge)
                P.tt(sel.re("p (a b c) -> p a b c", a=2, b=2), sel.re("p (a b c) -> p a b c", a=2, b=2), omv, ALU.max)
                P.ts(selb.re("p a b -> p (a b)"), sel, -1.0, -NEG, ALU.add, ALU.mult)
                Tpb = Tp.bc(BF16)
                for st in range(4):
                    P.transpose(Tpb[0:32, st * 128:(st + 1) * 128], selb[:, st, :], ident_b)
                P.copy(sT[0:32, qt * 512:(qt + 1) * 512], Tpb[0:32, 0:512], eng="act")
            for qt in range(NT):
                O = Ob[tcount % 2]
                yv = yo[tcount % 2]
                tcount += 1
                n = 4 * qt + 4
                qtile = qT[:, qt * 512:(qt + 1) * 512]

                def mA(kc):
                    jb = kc // 2
                    diag = kc >= 4 * qt
                    P.mm(X[kc % 4], kT[:, kc * 128:(kc + 1) * 128], qtile, start=True, stop=False)
                    P.mm(X[kc % 4], oh[:, jb, :], sT[:, qt * 512:(qt + 1) * 512], start=False, stop=not diag)
                    if diag:
                        P.mm(X[kc % 4], ident_b, cb[:, kc - 4 * qt, :], start=False, stop=True)

                def mB(kc):
                    P.act(P_b[kc % 4], X[kc % 4], AF.Exp, scale=0.125, bias=negm)

                def mC(kc):
                    P.mm(O, va[:, kc, :], P_b[kc % 4], start=(kc == 0), stop=(kc == n - 1))

                mA(0)
                mA(1)
                mA(2)
                for kc in range(n + 1):
                    if kc + 3 < n:
                        mA(kc + 3)
                    if kc < n:
                        mB(kc)
                    if kc >= 1:
                        mC(kc - 1)
                P.copy(O_sb, O[0:65, :], eng="act")
                P.recip(rl[64:65, :], O_sb[64:65, :])
                P.mm(Bc[0:64, :], ones_f[64:65, 0:64], rl[64:65, :])
                P.tt(yv, O_sb[0:64, :], Bc[0:64, :], ALU.mult)
                P.dma(mixT[512 + h * 64:512 + (h + 1) * 64, qt * 512:(qt + 1) * 512], yv)
        P.barrier()

        A.reset()
        Wo = A.alloc("Wo", [128, 8, D], BF16)
        gbc = A.alloc("gbc", [128, D], F32)
        wst = [A.alloc("wst%d" % i, [128, 4, D], F32) for i in range(2)]
        P.dma(gbc, modrow[l:l + 1, 2 * D:3 * D].bcast([128, D]))
        for n in range(2):
            w = wst[n % 2]
            P.dma(w, w_out[l][n * 512:(n + 1) * 512, :].re("(j p) n -> p j n", p=128))
            P.tt(Wo[:, n * 4:(n + 1) * 4, :], w, gbc.unsq(1).bcast([128, 4, D]), ALU.mult,
                 eng=("dve", "pool")[n % 2])
        mx = [A.alloc("mx%d" % i, [128, 8, 512], BF16) for i in range(2)]
        xs = [A.alloc("xs%d" % i, [128, D], F32) for i in range(2)]
        x1 = [A.alloc("x1_%d" % i, [128, D], F32) for i in range(2)]
        junk = A.alloc("junk", [128, D], BF16)
        ss = A.alloc("ss", [128, 4], F32)
        rs = A.alloc("rs", [128, 4], F32)
        xn = A.alloc("xn", [128, 4, D], BF16)
        hT = [A.alloc("hT%d" % i, [128, 8, 512], BF16) for i in range(2)]
        bi = 0
        for i in range(NT):
            t0 = i * 512
            m = mx[i % 2]
            P.dma(m, mixT[:, t0:t0 + 512].re("(c p) t -> p c t", p=128))
            for st in range(4):
                xt = xs[st % 2]
                x1t = x1[st % 2]
                P.dma(xt, xin[t0 + st * 128:t0 + (st + 1) * 128, :])
                for n in range(2):
                    bank = pb[bi % 8]; bi += 1
                    for j in range(8):
                        P.mm(bank, m[:, j, st * 128:(st + 1) * 128], Wo[:, j, n * 512:(n + 1) * 512],
                             start=(j == 0), stop=(j == 7))
                    P.tt(x1t[:, n * 512:(n + 1) * 512], bank, xt[:, n * 512:(n + 1) * 512], ALU.add)
                P.dma(x1s[t0 + st * 128:t0 + (st + 1) * 128, :], x1t)
                P.act(junk, x1t, AF.Square, accum_out=ss[:, st:st + 1])
                rstd_from_ss(rs[:, st:st + 1], ss[:, st:st + 1], D)
                P.act(xn[:, st, :], x1t, AF.Copy, scale=rs[:, st:st + 1])
            ht = hT[i % 2]
            for j in range(8):
                bank = pb[bi % 8]; bi += 1
                bkb = bank.bc(BF16)
                for st in range(4):
                    P.transpose(bkb[:, st * 128:(st + 1) * 128], xn[:, st, j * 128:(j + 1) * 128], ident_b)
                P.act(ht[:, j, :], bkb[:, 0:512], AF.Identity, scale=G2[:, j:j + 1], bias=sh2[:, j:j + 1])
            P.dma(h2Ts[:, t0:t0 + 512].re("(c p) t -> p c t", p=128), ht)
        P.barrier()

        A.reset()
        W1 = A.alloc("W1", [128, 8, DFF], BF16)
        W2 = A.alloc("W2", [128, 32, D], BF16)
        mark = A.off
        gbc = A.alloc("gbc", [128, D], F32)
        wst = [A.alloc("wst%d" % i, [128, 8, 512], F32) for i in range(2)]
        P.dma(gbc, modrow[l:l + 1, 5 * D:6 * D].bcast([128, D]))
        for n in range(8):
            w = wst[n % 2]
            P.dma(w, w_mlp1[l][:, n * 512:(n + 1) * 512].re("(j p) n -> p j n", p=128))
            P.copy(W1[:, :, n * 512:(n + 1) * 512], w, eng=("dve", "pool")[n % 2])
        for n in range(8):
            w = wst[n % 2].re("p j n -> p (j n)").re("p (j n) -> p j n", j=4)
            P.dma(w, w_mlp2[l][n * 512:(n + 1) * 512, :].re("(j p) n -> p j n", p=128))
            P.tt(W2[:, n * 4:(n + 1) * 4, :], w, gbc.unsq(1).bcast([128, 4, D]), ALU.mult,
                 eng=("dve", "pool")[n % 2])
        P.barrier()
        A.off = mark
        A.gen += 1
        h2 = [A.alloc("h2_%d" % i, [128, 8, 256], BF16) for i in range(2)]
        f1 = A.alloc("f1", [128, 32, 256], BF16)
        rbuf = [A.alloc("rbuf%d" % i, [128, 256], BF16) for i in range(3)]
        x1 = [A.alloc("x1_%d" % i, [128, D], F32) for i in range(2)]
        x2 = [A.alloc("x2_%d" % i, [128, D], F32) for i in range(2)]
        bi = 0
        for i in range(S // 256):
            t0 = i * 256
            hh = h2[i % 2]
            P.dma(hh, h2Ts[:, t0:t0 + 256].re("(c p) t -> p c t", p=128))
            for fc in range(32):
                bank = pb[bi % 8]; bi += 1
                for j in range(8):
                    P.mm(bank[:, 0:256], W1[:, j, fc * 128:(fc + 1) * 128], hh[:, j, :],
                         start=(j == 0), stop=(j == 7))
                rbf = rbuf[fc % 3]
                P.act(rbf, bank[:, 0:256], AF.Relu)
                P.tt(f1[:, fc, :], rbf, rbf, ALU.mult, eng=("dve", "pool")[fc % 2])
            for st in range(2):
                x1t = x1[st]
                x2t = x2[st]
                P.dma(x1t, x1s[t0 + st * 128:t0 + (st + 1) * 128, :])
                for n in range(2):
                    bank = pb[bi % 8]; bi += 1
                    for fc in range(32):
                        P.mm(bank, f1[:, fc, st * 128:(st + 1) * 128], W2[:, fc, n * 512:(n + 1) * 512],
                             start=(fc == 0), stop=(fc == 31))
                    P.tt(x2t[:, n * 512:(n + 1) * 512], bank, x1t[:, n * 512:(n + 1) * 512], ALU.add)
                P.dma(xout[t0 + st * 128:t0 + (st + 1) * 128, :], x2t)
        P.barrier()

    P.emit()
    return nc, P


_CACHE = {}


def _get_nc(S, depth):
    key = (S, depth)
    if key not in _CACHE:
        _CACHE[key] = build(S, depth)[0]
    return _CACHE[key]


def make_in_maps(inputs, S, depth, nb):
    consts = host_consts()
    maps = []
    shared = {}
    for name in ("w_ada", "b_ada", "g_norm1", "w_in", "w_sconv", "w_cconv", "b_cconv", "g_cln", "b_cln",
                 "g_q", "g_k", "w_out", "g_norm2", "w_mlp1", "w_mlp2"):
        shared[name] = np.ascontiguousarray(inputs[name], dtype=np.float32)
    for k, v in consts.items():
        shared["k_" + k] = v
    x = np.asarray(inputs["x"], dtype=np.float32)
    c = np.asarray(inputs["c"], dtype=np.float32)
    pos = np.asarray(inputs["positions"], dtype=np.int32)
    for b in range(nb):
        m = dict(shared)
        m["x"] = np.ascontiguousarray(x[b])
        m["cT"] = np.ascontiguousarray(c[b].reshape(8, 128).T)
        m["pos"] = np.ascontiguousarray(pos[b].reshape(128, S // 128))
        maps.append(m)
    return maps


def kernel(**inputs):
    x = inputs["x"]
    nb, S, _ = x.shape
    depth = inputs["w_ada"].shape[0]
    nc = _get_nc(S, depth)
    maps = make_in_maps(inputs, S, depth, nb)
    res = run_bass_kernel_spmd(nc, maps, core_ids=list(range(nb)))
    return np.stack([np.asarray(r["out"], dtype=np.float32) for r in res.results], axis=0)
```

```python
import numpy as np
from contextlib import ExitStack
import concourse.bass as bass
import concourse.mybir as mybir
from concourse.bass_utils import run_bass_kernel_spmd

F32 = mybir.dt.float32
BF16 = mybir.dt.bfloat16
I32 = mybir.dt.int32
AF = mybir.ActivationFunctionType
ALU = mybir.AluOpType
AX = mybir.AxisListType

NDSEM = 12
D = 1024
G = 256
NH = 4
HD = 64
DFF = 4096
INW = 11 * G
EPS = 1e-6
NEG = -30000.0


class V:
    __slots__ = ("key", "ap")

    def __init__(self, key, ap):
        self.key = key
        self.ap = ap

    def __getitem__(self, idx):
        return V(self.key, self.ap[idx])

    def k(self, sub):
        return V((self.key, sub), self.ap)

    def re(self, pat, **kw):
        return V(self.key, self.ap.rearrange(pat, **kw))

    def bc(self, dt):
        return V(self.key, self.ap.bitcast(dt))

    def bcast(self, shape):
        return V(self.key, self.ap.broadcast_to(list(shape)))

    def unsq(self, ax):
        return V(self.key, self.ap.unsqueeze(ax))


class Op:
    __slots__ = ("eng", "emit", "deps", "signal", "is_dma", "ev_sem", "ev_val")

    def __init__(self, eng, emit, is_dma=False):
        self.eng = eng
        self.emit = emit
        self.deps = []
        self.signal = False
        self.is_dma = is_dma
        self.ev_sem = None
        self.ev_val = 0


class Prog:
    ENGS = ("sp", "act", "dve", "pool", "pe")

    def __init__(self, nc):
        self.nc = nc
        self.es = ExitStack()
        self.ops = {e: [] for e in self.ENGS}
        self.lastreal = {e: None for e in self.ENGS}
        self.lastw = {}
        self.readers = {}
        self.dma_count = {e: 0 for e in self.ENGS}
        self.dma_last = {}
        self.n = 0

    def sb(self, name, shape, dt):
        t = self.es.enter_context(self.nc.sbuf_tensor(name, list(shape), dt))
        return V(name, t[:])

    def ps(self, name, shape, dt):
        t = self.es.enter_context(self.nc.psum_tensor(name, list(shape), dt))
        return V(name, t[:])

    def dram(self, name, shape, dt, kind="Internal"):
        t = self.nc.dram_tensor(name, list(shape), dt, kind=kind)
        return V(name, t.ap())

    def _add(self, eng, emit, reads, writes, is_dma=False):
        op = Op(eng, emit, is_dma)
        rk = [v.key for v in reads if v is not None]
        wk = [v.key for v in writes if v is not None]
        deps = []
        for k in rk:
            p = self.lastw.get(k)
            if p is not None:
                deps.append((p, True))
        for k in wk:
            p = self.lastw.get(k)
            if p is not None:
                deps.append((p, False))
            for r in self.readers.get(k, ()):
                deps.append((r, False))
        for p, raw in deps:
            if p is op:
                continue
            if p.is_dma:
                need = True
            elif p.eng != eng:
                need = True
            elif eng == "pe":
                need = False
            else:
                need = True
            if need and p not in op.deps:
                p.signal = True
                op.deps.append(p)
        if is_dma:
            op.signal = True
            c = self.dma_count[eng]
            self.dma_count[eng] = c + 1
            slot = (eng, c % NDSEM)
            prev = self.dma_last.get(slot)
            if prev is not None:
                op.deps.append(prev)
            self.dma_last[slot] = op
            op.ev_sem = slot
            op.ev_val = 16 * (c // NDSEM + 1)
        for k in wk:
            self.lastw[k] = op
            self.readers[k] = []
        for k in rk:
            if k not in wk:
                self.readers.setdefault(k, []).append(op)
        self.ops[eng].append(op)
        self.lastreal[eng] = op
        self.n += 1
        return op

    def barrier(self):
        lasts = []
        for e in self.ENGS:
            p = self.lastreal[e]
            if p is not None and not p.is_dma:
                p.signal = True
                lasts.append(p)
        lasts.extend(self.dma_last.values())
        for e in self.ENGS:
            op = Op(e, None)
            op.deps = [p for p in lasts if (p.is_dma or p.eng != e)]
            self.ops[e].append(op)
        self.lastw.clear()
        self.readers.clear()

    def dma(self, out, in_, eng="sp", **kw):
        return self._add(eng, lambda e: e.dma_start(out=out.ap, in_=in_.ap, **kw), [in_], [out], is_dma=True)

    def mm(self, out, lhsT, rhs, start=True, stop=True, **kw):
        return self._add("pe", lambda e: e.matmul(out.ap, lhsT.ap, rhs.ap, start=start, stop=stop, **kw),
                         [lhsT, rhs], [out])

    def transpose(self, out, in_, ident):
        return self._add("pe", lambda e: e.transpose(out.ap, in_.ap, ident.ap), [in_, ident], [out])

    def act(self, out, in_, func, bias=None, scale=None, accum_out=None, eng="act"):
        def emit(e):
            kw = {}
            if bias is not None:
                kw["bias"] = bias.ap if isinstance(bias, V) else bias
            if scale is not None:
                kw["scale"] = scale.ap if isinstance(scale, V) else scale
            if accum_out is not None:
                kw["accum_out"] = accum_out.ap
            return e.activation(out=out.ap, in_=in_.ap, func=func, **kw)
        rd = [in_] + [b for b in (bias, scale) if isinstance(b, V)]
        wr = [out] + ([accum_out] if accum_out is not None else [])
        return self._add(eng, emit, rd, wr)

    def ts(self, out, in0, s1, s2, op0, op1=None, eng="dve"):
        def emit(e):
            a1 = s1.ap if isinstance(s1, V) else s1
            a2 = s2.ap if isinstance(s2, V) else s2
            kw = {}
            if op1 is not None:
                kw["op1"] = op1
            return e.tensor_scalar(out.ap, in0.ap, a1, a2, op0, **kw)
        rd = [in0] + [s for s in (s1, s2) if isinstance(s, V)]
        return self._add(eng, emit, rd, [out])

    def tt(self, out, in0, in1, op, eng="dve"):
        return self._add(eng, lambda e: e.tensor_tensor(out.ap, in0.ap, in1.ap, op), [in0, in1], [out])

    def stt(self, out, in0, scalar, in1, op0, op1, eng="dve"):
        def emit(e):
            s = scalar.ap if isinstance(scalar, V) else scalar
            return e.scalar_tensor_tensor(out.ap, in0.ap, s, in1.ap, op0, op1)
        rd = [in0, in1] + ([scalar] if isinstance(scalar, V) else [])
        return self._add(eng, emit, rd, [out])

    def copy(self, out, in_, eng="dve"):
        if eng == "act":
            return self.act(out, in_, AF.Copy)
        return self._add(eng, lambda e: e.tensor_copy(out.ap, in_.ap), [in_], [out])

    def memset(self, out, val, eng="dve"):
        return self._add(eng, lambda e: e.memset(out.ap, val), [], [out])

    def reduce(self, out, in_, op, eng="dve"):
        return self._add(eng, lambda e: e.tensor_reduce(out.ap, in_.ap, AX.X, op), [in_], [out])

    def recip(self, out, in_):
        return self._add("dve", lambda e: e.reciprocal(out.ap, in_.ap), [in_], [out])

    def generic(self, eng, fn, reads, writes):
        return self._add(eng, fn, reads, writes)

    def emit(self):
        nc = self.nc
        es = self.es
        esem = {e: es.enter_context(nc.semaphore("s_" + e)) for e in self.ENGS}
        dsem = {}
        for e in self.ENGS:
            for i in range(min(NDSEM, self.dma_count[e])):
                dsem[(e, i)] = es.enter_context(nc.semaphore("d_%s_%d" % (e, i)))
        for e in self.ENGS:
            c = 0
            for op in self.ops[e]:
                if op.is_dma or op.emit is None:
                    continue
                if op.signal:
                    c += 1
                    op.ev_sem = e
                    op.ev_val = c
        prog = self

        def run(ename, eng):
            waited = {}
            for op in prog.ops[ename]:
                for p in op.deps:
                    key = p.ev_sem
                    sem = dsem[key] if p.is_dma else esem[key]
                    if waited.get(key, 0) < p.ev_val:
                        eng.wait_ge(sem, p.ev_val)
                        waited[key] = p.ev_val
                if op.emit is None:
                    continue
                ins = op.emit(eng)
                if op.signal:
                    if op.is_dma:
                        ins.then_inc(dsem[op.ev_sem], 16)
                    else:
                        ins.then_inc(esem[ename], 1)
            for (e2, i), last in prog.dma_last.items():
                if e2 == ename and waited.get((e2, i), 0) < last.ev_val:
                    eng.wait_ge(dsem[(e2, i)], last.ev_val)

        with nc.Block() as block:
            @block.sync
            def _(e):
                run("sp", e)

            @block.scalar
            def _(e):
                run("act", e)

            @block.vector
            def _(e):
                run("dve", e)

            @block.gpsimd
            def _(e):
                run("pool", e)

            @block.tensor
            def _(e):
                run("pe", e)
        es.close()


class Arena:
    def __init__(self, P, name, nbytes):
        self.v = P.sb(name, [128, nbytes // 4], F32)
        self.cap = nbytes
        self.off = 0
        self.gen = 0

    def reset(self):
        self.off = 0
        self.gen += 1

    def alloc(self, name, shape, dt):
        esz = 4 if dt in (F32, I32) else 2
        nel = int(np.prod(shape[1:]))
        n4 = (nel * esz + 3) // 4
        o4 = self.off // 4
        assert self.off + n4 * 4 <= self.cap, ("arena overflow", name, self.off, n4 * 4, self.cap)
        ap = self.v.ap[0:shape[0], o4:o4 + n4]
        if dt != F32:
            ap = ap.bitcast(dt)
            if esz == 2 and nel != n4 * 2:
                ap = ap[:, 0:nel]
        if len(shape) == 3:
            ap = ap.rearrange("p (a b) -> p a b", a=shape[1])
        elif len(shape) == 4:
            ap = ap.rearrange("p (a b c) -> p a b c", a=shape[1], b=shape[2])
        self.off += ((n4 * 4 + 63) // 64) * 64
        return V((name, self.gen), ap)


def host_consts():
    c = {}
    c["ident"] = np.eye(128, dtype=np.float32)
    s1 = np.arange(128)[:, None]
    s0 = np.arange(128)[None, :]
    c["negT"] = np.where(s1 >= s0, -1.0, 0.0).astype(np.float32)
    t = np.arange(512)[None, :]
    sb = np.zeros((128, 4, 512), np.float32)
    cb = np.zeros((128, 4, 512), np.float32)
    for cc in range(4):
        sb[:, cc, :] = (t > 128 * cc + s1).astype(np.float32)
        cb[:, cc, :] = np.where(t >= 128 * cc + s1, 0.0, NEG)
    c["sbmask"] = sb
    c["mbcb"] = cb
    c["blk"] = np.tile(np.arange(32, dtype=np.float32)[None, :], (128, 1))
    half = HD // 2
    inv = np.exp(-np.log(10000.0) * np.arange(half, dtype=np.float32) / half).astype(np.float32)
    c["invf"] = np.tile(inv[None, :], (128, 1)).astype(np.float32)
    cs = np.zeros((128, 128), np.float32)
    cs[0, :] = 1.0
    cs[32, :] = 1.0
    c["csel"] = cs
    return c


def build(S, depth):
    assert S % 512 == 0
    nc = bass.Bass("TRN2", target_bir_lowering=False)
    P = Prog(nc)
    NT = S // 512
    NS = S // 128
    NB = S // 256
    ext = lambda n, s, d: P.dram(n, s, d, kind="ExternalInput")
    x_in = ext("x", [S, D], F32)
    cT_in = ext("cT", [128, 8], F32)
    pos_in = ext("pos", [128, S // 128], I32)
    w_ada = ext("w_ada", [depth, D, 6 * D], F32)
    b_ada = ext("b_ada", [depth, 6 * D], F32)
    g_norm1 = ext("g_norm1", [depth, D], F32)
    w_in = ext("w_in", [depth, D, INW], F32)
    w_sconv = ext("w_sconv", [depth, 3, G], F32)
    w_cconv = ext("w_cconv", [depth, 31, G], F32)
    b_cconv = ext("b_cconv", [depth, G], F32)
    g_cln = ext("g_cln", [depth, G], F32)
    b_cln = ext("b_cln", [depth, G], F32)
    g_q = ext("g_q", [depth, HD], F32)
    g_k = ext("g_k", [depth, HD], F32)
    w_out = ext("w_out", [depth, D, D], F32)
    g_norm2 = ext("g_norm2", [depth, D], F32)
    w_mlp1 = ext("w_mlp1", [depth, D, DFF], F32)
    w_mlp2 = ext("w_mlp2", [depth, DFF, D], F32)
    k_ident = ext("k_ident", [128, 128], F32)
    k_negT = ext("k_negT", [128, 128], F32)
    k_sbmask = ext("k_sbmask", [128, 4, 512], F32)
    k_mbcb = ext("k_mbcb", [128, 4, 512], F32)
    k_blk = ext("k_blk", [128, 32], F32)
    k_invf = ext("k_invf", [128, 32], F32)
    k_csel = ext("k_csel", [128, 128], F32)
    out = P.dram("out", [S, D], F32, kind="ExternalOutput")

    xmid = P.dram("xmid", [S, D], F32)
    x1s = P.dram("x1s", [S, D], F32)
    h2Ts = P.dram("h2Ts", [D, S], BF16)
    mixT = P.dram("mixT", [D, S], BF16)
    sbqT = P.dram("sbqT", [G, S], BF16)
    sbkT = P.dram("sbkT", [G, S], BF16)
    mbqT = P.dram("mbqT", [G, S], BF16)
    mbkT = P.dram("mbkT", [G, S], BF16)
    sbv = P.dram("sbv", [S, G], BF16)
    mbv = P.dram("mbv", [S, G], BF16)
    csd = P.dram("csd", [S, 64], F32)
    modrow = P.dram("modrow", [depth, 6 * D], F32)

    ident_f = P.sb("ident_f", [128, 128], F32)
    ident_b = P.sb("ident_b", [128, 128], BF16)
    ones_f = P.sb("ones_f", [128, 128], F32)
    ones_b = P.sb("ones_b", [128, 128], BF16)
    avg_f = P.sb("avg_f", [128, 128], F32)
    modc = P.sb("modc", [128, 64], F32)
    G1 = P.sb("G1", [128, 8], F32)
    G2 = P.sb("G2", [128, 8], F32)
    prmT = P.sb("prmT", [128, 2, 37], F32)
    gqk_bc = P.sb("gqk_bc", [128, 512], F32)
    negm = P.sb("negm", [128, 1], F32)
    epsc = P.sb("epsc", [128, 1], F32)
    A = Arena(P, "arena", 203 * 1024)
    pb = [P.ps("pb%d" % i, [128, 512], F32) for i in range(8)]

    P.dma(ident_f, k_ident)
    P.copy(ident_b, ident_f)
    P.memset(ones_f, 1.0)
    P.memset(ones_b, 1.0)
    P.memset(avg_f, 1.0 / 256)
    P.memset(epsc, EPS)

    def rstd_from_ss(rs, ss, n):
        P.ts(rs, ss, 1.0 / n, EPS, ALU.mult, ALU.add)
        P.act(rs, rs, AF.Sqrt)
        P.recip(rs, rs)

    A.reset()
    NJ = S // 128
    posi = A.alloc("posi", [128, NJ], I32)
    posf = A.alloc("posf", [128, NJ], F32)
    invf = A.alloc("invf", [128, 32], F32)
    ang = A.alloc("ang", [128, NJ, 32], F32)
    tmpa = A.alloc("tmpa", [128, NJ, 32], F32)
    cst = A.alloc("cst", [128, NJ, 64], F32)
    mpi = A.alloc("mpi", [128, 1], F32)
    P.memset(mpi, -float(np.pi))
    P.dma(posi, pos_in)
    P.dma(invf, k_invf)
    P.copy(posf, posi)
    P.tt(ang, posf.unsq(2).bcast([128, NJ, 32]), invf.unsq(1).bcast([128, NJ, 32]), ALU.mult)
    TWO_PI = float(2 * np.pi)
    C1 = 6.28125
    C2 = TWO_PI - C1
    PI_ = float(np.pi)
    ni = A.alloc("ni", [128, NJ, 32], I32)
    nf = A.alloc("nf", [128, NJ, 32], F32)
    rr = A.alloc("rr", [128, NJ, 32], F32)
    mm_ = A.alloc("mm_", [128, NJ, 32], F32)
    P.ts(tmpa, ang, 1.0 / TWO_PI, None, ALU.mult)
    P.copy(ni, tmpa)
    P.copy(nf, ni)
    P.stt(rr, nf, -C1, ang, ALU.mult, ALU.add)
    P.stt(rr, nf, -C2, rr, ALU.mult, ALU.add)

    def fold(t):
        P.ts(mm_, t, PI_, None, ALU.is_gt)
        P.stt(t, mm_, -TWO_PI, t, ALU.mult, ALU.add)
        P.ts(mm_, t, -PI_, None, ALU.is_lt)
        P.stt(t, mm_, TWO_PI, t, ALU.mult, ALU.add)
        P.ts(t, t, 3.14159, -3.14159, ALU.min, ALU.max)

    fold(rr)
    P.act(cst[:, :, 32:64], rr, AF.Sin)
    P.ts(tmpa, rr, PI_ / 2, None, ALU.add)
    fold(tmpa)
    P.act(cst[:, :, 0:32], tmpa, AF.Sin)
    P.dma(csd.re("(p j) f -> p j f", p=128), cst)
    P.barrier()

    for l in range(depth):
        xin = x_in if l == 0 else xmid
        xout = out if l == depth - 1 else xmid
        A.reset()
        sc = A.alloc("sc", [128, 8], F32)
        cTs = A.alloc("cTs", [128, 8], F32)
        rowbuf = A.alloc("rowbuf", [1, 8 * D], F32)
        brow = A.alloc("brow", [1, 6 * D], F32)
        wst = [A.alloc("wst%d" % i, [128, 8, 512], F32) for i in range(2)]
        prm = A.alloc("prm", [37, G], F32)
        grow = A.alloc("grow", [1, 512], F32)
        gsq = A.alloc("gsq", [1, 128], F32)
        gmx = A.alloc("gmx", [1, 2], F32)
        P.dma(cTs, cT_in)
        P.act(sc, cTs, AF.Sigmoid)
        P.tt(sc, sc, cTs, ALU.mult)
        P.dma(brow, b_ada[l:l + 1, :])
        P.dma(rowbuf[:, 6 * D:7 * D], g_norm1[l:l + 1, :])
        P.dma(rowbuf[:, 7 * D:8 * D], g_norm2[l:l + 1, :])
        P.dma(prm[0:3, :], w_sconv[l])
        P.dma(prm[3:34, :], w_cconv[l])
        P.dma(prm[34:35, :], b_cconv[l:l + 1, :])
        P.dma(prm[35:36, :], g_cln[l:l + 1, :])
        P.dma(prm[36:37, :], b_cln[l:l + 1, :])
        for h in range(NH):
            P.dma(grow[:, h * 64:(h + 1) * 64], g_q[l:l + 1, :])
            P.dma(grow[:, 256 + h * 64:256 + (h + 1) * 64], g_k[l:l + 1, :])
        for n in range(12):
            w = wst[n % 2]
            P.dma(w, w_ada[l][:, n * 512:(n + 1) * 512].re("(j p) n -> p j n", p=128))
            bank = pb[n % 2]
            for j in range(8):
                P.mm(bank[0:1, :], sc[:, j:j + 1], w[:, j, :], start=(j == 0), stop=False)
            P.mm(bank[0:1, :], ones_f[0:1, 0:1], brow[:, n * 512:(n + 1) * 512], start=False, stop=True)
            P.copy(rowbuf[:, n * 512:(n + 1) * 512], bank[0:1, :], eng="act")
        P.dma(modrow[l:l + 1, :], rowbuf[:, 0:6 * D])
        for piece in range(8):
            for j in range(8):
                col = piece * 8 + j
                P.mm(pb[2][:, col:col + 1], rowbuf[:, piece * D + j * 128: piece * D + (j + 1) * 128],
                     ones_f[0:1, 0:1], start=True, stop=True)
        P.copy(modc, pb[2][:, 0:64])
        P.stt(G1, modc[:, 8:16], 1.0, modc[:, 48:56], ALU.add, ALU.mult)
        P.stt(G2, modc[:, 32:40], 1.0, modc[:, 56:64], ALU.add, ALU.mult)
        sh1 = modc[:, 0:8]
        sh2 = modc[:, 24:32]
        for c2 in range(2):
            P.mm(pb[3][:, c2 * 37:(c2 + 1) * 37], prm[:, c2 * 128:(c2 + 1) * 128], ident_f[0:37, 0:37])
        P.copy(prmT.re("p c r -> p (c r)"), pb[3][:, 0:74])
        P.mm(pb[4], ones_f[0:1, :], grow)
        P.copy(gqk_bc, pb[4])
        P.tt(gsq[:, 0:64], grow[:, 0:64], grow[:, 0:64], ALU.mult)
        P.tt(gsq[:, 64:128], grow[:, 256:320], grow[:, 256:320], ALU.mult)
        P.reduce(gmx, gsq.re("p (a b) -> p a b", a=2), ALU.max)
        P.tt(gmx[:, 0:1], gmx[:, 0:1], gmx[:, 1:2], ALU.mult)
        P.act(gmx[:, 0:1], gmx[:, 0:1], AF.Sqrt)
        P.ts(gmx[:, 0:1], gmx[:, 0:1], -8.0, None, ALU.mult)
        P.mm(pb[5][:, 0:1], ones_f[0:1, :], gmx[:, 0:1])
        P.copy(negm, pb[5][:, 0:1])
        P.barrier()

        A.reset()
        Wb = A.alloc("Wb", [128, 8, INW], BF16)
        mark = A.off
        wst = [A.alloc("wst%d" % i, [128, 8, 512], F32) for i in range(2)]
        ncol = [(n * 512, min(512, INW - n * 512)) for n in range((INW + 511) // 512)]
        for n, (c0, cw) in enumerate(ncol):
            w = wst[n % 2]
            P.dma(w[:, :, 0:cw], w_in[l][:, c0:c0 + cw].re("(j p) n -> p j n", p=128))
            P.copy(Wb[:, :, c0:c0 + cw], w[:, :, 0:cw], eng=("dve", "pool")[n % 2])
        P.barrier()
        A.off = mark
        A.gen += 1
        xs = [A.alloc("xs%d" % i, [128, D], F32) for i in range(2)]
        junk = A.alloc("junk", [128, D], BF16)
        ss = A.alloc("ss", [128, 4], F32)
        rs = A.alloc("rs", [128, 4], F32)
        xn = A.alloc("xn", [128, 4, D], BF16)
        hT = A.alloc("hT", [128, 8, 512], BF16)
        projT2 = [A.alloc("projT%d" % i, [128, 10, 512], F32) for i in range(2)]
        qkT = A.alloc("qkT", [128, 4, 512], BF16)
        vout = A.alloc("vout", [128, 4, 512], BF16)
        ua = A.alloc("ua", [128, 2, 514], F32)
        acc = A.alloc("acc", [128, 2, 512], F32)
        yaT = A.alloc("yaT", [128, 2, 512], BF16)
        ub = A.alloc("ub", [128, 2, 542], BF16)
        dg = A.alloc("dg", [128, 2, 31, 128], BF16)
        for cc in range(2):
            P.tt(dg[:, cc, :, :], ident_b.unsq(1).bcast([128, 31, 128]),
                 prmT[:, cc, 3:34].unsq(2).bcast([128, 31, 128]), ALU.mult)
        sg = A.alloc("sg", [128, 2, 512], F32)
        usq = A.alloc("usq", [128, 2, 512], F32)
        mean_sb = A.alloc("mean_sb", [128, 512], F32)
        var_sb = A.alloc("var_sb", [128, 512], F32)
        ydT = A.alloc("ydT", [128, 2, 512], BF16)
        cs4 = A.alloc("cs4", [128, 4, 64], F32)
        sq = A.alloc("sq", [128, 512], F32)
        ssh = A.alloc("ssh", [128, 8], F32)
        qn = A.alloc("qn", [128, 512], F32)
        ra = A.alloc("ra", [128, 8, 32], F32)
        rb = A.alloc("rb", [128, 8, 32], F32)
        qr = A.alloc("qr", [128, 4, 512], BF16)
        mbT = A.alloc("mbT", [128, 4, 512], BF16)
        P.memset(ua, 0.0)
        P.memset(ub, 0.0, eng="pool")
        FM = [(0, 0), (1, 128), (2, 256), (3, 384), (4, 512), (5, 640),
              (6, 2304), (7, 2432), (8, 2560), (9, 2688)]
        QK = [(0, 768), (1, 896), (2, 1024), (3, 1152)]
        bic = [0]

        def nbank():
            b_ = pb[bic[0] % 8]
            bic[0] += 1
            return b_

        def head_stages(i):
            t0 = i * 512
            projT = projT2[i % 2]

            def h1():
                for st in range(4):
                    xt = xs[st % 2]
                    P.dma(xt, xin[t0 + st * 128:t0 + (st + 1) * 128, :])
                    P.act(junk, xt, AF.Square, accum_out=ss[:, st:st + 1])
                    rstd_from_ss(rs[:, st:st + 1], ss[:, st:st + 1], D)
                    P.act(xn[:, st, :], xt, AF.Copy, scale=rs[:, st:st + 1])
                P.dma(cs4, csd[t0:t0 + 512, :].re("(s p) f -> p s f", p=128))

            def h2():
                for j in range(8):
                    bank = nbank()
                    bkb = bank.bc(BF16)
                    for st in range(4):
                        P.transpose(bkb[:, st * 128:(st + 1) * 128], xn[:, st, j * 128:(j + 1) * 128], ident_b)
                    P.act(hT[:, j, :], bkb[:, 0:512], AF.Identity, scale=G1[:, j:j + 1], bias=sh1[:, j:j + 1])

            def h3(lo, hi):
                def f():
                    for idx, c0 in FM[lo:hi]:
                        bank = nbank()
                        for j in range(8):
                            P.mm(bank, Wb[:, j, c0:c0 + 128], hT[:, j, :], start=(j == 0), stop=(j == 7))
                        P.copy(projT[:, idx, :], bank, eng=("act", "dve")[idx % 2])
                return f

            def h4():
                for idx, c0 in QK:
                    bank = nbank()
                    for j in range(8):
                        P.mm(bank, Wb[:, j, c0:c0 + 128], hT[:, j, :], start=(j == 0), stop=(j == 7))
                    if idx < 2:
                        P.act(qkT[:, idx, :], bank, AF.Copy, scale=0.125)
                    else:
                        P.copy(qkT[:, idx, :], bank, eng="dve")
                P.dma(sbqT[:, t0:t0 + 512].re("(c p) t -> p c t", p=128), qkT[:, 0:2, :])
                P.dma(sbkT[:, t0:t0 + 512].re("(c p) t -> p c t", p=128), qkT[:, 2:4, :])

            def h5(st):
                def f():
                    bank = nbank()
                    for j in range(8):
                        P.mm(bank[:, 0:256], hT[:, j, st * 128:(st + 1) * 128], Wb[:, j, 1280:1536],
                             start=(j == 0), stop=(j == 7))
                    for j in range(8):
                        P.mm(bank[:, 256:512], hT[:, j, st * 128:(st + 1) * 128], Wb[:, j, 2048:2304],
                             start=(j == 0), stop=(j == 7))
                    P.copy(vout[:, st, :], bank, eng="act")
                    bank = nbank()
                    for j in range(8):
                        P.mm(bank, hT[:, j, st * 128:(st + 1) * 128], Wb[:, j, 1536:2048],
                             start=(j == 0), stop=(j == 7))
                    P.act(sq, bank, AF.Square)
                    P.reduce(ssh, sq.re("p (a b) -> p a b", a=8), ALU.add)
                    rstd_from_ss(ssh, ssh, HD)
                    P.tt(qn.re("p (a b) -> p a b", a=8), bank.re("p (a b) -> p a b", a=8),
                         ssh.unsq(2).bcast([128, 8, 64]), ALU.mult)
                    P.tt(qn, qn, gqk_bc, ALU.mult, eng="pool")
                    q4 = qn.re("p (a h b) -> p a h b", a=8, h=2)
                    o4 = qr[:, st, :].re("p (a h b) -> p a h b", a=8, h=2)
                    cosb = cs4[:, st, 0:32].unsq(1).bcast([128, 8, 32])
                    sinb = cs4[:, st, 32:64].unsq(1).bcast([128, 8, 32])
                    P.tt(ra, q4[:, :, 0, :], cosb, ALU.mult)
                    P.tt(rb, q4[:, :, 1, :], sinb, ALU.mult, eng="pool")
                    P.tt(o4[:, :, 0, :], ra, rb, ALU.subtract)
                    P.tt(ra, q4[:, :, 1, :], cosb, ALU.mult)
                    P.tt(rb, q4[:, :, 0, :], sinb, ALU.mult, eng="pool")
                    P.tt(o4[:, :, 1, :], ra, rb, ALU.add)
                return f

            def h6():
                P.dma(sbv[t0:t0 + 512, :].re("(s p) f -> p s f", p=128), vout[:, :, 0:256])
                P.dma(mbv[t0:t0 + 512, :].re("(s p) f -> p s f", p=128), vout[:, :, 256:512])
                for blk in range(4):
                    bank = nbank()
                    bkb = bank.bc(BF16)
                    for st in range(4):
                        P.transpose(bkb[:, st * 128:(st + 1) * 128], qr[:, st, blk * 128:(blk + 1) * 128], ident_b)
                    P.copy(mbT[:, blk, :], bkb[:, 0:512], eng=("act", "dve")[blk % 2])
                P.dma(mbqT[:, t0:t0 + 512].re("(c p) t -> p c t", p=128), mbT[:, 0:2, :])
                P.dma(mbkT[:, t0:t0 + 512].re("(c p) t -> p c t", p=128), mbT[:, 2:4, :])

            return [h1, h2, h3(0, 5), h3(5, 10), h4, h5(0), h5(1), h5(2), h5(3), h6]

        def tail_stages(i):
            t0 = i * 512
            projT = projT2[i % 2]

            def t1():
                for cc in range(2):
                    w3 = prmT[:, cc, 0:3]
                    P.tt(ua[:, cc, 2:514], projT[:, 2 + cc, :], projT[:, 4 + cc, :], ALU.mult)
                    P.ts(acc[:, cc, :], ua[:, cc, 0:512], w3[:, 0:1], None, ALU.mult)
                    P.stt(acc[:, cc, :], ua[:, cc, 1:513], w3[:, 1:2], acc[:, cc, :], ALU.mult, ALU.add)
                    P.stt(acc[:, cc, :], ua[:, cc, 2:514], w3[:, 2:3], acc[:, cc, :], ALU.mult, ALU.add)
                    P.tt(yaT[:, cc, :], projT[:, cc, :], acc[:, cc, :], ALU.mult)
                    P.copy(ua[:, cc, 0:2], ua[:, cc, 512:514])
                P.dma(mixT[0:256, t0:t0 + 512].re("(c p) t -> p c t", p=128), yaT)

            def t2(cc):
                def f():
                    eng = ("dve", "pool")[cc]
                    P.act(sg[:, cc, :], projT[:, 8 + cc, :], AF.Sigmoid)
                    P.tt(ub[:, cc, 30:542], projT[:, 6 + cc, :], sg[:, cc, :], ALU.mult, eng=eng)
                    bank = nbank()
                    for k in range(31):
                        P.mm(bank, dg[:, cc, k, :], ub[:, cc, k:k + 512], start=(k == 0), stop=(k == 30))
                    P.act(acc[:, cc, :], bank, AF.Identity, bias=prmT[:, cc, 34:35])
                    P.copy(ub[:, cc, 0:30], ub[:, cc, 512:542], eng="dve")
                    P.act(usq[:, cc, :], acc[:, cc, :], AF.Square)
                return f

            def t3():
                bm = nbank()
                bq = nbank()
                for cc in range(2):
                    P.mm(bm, avg_f, acc[:, cc, :], start=(cc == 0), stop=(cc == 1))
                for cc in range(2):
                    P.mm(bq, avg_f, usq[:, cc, :], start=(cc == 0), stop=(cc == 1))
                P.copy(mean_sb, bm, eng="act")
                P.tt(var_sb, mean_sb, mean_sb, ALU.mult)
                P.tt(var_sb, bq, var_sb, ALU.subtract)
                P.ts(var_sb, var_sb, EPS, None, ALU.add)
                P.act(var_sb, var_sb, AF.Sqrt)
                P.recip(var_sb, var_sb)

            def t4():
                for cc in range(2):
                    eng = ("dve", "pool")[cc]
                    P.tt(acc[:, cc, :], acc[:, cc, :], mean_sb, ALU.subtract, eng=eng)
                    P.tt(acc[:, cc, :], acc[:, cc, :], var_sb, ALU.mult, eng=eng)
                    P.act(usq[:, cc, :], acc[:, cc, :], AF.Identity, scale=prmT[:, cc, 35:36], bias=prmT[:, cc, 36:37])
                    P.act(sg[:, cc, :], usq[:, cc, :], AF.Sigmoid)
                    P.tt(ydT[:, cc, :], usq[:, cc, :], sg[:, cc, :], ALU.mult, eng=eng)
                P.dma(mixT[768:1024, t0:t0 + 512].re("(c p) t -> p c t", p=128), ydT)

            return [t1, t2(0), t2(1), t3, t4]

        for f in head_stages(0):
            f()
        for i in range(NT):
            hs = head_stages(i + 1) if i + 1 < NT else []
            tl = tail_stages(i)
            order = []
            hi_, ti_ = 0, 0
            while hi_ < len(hs) or ti_ < len(tl):
                if hi_ < len(hs):
                    order.append(hs[hi_]); hi_ += 1
                if ti_ < len(tl):
                    order.append(tl[ti_]); ti_ += 1
            for f in order:
                f()
        P.barrier()

        A.reset()
        negT = A.alloc("negT", [128, 128], BF16)
        csel = A.alloc("csel", [128, 128], BF16)
        mk = A.alloc("mk", [128, 4, 512], BF16)
        stg = A.alloc("stg", [128, 4, 512], F32)
        P.dma(stg[:, 0, 0:128], k_negT)
        P.copy(negT, stg[:, 0, 0:128])
        P.dma(stg[:, 1, 0:128], k_csel)
        P.copy(csel, stg[:, 1, 0:128])
        stg2 = A.alloc("stg2", [128, 4, 512], F32)
        P.dma(stg2, k_sbmask)
        P.copy(mk, stg2)
        vall = A.alloc("vall", [128, NS, G], BF16)
        for c0 in range(0, NS, 16):
            c1 = min(NS, c0 + 16)
            P.dma(vall[:, c0:c1, :], sbv[c0 * 128:c1 * 128, :].re("(c p) f -> p c f", p=128))
        qTh = [A.alloc("qTh%d" % i, [128, S], BF16) for i in range(2)]
        kTh = [A.alloc("kTh%d" % i, [128, S], BF16) for i in range(2)]
        for i in range(2):
            P.memset(qTh[i][64:128, :], 0.0)
            P.memset(kTh[i][64:128, :], 0.0, eng="pool")
        R = 3
        negO = A.alloc("negO", [128, 128], BF16)
        P.memset(negO, -1.0)
        e_sb = [A.alloc("e_sb%d" % i, [128, 512], F32) for i in range(2)]
        L_b = [A.alloc("L_b%d" % i, [128, 512], BF16) for i in range(4)]
        A_b = [A.alloc("A_b%d" % i, [128, 512], BF16) for i in range(R)]
        ncb = [A.alloc("ncb%d" % i, [128, 512], BF16) for i in range(R)]
        ncf = A.alloc("ncf", [33, 512], F32)
        yo = [A.alloc("yo%d" % i, [128, 512], BF16) for i in range(2)]
        for r in range(R):
            P.memset(ncb[r], 0.0)
        X = pb[0:4]
        Cs = pb[4:6]
        Ob = pb[6:8]
        tcount = 0
        for h in range(NH):
            qT = qTh[h % 2]
            kT = kTh[h % 2]
            P.dma(qT[0:64, :], sbqT[h * 64:(h + 1) * 64, :])
            P.dma(kT[0:64, :], sbkT[h * 64:(h + 1) * 64, :])
            for qt in range(NT):
                O = Ob[tcount % 2]
                yv = yo[tcount % 2]
                tcount += 1
                steps = list(range(4 * qt + 3, -1, -1))
                n = len(steps)
                P.memset(ncf, 0.0)
                P.memset(ncb[0][0:1, :], 0.0)
                P.memset(ncb[0][32:33, :], 0.0)
                qtile = qT[:, qt * 512:(qt + 1) * 512]

                def stA(s):
                    kc = steps[s]
                    P.mm(X[s % 4], kT[:, kc * 128:(kc + 1) * 128], qtile, start=True, stop=False,
                         skip_group_check=True)

                def stB(s):
                    kc = steps[s]
                    P.act(e_sb[s % 2], X[s % 4], AF.Exp)
                    P.act(L_b[s % 4], e_sb[s % 2], AF.Ln, bias=1.0)
                    if kc >= 4 * qt:
                        P.tt(L_b[s % 4], L_b[s % 4], mk[:, kc - 4 * qt, :], ALU.mult, eng="pool")

                def stC(s):
                    pr = s // 2
                    P.mm(Cs[pr % 2], ones_b, L_b[s % 4], start=(s % 2 == 0), stop=(s % 2 == 1))
                    if s % 2 == 1 and s + 1 < n:
                        nb_ = ncb[(pr + 1) % R]
                        P.tt(ncf, ncf, Cs[pr % 2][0:33, :], ALU.subtract)
                        P.copy(nb_[0:33, :], ncf)
                        P.tt(nb_[32:33, :], ncf[32:33, :], nb_[32:33, :], ALU.subtract)

                def stD(s):
                    pr = s // 2
                    P.mm(X[s % 4], negT, L_b[s % 4], start=False, stop=False, skip_group_check=True)
                    if s % 2 == 1:
                        P.mm(X[s % 4], negO, L_b[(s - 1) % 4], start=False, stop=False, skip_group_check=True)
                    P.mm(X[s % 4], csel, ncb[pr % R], start=False, stop=True, skip_group_check=True)

                def stE(s):
                    kc = steps[s]
                    P.act(A_b[s % R], X[s % 4], AF.Exp)
                    if kc >= 4 * qt:
                        P.tt(A_b[s % R], A_b[s % R], mk[:, kc - 4 * qt, :], ALU.mult, eng="pool")

                def stF(s):
                    kc = steps[s]
                    P.mm(O, vall[:, kc, (h // 2) * 128:(h // 2 + 1) * 128], A_b[s % R],
                         start=(s == 0), stop=(s == n - 1))

                stA(0)
                for it in range(n + 2):
                    if it + 1 < n:
                        stA(it + 1)
                    if it < n:
                        stB(it)
                        stC(it)
                    if 0 <= it - 1 < n:
                        stD(it - 1)
                    if 0 <= it - 2 < n:
                        stE(it - 2)
                        stF(it - 2)
                r0 = (h % 2) * 64
                P.copy(yv[r0:r0 + 64, :], O[r0:r0 + 64, :], eng="dve")
                P.dma(mixT[256 + h * 64:256 + (h + 1) * 64, qt * 512:(qt + 1) * 512], yv[r0:r0 + 64, :])
        P.barrier()

        A.reset()
        cb = A.alloc("cb", [128, 4, 512], BF16)
        stg2 = A.alloc("stg2", [128, 4, 512], F32)
        P.dma(stg2, k_mbcb)
        P.copy(cb, stg2)
        blk = A.alloc("blk", [128, 32], F32)
        P.dma(blk, k_blk)
        pbias = A.alloc("pbias", [128, 32, 32], F32)
        ownm = A.alloc("ownm", [128, 32, 32], F32)
        for o in range(NB):
            P.ts(pbias[:, o, :], blk, float(o), -1e9, ALU.is_ge, ALU.mult)
            P.ts(ownm[:, o, :], blk, float(o), None, ALU.is_equal, eng="pool")
        oh = A.alloc("oh", [128, 32, 128], BF16)
        P.memset(oh, 0.0)
        P.copy(oh[0:32], ident_b[0:32, 0:32].unsq(2).bcast([32, 32, 128]))
        vall = A.alloc("vall", [128, NS, G], BF16)
        for c0 in range(0, NS, 16):
            c1 = min(NS, c0 + 16)
            P.dma(vall[:, c0:c1, :], mbv[c0 * 128:c1 * 128, :].re("(c p) f -> p c f", p=128))
        vaug = [A.alloc("vaug%d" % i, [128, NS, 128], BF16) for i in range(2)]
        qTh = [A.alloc("qTh%d" % i, [128, S], BF16) for i in range(2)]
        kTh = [A.alloc("kTh%d" % i, [128, S], BF16) for i in range(2)]
        for i in range(2):
            P.memset(qTh[i][64:128, :], 0.0)
            P.memset(kTh[i][64:128, :], 0.0, eng="pool")
        kmf = A.alloc("kmf", [64, 32], F32)
        kmh = A.alloc("kmh", [64, 32], BF16)
        kml = A.alloc("kml", [64, 32], BF16)
        kmr = A.alloc("kmr", [64, 32], F32)
        g2 = A.alloc("g2", [128, 128], F32)
        top8 = A.alloc("top8", [128, 4, 8], F32)
        thr = A.alloc("thr", [128, 4], F32)
        sel = A.alloc("sel", [128, 128], F32)
        selb = A.alloc("selb", [128, 4, 32], BF16)
        selbT = [A.alloc("selbT%d" % i, [128, S], BF16) for i in range(2)]
        for i in range(2):
            P.memset(selbT[i], 0.0, eng=("dve", "pool")[i])
        P_b = [A.alloc("P_b%d" % i, [128, 512], BF16) for i in range(4)]
        O_sb = A.alloc("O_sb", [65, 512], F32)
        rl = A.alloc("rl", [65, 512], F32)
        yo = [A.alloc("yo%d" % i, [64, 512], BF16) for i in range(2)]
        for i in range(2):
            P.memset(vaug[i], 0.0, eng=("dve", "pool")[i])
            P.memset(vaug[i][:, :, 64:65], 1.0, eng=("dve", "pool")[i])
        P.memset(kmf, 0.0)
        X = pb[0:4]
        Ob = pb[4:6]
        Gp = pb[6]
        Tp = pb[7]
        Bc = pb[6]
        tcount = 0
        for h in range(NH):
            qT = qTh[h % 2]
            kT = kTh[h % 2]
            va = vaug[h % 2]
            sT = selbT[h % 2]
            P.dma(qT[0:64, :], mbqT[h * 64:(h + 1) * 64, :])
            P.dma(kT[0:64, :], mbkT[h * 64:(h + 1) * 64, :])
            P.copy(va[:, :, 0:64], vall[:, :, h * 64:(h + 1) * 64], eng="pool")
            P.reduce(kmf[:, 0:NB], kT[0:64, :].re("p (n b) -> p n b", b=256), ALU.add)
            P.ts(kmf, kmf, 1.0 / 256, None, ALU.mult)
            P.copy(kmh, kmf)
            P.tt(kmr, kmf, kmh, ALU.subtract)
            P.copy(kml, kmr)
            for qt in range(NT):
                for st in range(4):
                    sub = qt * 4 + st
                    P.mm(Gp[:, st * 32:(st + 1) * 32], qT[0:64, sub * 128:(sub + 1) * 128], kmh, start=True, stop=False)
                    P.mm(Gp[:, st * 32:(st + 1) * 32], qT[0:64, sub * 128:(sub + 1) * 128], kml, start=False, stop=True)
                own0 = 2 * qt
                pbv = pbias[:, own0:own0 + 2, :].unsq(2).bcast([128, 2, 2, 32])
                omv = ownm[:, own0:own0 + 2, :].unsq(2).bcast([128, 2, 2, 32])
                P.tt(g2.re("p (a b c) -> p a b c", a=2, b=2), Gp[:, 0:128].re("p (a b c) -> p a b c", a=2, b=2),
                     pbv, ALU.add)
                for st in range(4):
                    P.generic("dve", lambda e, o_=top8[:, st, :], i_=g2[:, st * 32:(st + 1) * 32]: e.max(o_.ap, i_.ap),
                              [g2], [top8])
                P.ts(thr, top8[:, :, 2], -1e8, None, ALU.max)
                for st in range(4):
                    P.ts(sel[:, st * 32:(st + 1) * 32], g2[:, st * 32:(st + 1) * 32], thr[:, st:st + 1], None, ALU.is_ge)
                P.tt(sel.re("p (a b c) -> p a b c", a=2, b=2), sel.re("p (a b c) -> p a b c", a=2, b=2), omv, ALU.max)
                P.ts(selb.re("p a b -> p (a b)"), sel, -1.0, -NEG, ALU.add, ALU.mult)
                Tpb = Tp.bc(BF16)
                for st in range(4):
                    P.transpose(Tpb[0:32, st * 128:(st + 1) * 128], selb[:, st, :], ident_b)
                P.copy(sT[0:32, qt * 512:(qt + 1) * 512], Tpb[0:32, 0:512], eng="act")
            for qt in range(NT):
                O = Ob[tcount % 2]
                yv = yo[tcount % 2]
                tcount += 1
                n = 4 * qt + 4
                qtile = qT[:, qt * 512:(qt + 1) * 512]

                def mA(kc):
                    jb = kc // 2
                    diag = kc >= 4 * qt
                    P.mm(X[kc % 4], kT[:, kc * 128:(kc + 1) * 128], qtile, start=True, stop=False)
                    P.mm(X[kc % 4], oh[:, jb, :], sT[:, qt * 512:(qt + 1) * 512], start=False, stop=not diag)
                    if diag:
                        P.mm(X[kc % 4], ident_b, cb[:, kc - 4 * qt, :], start=False, stop=True)

                def mB(kc):
                    P.act(P_b[kc % 4], X[kc % 4], AF.Exp, scale=0.125, bias=negm)

                def mC(kc):
                    P.mm(O, va[:, kc, :], P_b[kc % 4], start=(kc == 0), stop=(kc == n - 1))

                mA(0)
                mA(1)
                mA(2)
                for kc in range(n + 1):
                    if kc + 3 < n:
                        mA(kc + 3)
                    if kc < n:
                        mB(kc)
                    if kc >= 1:
                        mC(kc - 1)
                P.copy(O_sb, O[0:65, :], eng="act")
                P.recip(rl[64:65, :], O_sb[64:65, :])
                P.mm(Bc[0:64, :], ones_f[64:65, 0:64], rl[64:65, :])
                P.tt(yv, O_sb[0:64, :], Bc[0:64, :], ALU.mult)
                P.dma(mixT[512 + h * 64:512 + (h + 1) * 64, qt * 512:(qt + 1) * 512], yv)
        P.barrier()

        A.reset()
        Wo = A.alloc("Wo", [128, 8, D], BF16)
        gbc = A.alloc("gbc", [128, D], F32)
        wst = [A.alloc("wst%d" % i, [128, 4, D], F32) for i in range(2)]
        P.dma(gbc, modrow[l:l + 1, 2 * D:3 * D].bcast([128, D]))
        for n in range(2):
            w = wst[n % 2]
            P.dma(w, w_out[l][n * 512:(n + 1) * 512, :].re("(j p) n -> p j n", p=128))
            P.tt(Wo[:, n * 4:(n + 1) * 4, :], w, gbc.unsq(1).bcast([128, 4, D]), ALU.mult,
                 eng=("dve", "pool")[n % 2])
        mx = [A.alloc("mx%d" % i, [128, 8, 512], BF16) for i in range(2)]
        xs = [A.alloc("xs%d" % i, [128, D], F32) for i in range(2)]
        x1 = [A.alloc("x1_%d" % i, [128, D], F32) for i in range(2)]
        junk = A.alloc("junk", [128, D], BF16)
        ss = A.alloc("ss", [128, 4], F32)
        rs = A.alloc("rs", [128, 4], F32)
        xn = A.alloc("xn", [128, 4, D], BF16)
        hT = [A.alloc("hT%d" % i, [128, 8, 512], BF16) for i in range(2)]
        bi = 0
        for i in range(NT):
            t0 = i * 512
            m = mx[i % 2]
            P.dma(m, mixT[:, t0:t0 + 512].re("(c p) t -> p c t", p=128))
            for st in range(4):
                xt = xs[st % 2]
                x1t = x1[st % 2]
                P.dma(xt, xin[t0 + st * 128:t0 + (st + 1) * 128, :])
                for n in range(2):
                    bank = pb[bi % 8]; bi += 1
                    for j in range(8):
                        P.mm(bank, m[:, j, st * 128:(st + 1) * 128], Wo[:, j, n * 512:(n + 1) * 512],
                             start=(j == 0), stop=(j == 7))
                    P.tt(x1t[:, n * 512:(n + 1) * 512], bank, xt[:, n * 512:(n + 1) * 512], ALU.add)
                P.dma(x1s[t0 + st * 128:t0 + (st + 1) * 128, :], x1t)
                P.act(junk, x1t, AF.Square, accum_out=ss[:, st:st + 1])
                rstd_from_ss(rs[:, st:st + 1], ss[:, st:st + 1], D)
                P.act(xn[:, st, :], x1t, AF.Copy, scale=rs[:, st:st + 1])
            ht = hT[i % 2]
            for j in range(8):
                bank = pb[bi % 8]; bi += 1
                bkb = bank.bc(BF16)
                for st in range(4):
                    P.transpose(bkb[:, st * 128:(st + 1) * 128], xn[:, st, j * 128:(j + 1) * 128], ident_b)
                P.act(ht[:, j, :], bkb[:, 0:512], AF.Identity, scale=G2[:, j:j + 1], bias=sh2[:, j:j + 1])
            P.dma(h2Ts[:, t0:t0 + 512].re("(c p) t -> p c t", p=128), ht)
        P.barrier()

        A.reset()
        W1 = A.alloc("W1", [128, 8, DFF], BF16)
        W2 = A.alloc("W2", [128, 32, D], BF16)
        mark = A.off
        gbc = A.alloc("gbc", [128, D], F32)
        wst = [A.alloc("wst%d" % i, [128, 8, 512], F32) for i in range(2)]
        P.dma(gbc, modrow[l:l + 1, 5 * D:6 * D].bcast([128, D]))
        for n in range(8):
            w = wst[n % 2]
            P.dma(w, w_mlp1[l][:, n * 512:(n + 1) * 512].re("(j p) n -> p j n", p=128))
            P.copy(W1[:, :, n * 512:(n + 1) * 512], w, eng=("dve", "pool")[n % 2])
        for n in range(8):
            w = wst[n % 2].re("p j n -> p (j n)").re("p (j n) -> p j n", j=4)
            P.dma(w, w_mlp2[l][n * 512:(n + 1) * 512, :].re("(j p) n -> p j n", p=128))
            P.tt(W2[:, n * 4:(n + 1) * 4, :], w, gbc.unsq(1).bcast([128, 4, D]), ALU.mult,
                 eng=("dve", "pool")[n % 2])
        P.barrier()
        A.off = mark
        A.gen += 1
        TT = 512
        h2 = [A.alloc("h2_%d" % i, [128, 8, TT], BF16) for i in range(2)]
        f1 = A.alloc("f1", [128, 32, TT], BF16)
        rbuf = [A.alloc("rbuf%d" % i, [128, TT], BF16) for i in range(3)]
        x1 = [A.alloc("x1_%d" % i, [128, D], F32) for i in range(2)]
        x2 = [A.alloc("x2_%d" % i, [128, D], F32) for i in range(2)]
        bi = 0
        for i in range(S // TT):
            t0 = i * TT
            hh = h2[i % 2]
            P.dma(hh, h2Ts[:, t0:t0 + TT].re("(c p) t -> p c t", p=128))
            for fc in range(32):
                bank = pb[bi % 8]; bi += 1
                for j in range(8):
                    P.mm(bank[:, 0:TT], W1[:, j, fc * 128:(fc + 1) * 128], hh[:, j, :],
                         start=(j == 0), stop=(j == 7))
                rbf = rbuf[fc % 3]
                P.act(rbf, bank[:, 0:TT], AF.Relu)
                P.tt(f1[:, fc, :], rbf, rbf, ALU.mult, eng=("dve", "pool")[fc % 2])
            for st in range(TT // 128):
                x1t = x1[st % 2]
                x2t = x2[st % 2]
                P.dma(x1t, x1s[t0 + st * 128:t0 + (st + 1) * 128, :])
                for n in range(2):
                    bank = pb[bi % 8]; bi += 1
                    for fc in range(32):
                        P.mm(bank, f1[:, fc, st * 128:(st + 1) * 128], W2[:, fc, n * 512:(n + 1) * 512],
                             start=(fc == 0), stop=(fc == 31))
                    P.tt(x2t[:, n * 512:(n + 1) * 512], bank, x1t[:, n * 512:(n + 1) * 512], ALU.add)
                P.dma(xout[t0 + st * 128:t0 + (st + 1) * 128, :], x2t)
        P.barrier()

    P.emit()
    return nc, P


_CACHE = {}


def _get_nc(S, depth):
    key = (S, depth)
    if key not in _CACHE:
        _CACHE[key] = build(S, depth)[0]
    return _CACHE[key]


def make_in_maps(inputs, S, depth, nb):
    consts = host_consts()
    maps = []
    shared = {}
    for name in ("w_ada", "b_ada", "g_norm1", "w_in", "w_sconv", "w_cconv", "b_cconv", "g_cln", "b_cln",
                 "g_q", "g_k", "w_out", "g_norm2", "w_mlp1", "w_mlp2"):
        shared[name] = np.ascontiguousarray(inputs[name], dtype=np.float32)
    for k, v in consts.items():
        shared["k_" + k] = v
    x = np.asarray(inputs["x"], dtype=np.float32)
    c = np.asarray(inputs["c"], dtype=np.float32)
    pos = np.asarray(inputs["positions"], dtype=np.int32)
    for b in range(nb):
        m = dict(shared)
        m["x"] = np.ascontiguousarray(x[b])
        m["cT"] = np.ascontiguousarray(c[b].reshape(8, 128).T)
        m["pos"] = np.ascontiguousarray(pos[b].reshape(128, S // 128))
        maps.append(m)
    return maps


def kernel(**inputs):
    x = inputs["x"]
    nb, S, _ = x.shape
    depth = inputs["w_ada"].shape[0]
    nc = _get_nc(S, depth)
    maps = make_in_maps(inputs, S, depth, nb)
    res = run_bass_kernel_spmd(nc, maps, core_ids=list(range(nb)))
    return np.stack([np.asarray(r["out"], dtype=np.float32) for r in res.results], axis=0)
```
